# Optimizing a Trainium2 kernel written in Bass

```python
import math
import jax, jax.numpy as jnp
from jax import lax
import numpy as np

D_MODEL = 1024
BATCH = 4
SEQ = 4096
DEPTH = 1

GRID_W = 64
NA_WIDTH = D_MODEL // 2
NA_HEAD_DIM = 64
NA_HEADS = NA_WIDTH // NA_HEAD_DIM
NA_KH_MAX = 8
NA_KW = 16
S5_WIDTH = D_MODEL // 2
S5_GROUP = 16
S5_GROUPS = S5_WIDTH // S5_GROUP
S5_STATE = 64
DT_MIN = 1e-3
DT_MAX = 1e-1
MEM_TOKENS = 256
MEM_HEADS = 4
MEM_WIDTH = D_MODEL // 2
MEM_HEAD_DIM = MEM_WIDTH // MEM_HEADS
N_BRANCH = 3
BRANCH_WIDTH = D_MODEL // 2
IN_WIDTH = 3 * NA_WIDTH + S5_WIDTH + MEM_WIDTH
D_FF = -(-8 * D_MODEL // (3 * 256)) * 256
EPS = 1e-6
NEG_INF = -1e30

kernel_name = "hybrid_natten_s5_memxattn_gated_block"


def rms_norm(x, g):
    x32 = x.astype(jnp.float32)
    y = x32 * lax.rsqrt(jnp.mean(x32 * x32, axis=-1, keepdims=True) + EPS)
    return (y * g.astype(jnp.float32)).astype(x.dtype)


def neighbourhood_attention(q, k, v, rpb):
    b, s, h, dh = q.shape
    rows = s // GRID_W
    kh = min(NA_KH_MAX, rows)
    qg = q.reshape(b, rows, GRID_W, h, dh)
    kg = k.reshape(b, rows, GRID_W, h, dh)
    vg = v.reshape(b, rows, GRID_W, h, dh)
    r = jnp.arange(rows)
    row_start = jnp.clip(r - kh // 2, 0, rows - kh)
    row_idx = row_start[:, None] + jnp.arange(kh)[None, :]
    k_band = kg[:, row_idx]
    v_band = vg[:, row_idx]
    scores = jnp.einsum('brqhd,brikhd->bhrqik', qg, k_band).astype(jnp.float32) * (dh ** -0.5)
    c = jnp.arange(GRID_W)
    col_start = jnp.clip(c - NA_KW // 2, 0, GRID_W - NA_KW)
    in_win = (c[None, :] >= col_start[:, None]) & (c[None, :] < col_start[:, None] + NA_KW)
    rel_row = row_idx - r[:, None] + (NA_KH_MAX - 1)
    rel_col = jnp.clip(c[None, :] - c[:, None] + (NA_KW - 1), 0, 2 * NA_KW - 2)
    bias = rpb[:, rel_row[:, None, :, None], rel_col[None, :, None, :]]
    scores = jnp.where(in_win[:, None, :], scores + bias[None].astype(jnp.float32), NEG_INF)
    probs = jax.nn.softmax(scores.reshape(b, h, rows, GRID_W, kh * GRID_W), axis=-1)
    probs = probs.reshape(b, h, rows, GRID_W, kh, GRID_W).astype(v.dtype)
    out = jnp.einsum('bhrqik,brikhd->brqhd', probs, v_band)
    return out.reshape(b, s, h * dh)


def _ssm_combine(e1, e2):
    a1, b1 = e1
    a2, b2 = e2
    return a2 * a1, a2 * b1 + b2


def s5_bidirectional(u, a_re, a_im, log_dt, b_re, b_im, c_re, c_im, d, w_glu):
    bsz, s, _ = u.shape
    f32 = jnp.float32
    u32 = u.astype(f32)
    ug = u32.reshape(bsz, s, S5_GROUPS, S5_GROUP).astype(jnp.complex64)
    y = d.astype(f32) * u32
    for direction, reverse in ((0, False), (1, True)):
        lam = lax.complex(a_re[direction].astype(f32), a_im[direction].astype(f32))
        dt = jnp.exp(log_dt[direction].astype(f32))[:, None]
        lam_bar = jnp.exp(lam * dt)
        b = lax.complex(b_re[direction].astype(f32), b_im[direction].astype(f32))
        b_bar = ((lam_bar - 1.0) / lam)[..., None] * b
        bu = jnp.einsum('bsgc,gpc->bsgp', ug, b_bar)
        a = jnp.broadcast_to(lam_bar, bu.shape)
        _, states = lax.associative_scan(_ssm_combine, (a, bu), reverse=reverse, axis=1)
        cm = lax.complex(c_re[direction].astype(f32), c_im[direction].astype(f32))
        y = y + jnp.einsum('bsgp,gcp->bsgc', states, cm).real.reshape(bsz, s, S5_WIDTH)
    z = jax.nn.gelu(y)
    out = z * jax.nn.sigmoid(z @ w_glu.astype(f32))
    return out.astype(u.dtype)


def memory_cross_attention(q, mem_n, w_mem_kv):
    b, s, _ = q.shape
    m = mem_n.shape[1]
    kv = mem_n @ w_mem_kv
    k, v = jnp.split(kv, 2, axis=-1)
    qh = q.reshape(b, s, MEM_HEADS, MEM_HEAD_DIM)
    kh = k.reshape(b, m, MEM_HEADS, MEM_HEAD_DIM)
    vh = v.reshape(b, m, MEM_HEADS, MEM_HEAD_DIM)
    scores = jnp.einsum('bshd,bmhd->bhsm', qh, kh).astype(jnp.float32) * (MEM_HEAD_DIM ** -0.5)
    probs = jax.nn.softmax(scores, axis=-1).astype(vh.dtype)
    return jnp.einsum('bhsm,bmhd->bshd', probs, vh).reshape(b, s, MEM_WIDTH)


def setup_inputs(seed: int = 0) -> dict:
    key = jax.random.key(seed)
    ks = jax.random.split(key, 32)
    f32 = jnp.float32
    L = DEPTH

    def nrm(i, shape, scale):
        return jax.random.normal(ks[i], shape, f32) * scale

    n = jnp.arange(S5_STATE, dtype=f32)
    log_dt = math.log(DT_MIN) + jax.random.uniform(ks[13], (L, 2, S5_GROUPS), f32) * (math.log(DT_MAX) - math.log(DT_MIN))
    return {
        "x": nrm(0, (BATCH, SEQ, D_MODEL), 1.0),
        "mem": nrm(1, (BATCH, MEM_TOKENS, D_MODEL), 1.0),
        "g_mix": 1.0 + nrm(2, (L, D_MODEL), 0.02),
        "g_mem": 1.0 + nrm(3, (L, D_MODEL), 0.02),
        "g_ffn": 1.0 + nrm(4, (L, D_MODEL), 0.02),
        "g_final": 1.0 + nrm(5, (D_MODEL,), 0.02),
        "w_in": nrm(6, (L, D_MODEL, IN_WIDTH), D_MODEL ** -0.5),
        "w_gate": nrm(7, (L, D_MODEL, N_BRANCH * D_MODEL), D_MODEL ** -0.5),
        "b_gate": nrm(8, (L, N_BRANCH * D_MODEL), 0.01),
        "rpb": nrm(9, (L, NA_HEADS, 2 * NA_KH_MAX - 1, 2 * NA_KW - 1), 0.02),
        "w_mem_kv": nrm(10, (L, D_MODEL, 2 * MEM_WIDTH), D_MODEL ** -0.5),
        "a_re": -0.5 + nrm(11, (L, 2, S5_GROUPS, S5_STATE), 0.01),
        "a_im": math.pi * n + nrm(12, (L, 2, S5_GROUPS, S5_STATE), 0.01),
        "log_dt": log_dt,
        "b_re": nrm(14, (L, 2, S5_GROUPS, S5_STATE, S5_GROUP), (2 * S5_GROUP) ** -0.5),
        "b_im": nrm(15, (L, 2, S5_GROUPS, S5_STATE, S5_GROUP), (2 * S5_GROUP) ** -0.5),
        "c_re": nrm(16, (L, 2, S5_GROUPS, S5_GROUP, S5_STATE), S5_STATE ** -0.5),
        "c_im": nrm(17, (L, 2, S5_GROUPS, S5_GROUP, S5_STATE), S5_STATE ** -0.5),
        "s5_d": nrm(18, (L, S5_WIDTH), 1.0),
        "w_glu": nrm(19, (L, S5_WIDTH, S5_WIDTH), S5_WIDTH ** -0.5),
        "w_branch": nrm(20, (L, N_BRANCH, BRANCH_WIDTH, D_MODEL), BRANCH_WIDTH ** -0.5),
        "w_o": nrm(21, (L, D_MODEL, D_MODEL), D_MODEL ** -0.5),
        "w_ffn1": nrm(22, (L, D_MODEL, D_FF), D_MODEL ** -0.5),
        "w_ffn3": nrm(23, (L, D_MODEL, D_FF), D_MODEL ** -0.5),
        "w_ffn2": nrm(24, (L, D_FF, D_MODEL), D_FF ** -0.5),
    }


def reference(x, mem, g_mix, g_mem, g_ffn, g_final, w_in, w_gate, b_gate, rpb, w_mem_kv,
              a_re, a_im, log_dt, b_re, b_im, c_re, c_im, s5_d, w_glu, w_branch, w_o,
              w_ffn1, w_ffn3, w_ffn2):
    bsz, s, d = x.shape
    split_at = [NA_WIDTH, 2 * NA_WIDTH, 3 * NA_WIDTH, 3 * NA_WIDTH + S5_WIDTH]
    for l in range(DEPTH):
        h = rms_norm(x, g_mix[l])
        proj = h @ w_in[l]
        q_na, k_na, v_na, u_s5, q_mem = jnp.split(proj, split_at, axis=-1)
        hd = (bsz, s, NA_HEADS, NA_HEAD_DIM)
        y_na = neighbourhood_attention(q_na.reshape(hd), k_na.reshape(hd), v_na.reshape(hd), rpb[l])
        y_s5 = s5_bidirectional(u_s5, a_re[l], a_im[l], log_dt[l], b_re[l], b_im[l],
                                c_re[l], c_im[l], s5_d[l], w_glu[l])
        y_mem = memory_cross_attention(q_mem, rms_norm(mem, g_mem[l]), w_mem_kv[l])
        ys = jnp.stack([y_na, y_s5, y_mem], axis=2)
        up = jnp.einsum('bsnc,ncd->bsnd', ys, w_branch[l])
        gates = jax.nn.sigmoid(h @ w_gate[l] + b_gate[l]).reshape(bsz, s, N_BRANCH, d)
        merged = jnp.sum(gates * up, axis=2)
        x = x + merged @ w_o[l]
        h2 = rms_norm(x, g_ffn[l])
        x = x + (jax.nn.silu(h2 @ w_ffn1[l]) * (h2 @ w_ffn3[l])) @ w_ffn2[l]
    return rms_norm(x, g_final)
```

```python
import math
import numpy as np
import ml_dtypes
from contextlib import ExitStack
import concourse.bass as bass
import concourse.mybir as mybir
from concourse.bass_utils import run_bass_kernel_spmd

F32 = mybir.dt.float32
BF16 = mybir.dt.bfloat16
I32 = mybir.dt.int32
ALU = mybir.AluOpType
AF = mybir.ActivationFunctionType
AX = mybir.AxisListType

D = 1024
NT = 2048
NHALO = 512
NALL = NT + NT + NHALO
DFF = 2816
NFC = DFF // 128
EPS = 1e-6
L = 16
TWO_PI = 6.28318
NEG = -1.0e30

DEBUG = {}
NA_LAG = 1
NO_SELF_SYNC = ()


class Res:
    __slots__ = ("lw", "rd", "dsem", "dtot", "name")

    def __init__(self, name):
        self.lw = None
        self.rd = {}
        self.dsem = None
        self.dtot = 0
        self.name = name


class KB:
    ENG = ("pe", "act", "dve", "pool", "sp")

    def __init__(self, nc, es):
        self.nc = nc
        self.es = es
        self.eng = {"pe": nc.tensor, "act": nc.scalar, "dve": nc.vector, "pool": nc.gpsimd, "sp": nc.sync}
        self.sem = {}
        self.cnt = {}
        self.waited = {}
        self.pending = {}
        for e in self.ENG:
            self.sem[e] = es.enter_context(nc.semaphore("sem_" + e))
            self.cnt[e] = 0
            self.waited[e] = {}
            self.pending[e] = []
        self.semtot = {}
        self.nres = 0
        self.nd = 0
        self.dres = []
        self.ev = {e: [] for e in self.ENG}

    def res(self, name=None):
        self.nres += 1
        return Res(name or f"r{self.nres}")

    def _need(self, deps, mark):
        if mark is None:
            return
        key, val = mark
        if deps.get(key, 0) < val:
            deps[key] = val

    def _wait_all(self, engname, r, w):
        deps = {}
        for x in r:
            self._need(deps, x.lw)
        for x in w:
            self._need(deps, x.lw)
            for k, v in x.rd.items():
                self._need(deps, (k, v))
        eng = self.eng[engname]
        wt = self.waited[engname]
        for key, val in deps.items():
            kind, obj = key
            if kind == "e":
                if obj == engname and (engname == "pe" or engname in NO_SELF_SYNC):
                    continue
                semh = self.sem[obj]
            else:
                semh = obj.dsem
                val = obj.dtot
            if wt.get(key, 0) >= val:
                continue
            eng.wait_ge(semh, val)
            wt[key] = val
            self.ev[engname].append(("w", key, val, None))

    def op(self, engname, fn, r=(), w=(), signal=True):
        self._wait_all(engname, r, w)
        ins = fn()
        self.pending[engname].append((tuple(r), tuple(w)))
        if signal:
            self.cnt[engname] += 1
            ins.then_inc(self.sem[engname], 1)
            mark = (("e", engname), self.cnt[engname])
            import sys as _s
            self.ev[engname].append(("i", mark[0], 1, _s._getframe(1).f_lineno))
            for (rr, ww) in self.pending[engname]:
                for x in rr:
                    x.rd[mark[0]] = mark[1]
                for x in ww:
                    x.lw = mark
                    x.rd = {}
            self.pending[engname] = []
        return ins

    def dma(self, out, in_, r=(), w=(), q="sp", semres=None):
        self._wait_all(q, r, w)
        sr = semres or (w[0] if w else r[0])
        if sr.dsem is None:
            self.nd += 1
            sr.dsem = self.es.enter_context(self.nc.semaphore(f"dsem{self.nd}"))
            self.dres.append(sr)
        ins = self.eng[q].dma_start(out=out, in_=in_)
        sr.dtot += 16
        ins.then_inc(sr.dsem, 16)
        mark = (("d", sr), sr.dtot)
        self.ev[q].append(("i", mark[0], 16, None))
        for x in r:
            x.rd[mark[0]] = mark[1]
        for x in w:
            x.lw = mark
            x.rd = {}
        return ins

    def barrier(self):
        sp = self.eng["sp"]
        wt = self.waited["sp"]
        for e in ("pe", "act", "dve", "pool"):
            key = ("e", e)
            if self.cnt[e] > wt.get(key, 0):
                sp.wait_ge(self.sem[e], self.cnt[e])
                wt[key] = self.cnt[e]
                self.ev["sp"].append(("w", key, self.cnt[e], None))
        for r in self.dres:
            key = ("d", r)
            if r.dtot > wt.get(key, 0):
                sp.wait_ge(r.dsem, r.dtot)
                wt[key] = r.dtot
                self.ev["sp"].append(("w", key, r.dtot, None))
        self.cnt["sp"] += 1
        sp.nop().then_inc(self.sem["sp"], 1)
        self.ev["sp"].append(("i", ("e", "sp"), 1, None))
        for e in ("pe", "act", "dve", "pool"):
            self.eng[e].wait_ge(self.sem["sp"], self.cnt["sp"])
            self.waited[e][("e", "sp")] = self.cnt["sp"]
            self.ev[e].append(("w", ("e", "sp"), self.cnt["sp"], None))

    def check_deadlock(self):
        val = {}
        ptr = {e: 0 for e in self.ENG}
        while True:
            prog = False
            for e in self.ENG:
                evs = self.ev[e]
                while ptr[e] < len(evs):
                    kind, key, v, ln = evs[ptr[e]]
                    if kind == "w":
                        if val.get(key, 0) < v:
                            break
                    else:
                        val[key] = val.get(key, 0) + v
                    ptr[e] += 1
                    prog = True
            if not prog:
                break
        stuck = {e: (ptr[e], len(self.ev[e])) for e in self.ENG if ptr[e] < len(self.ev[e])}
        if stuck:
            msg = []
            for e, (p, n) in stuck.items():
                kind, key, v, ln = self.ev[e][p]
                nxt = [x for x in self.ev[e][p:p + 6] if x[0] == "i"][:1]
                msg.append(f"{e}: at {p}/{n} waits {key[0]}:{key[1] if key[0]=='e' else key[1].name} >= {v} (have {val.get(key, 0)}) next-inc-line {nxt[0][3] if nxt else None}")
            raise RuntimeError("DEADLOCK in program:\n" + "\n".join(msg))

    def finish(self, resources):
        self._wait_all("sp", list(resources), [])


class Tile:
    def __init__(self, kb, es, name, shape, dtype, psum=False, nsub=0, side=None):
        nc = kb.nc
        if psum:
            self.t = es.enter_context(nc.psum_tensor(name, list(shape), dtype))
        elif side is not None:
            self.t = es.enter_context(nc.sbuf_tensor(name, list(shape), dtype, side=side))
        else:
            self.t = es.enter_context(nc.sbuf_tensor(name, list(shape), dtype))
        self.r = kb.res(name)
        self.sub = [kb.res(f"{name}.{i}") for i in range(nsub)]

    def __getitem__(self, idx):
        return self.t[idx]


def swpipe(n, stages, lag=1):
    S = len(stages)
    for t in range(n + (S - 1) * lag):
        for k in range(S):
            i = t - k * lag
            if 0 <= i < n:
                stages[k](i)


def swpipe_gen(n, stages, lag=1):
    S = len(stages)
    for t in range(n + (S - 1) * lag):
        for k in range(S):
            i = t - k * lag
            if 0 <= i < n:
                stages[k](i)
        yield


def bcast_free(ap2d, n):
    return ap2d.unsqueeze(1).to_broadcast([ap2d.shape[0], n, ap2d.shape[1]])


def build_nc(debug=()):
    nc = bass.Bass("TRN2", target_bir_lowering=False)
    es = ExitStack()
    with es:
        kb = KB(nc, es)
        _build(nc, kb, es, debug)
        kb.check_deadlock()
    return nc


def dram_in(nc, name, shape, dtype=F32):
    return nc.dram_tensor(name, list(shape), dtype, kind="ExternalInput").ap()


def _build(nc, kb, es, debug):
    x_all = dram_in(nc, "x_all", [NALL, D])
    mem = dram_in(nc, "mem", [256, D])
    gvec = dram_in(nc, "gvec", [4, D])
    w_in = dram_in(nc, "w_in", [D, 2560])
    out = nc.dram_tensor("out", [NT, D], F32, kind="ExternalOutput").ap()
    ident_d = dram_in(nc, "ident", [128, 128])

    def scratch(name, shape, dtype):
        kind = "ExternalOutput" if name in debug else "Internal"
        return nc.dram_tensor(name, list(shape), dtype, kind=kind).ap()

    hT_d = scratch("hT_scr", [128, 8, NALL], BF16 if "hT_scr" not in debug else F32)
    hT_res = kb.res("hT_d")

    ident = Tile(kb, es, "ident_sb", [128, 128], BF16)
    kb.dma(ident[:], ident_d, w=[ident.r], q="pool")
    gbc = None
    epsb = Tile(kb, es, "epsb", [128, 1], F32)
    kb.op("dve", lambda: nc.vector.memset(epsb[:], EPS), w=[epsb.r])
    psum = [Tile(kb, es, f"ps{i}", [128, 512], F32, psum=True) for i in range(8)]
    psi = [0]

    def next_ps(nb=8):
        p = psum[psi[0] % nb]
        psi[0] += 1
        return p

    pa = ExitStack()
    if True:
        def TileR(kb_, es_, name, shape, dtype, **kw):
            return Tile(kb_, es_, name, shape, dtype, side="right", **kw)
        gbc = TileR(kb, pa, "gbc", [128, 1, D], F32)
        kb.dma(gbc[:], gvec[0:1, :].unsqueeze(0).to_broadcast([128, 1, D]), w=[gbc.r])
        xt = [TileR(kb, pa, f"xt{i}", [128, D], F32) for i in range(4)]
        xs = [TileR(kb, pa, f"xs{i}", [128, D], BF16) for i in range(3)]
        junk = TileR(kb, pa, "junk", [128, D], BF16)
        ss = [TileR(kb, pa, f"ss{i}", [128, 4], F32) for i in range(3)]
        hst = [TileR(kb, pa, f"hst{i}", [128, 8, 128], F32 if "hT_scr" in debug else BF16) for i in range(3)]
        ntile = NALL // 128
        for i in range(2):
            kb.dma(xt[i][:], x_all[i * 128:(i + 1) * 128, :], w=[xt[i].r])
        abank = {}

        def a1(i):
            if i + 2 < ntile:
                nx = xt[(i + 2) % 4]
                kb.dma(nx[:], x_all[(i + 2) * 128:(i + 3) * 128, :], w=[nx.r])
            x_ = xt[i % 4]; s_ = ss[i % 3]; xs_ = xs[i % 3]
            kb.op("act", lambda: nc.scalar.activation(out=junk[:], in_=x_[:], func=AF.Square, accum_out=s_[:, 0:1]),
                  r=[x_.r], w=[junk.r, s_.r])
            kb.op("act", lambda: nc.scalar.activation(out=s_[:, 1:2], in_=s_[:, 0:1], func=AF.Sqrt, scale=1.0 / D, bias=epsb[:, 0:1]),
                  r=[s_.r, epsb.r], w=[s_.r])
            kb.op("dve", lambda: nc.vector.reciprocal(out=s_[:, 2:3], in_=s_[:, 1:2]), r=[s_.r], w=[s_.r])
            kb.op("dve", lambda: nc.vector.scalar_tensor_tensor(out=xs_[:], in0=x_[:], scalar=s_[:, 2:3], in1=gbc[:, 0, :],
                                                                op0=ALU.mult, op1=ALU.mult),
                  r=[x_.r, s_.r, gbc.r], w=[xs_.r])

        def a2(i):
            xs_ = xs[i % 3]
            abank[i] = []
            for half in range(2):
                ps = next_ps()
                abank[i].append(ps)
                for k in range(4):
                    kc = half * 4 + k
                    kb.op("pe", lambda: nc.tensor.matmul(ps[:, k * 128:(k + 1) * 128], lhsT=xs_[:, kc * 128:(kc + 1) * 128],
                                                         rhs=ident[:], start=True, stop=True),
                          r=[xs_.r, ident.r], w=[ps.r], signal=(k == 3))

        def a3(i):
            h_ = hst[i % 3]
            for half, ps in enumerate(abank.pop(i)):
                kb.op("act", lambda: nc.scalar.copy(out=h_[:, half * 4:(half + 1) * 4, :],
                                                    in_=ps[:].rearrange("p (k t) -> p k t", k=4)),
                      r=[ps.r], w=[h_.r])
            kb.dma(hT_d[:, :, i * 128:(i + 1) * 128], h_[:], r=[h_.r], w=[hT_res], semres=hT_res)

        agen = swpipe_gen(ntile, [a1, a2, a3], lag=1)

    def pump(n=1):
        for _ in range(n):
            try:
                next(agen)
            except StopIteration:
                return

    def finish_A():
        pump(10 ** 6)
        if pa is not None:
            pa.close()
        kb.barrier()
    pump(3)
    ctx = dict(nc=nc, kb=kb, es=es, debug=debug, pump=pump, finish_A=finish_A, scratch=scratch, hT_d=hT_d, hT_res=hT_res, ident=ident, gbc=gbc,
               next_ps=next_ps, w_in=w_in, x_all=x_all, out=out, epsb=epsb)
    fin = [hT_res]
    ctx["psum"] = psum
    ctx["mem"] = mem
    ctx["gvec"] = gvec
    if "stopA" in debug:
        finish_A()
    if "stopA" not in debug:
        fin += phase_s5(ctx)
    if "stopS" not in debug and "stopA" not in debug and "stopS1" not in debug and "stopS2" not in debug:
        fin += phase_attn(ctx)
        if "stopN" not in debug:
            fin += phase_merge_ffn(ctx)
    kb.finish(fin)
    if "right_es" in ctx:
        ctx["right_es"].close()


def kv_layout():
    off = {}
    vecs = []

    def add(name, v):
        off[name] = (sum(len(x) for x in vecs), len(v))
        vecs.append(np.asarray(v, np.float32))
    i16 = np.arange(16)
    t = np.arange(129)
    add("i", i16)
    add("ni", -i16)
    add("preF_a", 15 - 16 * t)
    add("c15", np.full(129, 15.0))
    add("preB_a", -16 * t)
    add("c0", np.zeros(129))
    add("postF_a", 1 + 16 * t)
    add("c1", np.ones(129))
    add("postB_a", 16 + 16 * t)
    add("c16", np.full(129, 16.0))
    add("c2048", np.full(129, 2048.0))
    return np.concatenate(vecs), off


NKV = len(kv_layout()[0])


def phase_s5(ctx):
    nc, kb, es = ctx["nc"], ctx["kb"], ctx["es"]
    scratch, debug = ctx["scratch"], ctx["debug"]
    ident, next_ps = ctx["ident"], ctx["next_ps"]
    hT_d, hT_res, w_in = ctx["hT_d"], ctx["hT_res"], ctx["w_in"]
    V, G, A, PE = nc.vector, nc.gpsimd, nc.scalar, nc.tensor
    _, koff = kv_layout()
    pump, finish_A = ctx["pump"], ctx["finish_A"]

    s5_small = dram_in(nc, "s5_small", [128, 2, 3, 16])
    s5_bc = dram_in(nc, "s5_bc", [128, 2, 4, 256])
    s5_dbc = dram_in(nc, "s5_dbc", [128, 512])
    kv_d = dram_in(nc, "kv", [128, NKV])
    msk_d = dram_in(nc, "s5_mask", [128, 2, 2, 256])
    flags_d = dram_in(nc, "flags", [128, 2])
    w_glu_d = dram_in(nc, "w_glu", [512, 512])
    dbgf = lambda n: BF16
    Q_d = scratch("Q_scr", [128, 16 * 2 * 2 * 256], dbgf("Q_scr")); Q_res = kb.res("Q_d")
    T_d = scratch("T_scr", [128, 32 * 2 * 256], dbgf("T_scr")); T_res = kb.res("T_d")
    Ug_d = scratch("Ug_scr", [128, 32 * 2 * 128], BF16); Ug_res = kb.res("Ug_d")
    Xp_d = scratch("Xp_scr", [128, 2 * 2 * 16 * 128], dbgf("Xp_scr")); Xp_res = kb.res("Xp_d")
    s5T_d = scratch("s5T_scr", [128, 4, NT], dbgf("s5T_scr")); s5T_res = kb.res("s5T_d")
    ctx["s5T_d"], ctx["s5T_res"] = s5T_d, s5T_res

    dumps = []

    def dump(name, ap, shape, res):
        if name in debug:
            dd = nc.dram_tensor(name, list(shape), F32, kind="ExternalOutput").ap()
            rr = kb.res(name)
            kb.dma(dd, ap, r=[res], w=[rr], q="pool")
            dumps.append(rr)

    with ExitStack() as p5:
        sm = Tile(kb, p5, "s5sm", [128, 2, 3, 16], F32)
        kb.dma(sm[:], s5_small, w=[sm.r])
        kv = Tile(kb, p5, "kv_sb", [128, NKV], F32)
        kb.dma(kv[:], kv_d, w=[kv.r])
        flags = Tile(kb, p5, "flags_sb", [128, 2], F32)
        kb.dma(flags[:], flags_d, w=[flags.r])
        par = Tile(kb, p5, "s5par", [128, 2, 8, 16], F32)
        Eend = Tile(kb, p5, "s5Eend", [128, 2, 2, 16], F32)
        KkT = Tile(kb, p5, "s5KkT", [128, 16 * 2 * 2 * 2 * 128], BF16)

        def kvv(name, T):
            o, n = koff[name]
            return kv[:, o:o + T]

        for d in range(2):
            pr = par
            kb.op("act", lambda: A.activation(out=pr[:, d, 0, :], in_=sm[:, d, 2, :], func=AF.Exp), r=[sm.r], w=[pr.r])
            kb.op("dve", lambda: V.tensor_tensor(out=pr[:, d, 1, :], in0=sm[:, d, 0, :], in1=pr[:, d, 0, :], op=ALU.mult), r=[sm.r, pr.r], w=[pr.r])
            kb.op("dve", lambda: V.tensor_tensor(out=pr[:, d, 2, :], in0=sm[:, d, 1, :], in1=pr[:, d, 0, :], op=ALU.mult), r=[sm.r, pr.r], w=[pr.r])
            kb.op("dve", lambda: V.tensor_scalar(out=pr[:, d, 2, :], in0=pr[:, d, 2, :], scalar1=1.0 / (2.0 * math.pi), scalar2=None, op0=ALU.mult), r=[pr.r], w=[pr.r])
            kb.op("act", lambda: A.activation(out=pr[:, d, 3, :], in_=pr[:, d, 1, :], func=AF.Exp, scale=16.0), r=[pr.r], w=[pr.r])

        def tab(st, dre, dim, d, kang, kmag, T, wres):
            tr, ti, tf = st
            n = 16 * T
            r3 = tr[:, 0:n].rearrange("p (g t) -> p g t", g=16)
            i3 = ti[:, 0:n].rearrange("p (g t) -> p g t", g=16)
            f3 = tf[:, 0:n].rearrange("p (g t) -> p g t", g=16)
            thb = par[:, d, 2, :].unsqueeze(2).to_broadcast([128, 16, T])
            arb = par[:, d, 1, :].unsqueeze(2).to_broadcast([128, 16, T])
            kab = kang.unsqueeze(1).to_broadcast([128, 16, T])
            kmb = kmag.unsqueeze(1).to_broadcast([128, 16, T])
            kb.op("dve", lambda: V.tensor_tensor(out=r3, in0=thb, in1=kab, op=ALU.mult), r=[par.r, kv.r], w=[tr.r])
            for which, dst in ((0, dim), (1, dre)):
                if which == 1:
                    kb.op("dve", lambda: V.tensor_scalar_add(out=r3, in0=r3, scalar1=0.25), r=[tr.r], w=[tr.r])
                kb.op("dve", lambda: V.tensor_copy(out=i3, in_=r3), r=[tr.r], w=[ti.r])
                kb.op("dve", lambda: V.tensor_copy(out=f3, in_=i3), r=[ti.r], w=[tf.r])
                kb.op("dve", lambda: V.tensor_tensor(out=f3, in0=r3, in1=f3, op=ALU.subtract), r=[tr.r, tf.r], w=[tf.r])
                kb.op("act", lambda: A.activation(out=dst, in_=f3, func=AF.Sin, scale=TWO_PI), r=[tf.r], w=[wres])
            kb.op("dve", lambda: V.tensor_tensor(out=f3, in0=arb, in1=kmb, op=ALU.mult), r=[par.r, kv.r], w=[tf.r])
            kb.op("act", lambda: A.activation(out=f3, in_=f3, func=AF.Exp), r=[tf.r], w=[tf.r])
            kb.op("dve", lambda: V.tensor_tensor(out=dre, in0=dre, in1=f3, op=ALU.mult), r=[tf.r, wres], w=[wres])
            kb.op("dve", lambda: V.tensor_tensor(out=dim, in0=dim, in1=f3, op=ALU.mult), r=[tf.r, wres], w=[wres])
            pump(2)

        with ExitStack() as s1:
            bc = Tile(kb, s1, "s5bc", [128, 2, 4, 256], F32)
            kb.dma(bc[:], s5_bc, w=[bc.r])
            dbc = Tile(kb, s1, "s5dbc", [128, 512], F32)
            kb.dma(dbc[:], s5_dbc, w=[dbc.r])
            Et = Tile(kb, s1, "s5E", [128, 2, 2, 2, 256], F32)
            Bk = Tile(kb, s1, "s5Bk", [128, 2, 2, 256], F32)
            st = (Tile(kb, s1, "tb_r", [128, 16 * 16], F32), Tile(kb, s1, "tb_i", [128, 16 * 16], I32),
                  Tile(kb, s1, "tb_f", [128, 16 * 16], F32))
            for d in range(2):
                for sg, nm in ((0, "i"), (1, "ni")):
                    v3 = lambda part: Et[:, d, sg, part, :].rearrange("p (g t) -> p g t", g=16)
                    tab(st, v3(0), v3(1), d, kvv(nm, 16), kvv(nm, 16), 16, Et.r)
            for d in range(2):
                pr = par
                E3 = lambda part: Et[:, d, 0, part, :].rearrange("p (g t) -> p g t", g=16)
                lr, li = E3(0)[:, :, 1], E3(1)[:, :, 1]
                ar_, ai_ = sm[:, d, 0, :], sm[:, d, 1, :]
                t6, t7, kr, ki = pr[:, d, 6, :], pr[:, d, 7, :], pr[:, d, 4, :], pr[:, d, 5, :]
                o = lambda fn: kb.op("dve", fn, r=[pr.r, sm.r, Et.r], w=[pr.r])
                o(lambda: V.tensor_scalar_add(out=t6, in0=lr, scalar1=-1.0))
                o(lambda: V.tensor_tensor(out=kr, in0=t6, in1=ar_, op=ALU.mult))
                o(lambda: V.tensor_tensor(out=t7, in0=li, in1=ai_, op=ALU.mult))
                o(lambda: V.tensor_tensor(out=kr, in0=kr, in1=t7, op=ALU.add))
                o(lambda: V.tensor_tensor(out=ki, in0=li, in1=ar_, op=ALU.mult))
                o(lambda: V.tensor_tensor(out=t7, in0=t6, in1=ai_, op=ALU.mult))
                o(lambda: V.tensor_tensor(out=ki, in0=ki, in1=t7, op=ALU.subtract))
                o(lambda: V.tensor_tensor(out=t6, in0=ar_, in1=ar_, op=ALU.mult))
                o(lambda: V.tensor_tensor(out=t7, in0=ai_, in1=ai_, op=ALU.mult))
                o(lambda: V.tensor_tensor(out=t6, in0=t6, in1=t7, op=ALU.add))
                o(lambda: V.reciprocal(out=t6, in_=t6))
                o(lambda: V.tensor_tensor(out=kr, in0=kr, in1=t6, op=ALU.mult))
                o(lambda: V.tensor_tensor(out=ki, in0=ki, in1=t6, op=ALU.mult))
                Br = bc[:, d, 0, :].rearrange("p (g c) -> p g c", g=16)
                Bi = bc[:, d, 1, :].rearrange("p (g c) -> p g c", g=16)
                krb = kr.unsqueeze(2).to_broadcast([128, 16, 16])
                kib = ki.unsqueeze(2).to_broadcast([128, 16, 16])
                t1 = st[0][:, 0:256].rearrange("p (g c) -> p g c", g=16)
                t2 = st[2][:, 0:256].rearrange("p (g c) -> p g c", g=16)
                Bkr = Bk[:, d, 0, :].rearrange("p (g c) -> p g c", g=16)
                Bki = Bk[:, d, 1, :].rearrange("p (g c) -> p g c", g=16)
                o2 = lambda fn: kb.op("dve", fn, r=[pr.r, bc.r, st[0].r, st[2].r, Bk.r], w=[st[0].r, st[2].r, Bk.r])
                o2(lambda: V.tensor_tensor(out=t1, in0=Br, in1=krb, op=ALU.mult))
                o2(lambda: V.tensor_tensor(out=t2, in0=Bi, in1=kib, op=ALU.mult))
                o2(lambda: V.tensor_tensor(out=Bkr, in0=t1, in1=t2, op=ALU.subtract))
                o2(lambda: V.tensor_tensor(out=t1, in0=Bi, in1=krb, op=ALU.mult))
                o2(lambda: V.tensor_tensor(out=t2, in0=Br, in1=kib, op=ALU.mult))
                o2(lambda: V.tensor_tensor(out=Bki, in0=t1, in1=t2, op=ALU.add))

            Kk = Tile(kb, s1, "s5Kk", [128, 16, 2, 2, 256], BF16)
            Qt = Tile(kb, s1, "s5Q", [128, 16, 2, 2, 256], BF16)
            s1o = ExitStack()
            o1 = Tile(kb, s1o, "s5o1", [128, 4096], F32)
            o2t = Tile(kb, s1o, "s5o2", [128, 4096], F32)

            o1r = [kb.res("o1a"), kb.res("o1b")]
            o2r = [kb.res("o2a"), kb.res("o2b")]
            GS = 12

            def outer(dst, d, Mr, Mi, sg, neg_im):
                Er = Et[:, d, sg, 0, :].rearrange("p (g t) -> p g t", g=16).unsqueeze(3).to_broadcast([128, 16, 16, 16])
                Ei = Et[:, d, sg, 1, :].rearrange("p (g t) -> p g t", g=16).unsqueeze(3).to_broadcast([128, 16, 16, 16])
                Mrb = Mr.rearrange("p (g c) -> p g c", g=16).unsqueeze(2).to_broadcast([128, 16, 16, 16])
                Mib = Mi.rearrange("p (g c) -> p g c", g=16).unsqueeze(2).to_broadcast([128, 16, 16, 16])
                a1 = o1[:, :].rearrange("p (g i c) -> p g i c", g=16, i=16)
                a2 = o2t[:, :].rearrange("p (g i c) -> p g i c", g=16, i=16)
                dre = dst[:, :, d, 0, :].rearrange("p g (i c) -> p g i c", i=16)
                dimm = dst[:, :, d, 1, :].rearrange("p g (i c) -> p g i c", i=16)
                rr = [Et.r, Bk.r, bc.r]

                def mul2(o, orr, x, y):
                    kb.op("dve", lambda: V.tensor_tensor(out=o[:, 0:GS], in0=x[:, 0:GS], in1=y[:, 0:GS], op=ALU.mult), r=rr, w=[orr[0]])
                    kb.op("pool", lambda: G.tensor_tensor(out=o[:, GS:16], in0=x[:, GS:16], in1=y[:, GS:16], op=ALU.mult), r=rr, w=[orr[1]])
                mul2(a1, o1r, Mrb, Er)
                pump(1)
                mul2(a2, o2r, Mib, Ei)
                pump(1)
                kb.op("dve", lambda: V.tensor_tensor(out=dre, in0=a1, in1=a2, op=ALU.subtract), r=o1r + o2r, w=[dst.r])
                pump(1)
                mul2(a1, o1r, Mrb, Ei)
                pump(1)
                mul2(a2, o2r, Mib, Er)
                pump(1)
                if neg_im:
                    kb.op("dve", lambda: V.scalar_tensor_tensor(out=dimm, in0=a1, scalar=-1.0, in1=a2, op0=ALU.mult, op1=ALU.subtract),
                          r=o1r + o2r, w=[dst.r])
                else:
                    kb.op("dve", lambda: V.tensor_tensor(out=dimm, in0=a1, in1=a2, op=ALU.add), r=o1r + o2r, w=[dst.r])

            outer(Kk, 0, Bk[:, 0, 0, :], Bk[:, 0, 1, :], 1, False)
            outer(Kk, 1, Bk[:, 1, 0, :], Bk[:, 1, 1, :], 0, False)
            outer(Qt, 0, bc[:, 0, 2, :], bc[:, 0, 3, :], 0, True)
            outer(Qt, 1, bc[:, 1, 2, :], bc[:, 1, 3, :], 1, True)
            if True:
                kb.dma(Q_d, Qt[:].rearrange("p g d t n -> p (g d t n)"), r=[Qt.r], w=[Q_res])

            s1o.close()
            kb.barrier()
            n = 0
            ps = None
            for g2 in range(16):
                for d in range(2):
                    for part in range(2):
                        for kc2 in range(2):
                            if n % 4 == 0:
                                ps = next_ps()
                            kb.op("pe", lambda: PE.matmul(ps[:, (n % 4) * 128:(n % 4 + 1) * 128], lhsT=Kk[:, g2, d, part, kc2 * 128:(kc2 + 1) * 128],
                                                          rhs=ident[:], start=True, stop=True), r=[Kk.r, ident.r], w=[ps.r], signal=(n % 4 == 3))
                            if n % 4 == 3:
                                kb.op("act", lambda: A.copy(out=KkT[:, (n - 3) * 128:(n + 1) * 128], in_=ps[:]), r=[ps.r], w=[KkT.r])
                                if n % 16 == 15:
                                    pump(1)
                            n += 1

            msk = Tile(kb, s1, "s5msk", [128, 2, 2, 256], F32)
            kb.dma(msk[:], msk_d, w=[msk.r])
            Tts = [Tile(kb, s1, f"s5T{i}", [128, 256], BF16) for i in range(4)]
            Dm = [Tile(kb, s1, f"s5Dm{i}", [128, 128], BF16) for i in range(4)]
            pfs = [Tile(kb, s1, f"s5pf{i}", [128, 512], F32) for i in range(4)]
            tq = [Tile(kb, s1, f"s5tq{i}", [128, 512], F32) for i in range(4)]
            tgb = {}

            def tg1(it):
                g, kc2 = it // 2, it % 2
                gp, g2 = g % 2, g // 2
                dm = Dm[g % 4]
                if kc2 == 0:
                    kb.op("pool", lambda: G.tensor_tensor(out=dm[:].rearrange("p (i c) -> p i c", i=8), in0=ident[:].rearrange("p (i c) -> p i c", i=8),
                                                           in1=dbc[:, g * 16:(g + 1) * 16].unsqueeze(1).to_broadcast([128, 8, 16]), op=ALU.mult),
                          r=[ident.r, dbc.r], w=[dm.r])
                ps = next_ps()
                tgb[it] = ps
                mms = []
                for d in range(2):
                    for hh in range(2):
                        with_d = (d == 1 and hh == kc2)
                        for part in range(2):
                            mms.append((ps[:, d * 256 + hh * 128:d * 256 + (hh + 1) * 128],
                                        Kk[gp * 64:(gp + 1) * 64, g2, d, part, kc2 * 128:(kc2 + 1) * 128],
                                        Qt[gp * 64:(gp + 1) * 64, g2, d, part, hh * 128:(hh + 1) * 128],
                                        part == 0, part == 1 and not with_d))
                        if with_d:
                            mms.append((ps[:, 256 + kc2 * 128:256 + (kc2 + 1) * 128], ident[:], dm[:], False, True))
                for mi, (o_, l_, r_, st_, sp_) in enumerate(mms):
                    kb.op("pe", lambda: PE.matmul(o_, lhsT=l_, rhs=r_, start=st_, stop=sp_), r=[Kk.r, Qt.r, ident.r, dm.r], w=[ps.r],
                          signal=(mi == len(mms) - 1))

            def tg2(it):
                pf = pfs[it % 4]
                ps = tgb.pop(it)
                kb.op("act", lambda: A.copy(out=pf[:], in_=ps[:]), r=[ps.r], w=[pf.r])

            def tg3(it):
                g, kc2 = it // 2, it % 2
                pf = pfs[it % 4]; tt = tq[it % 4]
                kb.op("dve", lambda: V.tensor_tensor(out=tt[:].rearrange("p (a n) -> p a n", a=2), in0=pf[:].rearrange("p (a n) -> p a n", a=2),
                                                      in1=msk[:, :, kc2, :], op=ALU.mult), r=[pf.r, msk.r], w=[tt.r])
                Tt = Tts[it % 4]
                kb.op("dve", lambda: V.tensor_tensor(out=Tt[:], in0=tt[:, 0:256], in1=tt[:, 256:512], op=ALU.add), r=[tt.r], w=[Tt.r])
                kb.dma(T_d[:, it * 256:(it + 1) * 256], Tt[:], r=[Tt.r], w=[T_res], semres=T_res)
                if it % 6 == 5:
                    pump(1)

            swpipe(64, [tg1, tg2, tg3], lag=1)
        kb.barrier()
        if "stopS1" in debug:
            return [Q_res, T_res]
        def do_half(hidx, own):
            off = 0 if own else NT
            with ExitStack() as s2:
                Ug = Tile(kb, s2, f"s5Ug{hidx}", [128, 32 * 2 * 128], BF16)
                with ExitStack() as s2a:
                    Up = Tile(kb, s2a, f"s5Up{hidx}", [128, 32 * 256], BF16)
                    hT = Tile(kb, s2a, f"s5hT{hidx}", [128, 8, NT], BF16)
                    kb.dma(hT[:], hT_d[:, :, off:off + NT], r=[hT_res], w=[hT.r])
                    Up4 = Up[:, :].rearrange("p (g i c) -> p g i c", g=32, i=16)
                    for i in range(16):
                        ps = next_ps()
                        for kc in range(8):
                            kb.op("pe", lambda: PE.matmul(ps[:, :], lhsT=hT[:, kc, i:NT:16], rhs=w_s5[:, kc, :], start=(kc == 0), stop=(kc == 7)),
                                  r=[hT.r, w_s5.r], w=[ps.r], signal=(kc == 7))
                        src = ps[:, :].rearrange("p (g c) -> p g c", g=32)
                        if i % 2 == 0:
                            kb.op("act", lambda: A.copy(out=Up4[:, :, i, :], in_=src), r=[ps.r], w=[Up.r])
                        else:
                            kb.op("dve", lambda: V.tensor_copy(out=Up4[:, :, i, :], in_=src), r=[ps.r], w=[Up.r])
                    n = 0
                    for g in range(32):
                        for kc2 in range(2):
                            if n % 4 == 0:
                                ps = next_ps()
                            kb.op("pe", lambda: PE.matmul(ps[:, (n % 4) * 128:(n % 4 + 1) * 128], lhsT=Up[:, g * 256 + kc2 * 128:g * 256 + (kc2 + 1) * 128],
                                                          rhs=ident[:], start=True, stop=True), r=[Up.r, ident.r], w=[ps.r], signal=(n % 4 == 3))
                            if n % 4 == 3:
                                if (n // 4) % 2 == 0:
                                    kb.op("act", lambda: A.copy(out=Ug[:, (n - 3) * 128:(n + 1) * 128], in_=ps[:]), r=[ps.r], w=[Ug.r])
                                else:
                                    kb.op("dve", lambda: V.tensor_copy(out=Ug[:, (n - 3) * 128:(n + 1) * 128], in_=ps[:]), r=[ps.r], w=[Ug.r])
                            n += 1
                kb.barrier()
                if own:
                    kb.dma(Ug_d, Ug[:], r=[Ug.r], w=[Ug_res])
                    dump("dbg_Ug", Ug[:], [128, 8192], Ug.r)
                    dump("dbg_KkT", KkT[:], [128, 16384], KkT.r)
                    dump("dbg_ws5", w_s5[:], [128, 8, 512], w_s5.r)
                Ssb = Tile(kb, s2, f"s5Ssb{hidx}", [128, 2, 16, 128], F32)
                d1 = Tile(kb, s2, f"s5d1{hidx}", [128, 2, 16, 129], F32)
                d0 = Tile(kb, s2, f"s5d0{hidx}", [128, 16, 129], F32)
                Wt = Tile(kb, s2, f"s5W{hidx}", [128, 2, 16, 129], F32)
                Xp = Tile(kb, s2, f"s5Xp{hidx}", [128, 2, 2, 16, 128], BF16) if own else None
                for d in range(2):
                    tre, tim = PRE[d]
                    for part in range(2):
                        for b4 in range(4):
                            ps = next_ps()
                            for q in range(4):
                                g2 = b4 * 4 + q
                                for gp in range(2):
                                    g = 2 * g2 + gp
                                    for kc2 in range(2):
                                        base = (((g2 * 2 + d) * 2 + part) * 2 + kc2) * 128
                                        kb.op("pe", lambda: PE.matmul(ps[gp * 64:(gp + 1) * 64, q * 128:(q + 1) * 128],
                                                                      lhsT=KkT[:, base + gp * 64:base + (gp + 1) * 64],
                                                                      rhs=Ug[:, (g * 2 + kc2) * 128:(g * 2 + kc2 + 1) * 128],
                                                                      start=(kc2 == 0), stop=(kc2 == 1)),
                                              r=[KkT.r, Ug.r], w=[ps.r], signal=(q == 3 and gp == 1 and kc2 == 1))
                            kb.op("act", lambda: A.copy(out=Ssb[:, part, b4 * 4:(b4 + 1) * 4, :], in_=ps[:, :].rearrange("p (q j) -> p q j", q=4)),
                                  r=[ps.r], w=[Ssb.r])
                    Sre = Ssb[:, 0, :, :] if d == 0 else Ssb[:, 0, :, ::-1]
                    Sim = Ssb[:, 1, :, :] if d == 0 else Ssb[:, 1, :, ::-1]
                    pr_, pi_ = tre[:, :, 1:129], tim[:, :, 1:129]
                    ta, tb = Wt[:, 0, :, 0:128], Wt[:, 1, :, 0:128]
                    o_re, o_im = d1[:, 0, :, 1:129], d1[:, 1, :, 1:129]
                    rr = [Ssb.r, tre.r]
                    kb.op("dve", lambda: V.tensor_tensor(out=o_re, in0=pr_, in1=Sre, op=ALU.mult), r=rr, w=[d1.r])
                    kb.op("pool", lambda: G.tensor_tensor(out=ta, in0=pi_, in1=Sim, op=ALU.mult), r=rr, w=[Wt.r])
                    kb.op("dve", lambda: V.tensor_tensor(out=o_im, in0=pr_, in1=Sim, op=ALU.mult), r=rr, w=[d1.r])
                    kb.op("pool", lambda: G.tensor_tensor(out=tb, in0=pi_, in1=Sre, op=ALU.mult), r=rr, w=[Wt.r])
                    kb.op("dve", lambda: V.tensor_tensor(out=o_re, in0=o_re, in1=ta, op=ALU.subtract), r=[Wt.r, d1.r], w=[d1.r])
                    kb.op("dve", lambda: V.tensor_tensor(out=o_im, in0=o_im, in1=tb, op=ALU.add), r=[Wt.r, d1.r], w=[d1.r])
                    for part in range(2):
                        if own:
                            kb.op("dve", lambda: V.tensor_scalar(out=d1[:, part, :, 0], in0=Eend[:, d, part, :], scalar1=flags[:, d:d + 1], scalar2=None, op0=ALU.mult),
                                  r=[Eend.r, flags.r], w=[d1.r])
                        else:
                            kb.op("dve", lambda: V.memset(d1[:, part, :, 0], 0.0), w=[d1.r])
                    kb.op("dve", lambda: V.tensor_copy(out=d0[:, :, 1:129], in_=par[:, d, 3, :].unsqueeze(2).to_broadcast([128, 16, 128])), r=[par.r], w=[d0.r])
                    kb.op("dve", lambda: V.memset(d0[:, :, 0:1], 0.0), w=[d0.r])
                    for part in range(2):
                        kb.op("dve", lambda: V.tensor_tensor_scan(out=Wt[:, part, :, :].rearrange("p g t -> p (g t)"), data0=d0[:].rearrange("p g t -> p (g t)"),
                                                                   data1=d1[:, part, :, :].rearrange("p g t -> p (g t)"), initial=0.0, op0=ALU.mult, op1=ALU.add),
                              r=[d0.r, d1.r], w=[Wt.r])
                    if own:
                        tre, tim = POST[d]
                        Wr, Wi = Wt[:, 0, :, 0:128], Wt[:, 1, :, 0:128]
                        pr_, pi_ = tre[:, :, 0:128], tim[:, :, 0:128]
                        oR = Xp[:, d, 0, :, :] if d == 0 else Xp[:, d, 0, :, ::-1]
                        oI = Xp[:, d, 1, :, :] if d == 0 else Xp[:, d, 1, :, ::-1]
                        t1a, t1b = d1[:, 0, :, 0:128], d1[:, 1, :, 0:128]
                        t2a, t2b = Ssb[:, 0, :, :], Ssb[:, 1, :, :]
                        rr = [Wt.r, tre.r]
                        kb.op("dve", lambda: V.tensor_tensor(out=t1a, in0=pr_, in1=Wr, op=ALU.mult), r=rr, w=[d1.r])
                        kb.op("pool", lambda: G.tensor_tensor(out=t2a, in0=pi_, in1=Wi, op=ALU.mult), r=rr, w=[Ssb.r])
                        kb.op("dve", lambda: V.tensor_tensor(out=t1b, in0=pr_, in1=Wi, op=ALU.mult), r=rr, w=[d1.r])
                        kb.op("pool", lambda: G.tensor_tensor(out=t2b, in0=pi_, in1=Wr, op=ALU.mult), r=rr, w=[Ssb.r])
                        kb.op("dve", lambda: V.tensor_tensor(out=oR, in0=t1a, in1=t2a, op=ALU.subtract), r=[d1.r, Ssb.r], w=[Xp.r])
                        kb.op("dve", lambda: V.tensor_tensor(out=oI, in0=t1b, in1=t2b, op=ALU.add), r=[d1.r, Ssb.r], w=[Xp.r])
                    else:
                        tre, tim = POSTE[d]
                        Wr, Wi = Wt[:, 0, :, 128], Wt[:, 1, :, 128]
                        pr_, pi_ = tre[:, :, 0], tim[:, :, 0]
                        a1, a2 = d1[:, 0, :, 1], d1[:, 1, :, 1]
                        rr = [Wt.r, tre.r, d1.r]
                        kb.op("dve", lambda: V.tensor_tensor(out=a1, in0=pr_, in1=Wr, op=ALU.mult), r=rr, w=[d1.r])
                        kb.op("dve", lambda: V.tensor_tensor(out=a2, in0=pi_, in1=Wi, op=ALU.mult), r=rr, w=[d1.r])
                        kb.op("dve", lambda: V.tensor_tensor(out=Eend[:, d, 0, :], in0=a1, in1=a2, op=ALU.subtract), r=rr, w=[Eend.r])
                        kb.op("dve", lambda: V.tensor_tensor(out=a1, in0=pr_, in1=Wi, op=ALU.mult), r=rr, w=[d1.r])
                        kb.op("dve", lambda: V.tensor_tensor(out=a2, in0=pi_, in1=Wr, op=ALU.mult), r=rr, w=[d1.r])
                        kb.op("dve", lambda: V.tensor_tensor(out=Eend[:, d, 1, :], in0=a1, in1=a2, op=ALU.add), r=rr, w=[Eend.r])
                if own:
                    kb.dma(Xp_d, Xp[:].rearrange("p d t g j -> p (d t g j)"), r=[Xp.r], w=[Xp_res])

        with ExitStack() as s2t:
            w_s5 = Tile(kb, s2t, "w_s5", [128, 8, 512], BF16)
            kb.dma(w_s5[:], w_in.rearrange("(kc p) n -> p kc n", p=128)[:, :, 1536:2048], w=[w_s5.r], q="pool")
            PRE, POST, POSTE = [], [], []
            for d in range(2):
                PRE.append((Tile(kb, s2t, f"s5pre_r{d}", [128, 16, 129], F32), Tile(kb, s2t, f"s5pre_i{d}", [128, 16, 129], F32)))
                POST.append((Tile(kb, s2t, f"s5post_r{d}", [128, 16, 128], F32), Tile(kb, s2t, f"s5post_i{d}", [128, 16, 128], F32)))
                POSTE.append((Tile(kb, s2t, f"s5poste_r{d}", [128, 16, 1], F32), Tile(kb, s2t, f"s5poste_i{d}", [128, 16, 1], F32)))
            with ExitStack() as s2tt:
                st = (Tile(kb, s2tt, "tc_r", [128, 16 * 129], F32), Tile(kb, s2tt, "tc_i", [128, 16 * 129], I32),
                      Tile(kb, s2tt, "tc_f", [128, 16 * 129], F32))
                for d in range(2):
                    tab(st, PRE[d][0][:, :, :], PRE[d][1][:, :, :], d, kvv("preF_a" if d == 0 else "preB_a", 129), kvv("c15" if d == 0 else "c0", 129), 129, PRE[d][0].r)
                    tab(st, POST[d][0][:, :, :], POST[d][1][:, :, :], d, kvv("postF_a" if d == 0 else "postB_a", 128), kvv("c1" if d == 0 else "c16", 128), 128, POST[d][0].r)
                    tab(st, POSTE[d][0][:, :, :], POSTE[d][1][:, :, :], d, kvv("c2048", 1), kvv("c0", 1), 1, POSTE[d][0].r)
            finish_A()
            do_half(0, False)
            kb.barrier()
            do_half(1, True)
            kb.barrier()
        kb.barrier()
        if "stopS2" in debug:
            return [Q_res, T_res, Xp_res, Ug_res] + dumps

        with ExitStack() as s4:
            Tt = Tile(kb, s4, "s4T", [128, 32, 2, 256], BF16)
            kb.dma(Tt[:].rearrange("p g k n -> p (g k n)"), T_d, r=[T_res], w=[Tt.r])
            Qt = Tile(kb, s4, "s4Q", [128, 16, 2, 2, 256], BF16)
            kb.dma(Qt[:].rearrange("p g d t n -> p (g d t n)"), Q_d, r=[Q_res], w=[Qt.r])
            Ug = Tile(kb, s4, "s4Ug", [128, 32 * 2 * 128], BF16)
            kb.dma(Ug[:], Ug_d, r=[Ug_res], w=[Ug.r])
            Xp = Tile(kb, s4, "s4Xp", [128, 2, 2, 16, 128], BF16)
            kb.dma(Xp[:].rearrange("p d t g j -> p (d t g j)"), Xp_d, r=[Xp_res], w=[Xp.r])
            zp = Tile(kb, s4, "s4zp", [128, 16, 512], BF16)
            w_glu = Tile(kb, s4, "w_glu_sb", [128, 4, 512], BF16)
            kb.dma(w_glu[:], w_glu_d.rearrange("(kc p) n -> p kc n", p=128), w=[w_glu.r], q="pool")
            for g in range(32):
                gp, g2 = g % 2, g // 2
                ps = next_ps()
                reg = ps[:, 0:256]
                for kc2 in range(2):
                    kb.op("pe", lambda: PE.matmul(reg, lhsT=Ug[:, (g * 2 + kc2) * 128:(g * 2 + kc2 + 1) * 128], rhs=Tt[:, g, kc2, :], start=(kc2 == 0), stop=False),
                          r=[Ug.r, Tt.r], w=[ps.r], signal=False)
                for d in range(2):
                    for part in range(2):
                        last = (d == 1 and part == 1)
                        kb.op("pe", lambda: PE.matmul(reg, lhsT=Xp[gp * 64:(gp + 1) * 64, d, part, g2, :], rhs=Qt[gp * 64:(gp + 1) * 64, g2, d, part, :],
                                                      start=False, stop=last), r=[Xp.r, Qt.r], w=[ps.r], signal=last)
                kb.op("act", lambda: A.activation(out=zp[:, :, g * 16:(g + 1) * 16], in_=reg.rearrange("p (i c) -> p i c", i=16), func=AF.Gelu_apprx_tanh),
                      r=[ps.r], w=[zp.r])
            zT = Tile(kb, s4, "s4zT", [128, 4, NT], BF16)
            for ch in range(4):
                zv = zT[:, ch, :].rearrange("p (j i) -> p i j", i=16)
                for i in range(16):
                    if i % 4 == 0:
                        ps = next_ps()
                    kb.op("pe", lambda: PE.matmul(ps[:, (i % 4) * 128:(i % 4 + 1) * 128], lhsT=zp[:, i, ch * 128:(ch + 1) * 128], rhs=ident[:], start=True, stop=True),
                          r=[zp.r, ident.r], w=[ps.r], signal=(i % 4 == 3))
                    if i % 4 == 3:
                        kb.op("dve", lambda: V.tensor_copy(out=zv[:, i - 3:i + 1, :], in_=ps[:, :].rearrange("p (i j) -> p i j", i=4)), r=[ps.r], w=[zT.r])
            sg = [Tile(kb, s4, f"s4sg{i}", [128, 512], BF16) for i in range(2)]
            so = [Tile(kb, s4, f"s4so{i}", [128, 512], BF16) for i in range(2)]
            it = 0
            for ch in range(4):
                for tt in range(4):
                    ps = next_ps()
                    s_ = sg[it % 2]; o_ = so[it % 2]; it += 1
                    for kc in range(4):
                        kb.op("pe", lambda: PE.matmul(ps[:, :], lhsT=w_glu[:, kc, ch * 128:(ch + 1) * 128], rhs=zT[:, kc, tt * 512:(tt + 1) * 512], start=(kc == 0), stop=(kc == 3)),
                              r=[w_glu.r, zT.r], w=[ps.r], signal=(kc == 3))
                    kb.op("act", lambda: A.activation(out=s_[:], in_=ps[:, :], func=AF.Sigmoid), r=[ps.r], w=[s_.r])
                    kb.op("dve", lambda: V.tensor_tensor(out=o_[:], in0=zT[:, ch, tt * 512:(tt + 1) * 512], in1=s_[:], op=ALU.mult),
                          r=[zT.r, s_.r], w=[o_.r])
                    kb.dma(s5T_d[:, ch, tt * 512:(tt + 1) * 512], o_[:], r=[o_.r], w=[s5T_res], semres=s5T_res)
        kb.barrier()
    kb.barrier()
    return [s5T_res]

def norm_rows(ctx, pool, x_, gtile, xs_, s_, junk):
    nc, kb, epsb = ctx["nc"], ctx["kb"], ctx["epsb"]
    A, V = nc.scalar, nc.vector
    kb.op("act", lambda: A.activation(out=junk[:], in_=x_[:], func=AF.Square, accum_out=s_[:, 0:1]), r=[x_.r], w=[junk.r, s_.r])
    kb.op("act", lambda: A.activation(out=s_[:, 1:2], in_=s_[:, 0:1], func=AF.Sqrt, scale=1.0 / D, bias=epsb[:, 0:1]), r=[s_.r, epsb.r], w=[s_.r])
    kb.op("dve", lambda: V.reciprocal(out=s_[:, 2:3], in_=s_[:, 1:2]), r=[s_.r], w=[s_.r])
    kb.op("dve", lambda: V.scalar_tensor_tensor(out=xs_[:], in0=x_[:], scalar=s_[:, 2:3], in1=gtile[:], op0=ALU.mult, op1=ALU.mult),
          r=[x_.r, s_.r, gtile.r], w=[xs_.r])


def phase_attn(ctx):
    nc, kb, es = ctx["nc"], ctx["kb"], ctx["es"]
    scratch, debug = ctx["scratch"], ctx["debug"]
    ident, next_ps = ctx["ident"], ctx["next_ps"]
    hT_d, hT_res, w_in = ctx["hT_d"], ctx["hT_res"], ctx["w_in"]
    V, G, A, PE = nc.vector, nc.gpsimd, nc.scalar, nc.tensor
    mem_d = ctx["mem"]
    gvec = ctx["gvec"]
    w_mkv_d = dram_in(nc, "w_mem_kv", [D, 1024])
    ctx["w_gate_d"] = dram_in(nc, "w_gate", [D, 3072])
    bext_d = dram_in(nc, "bext", [128, 4 * 15 * 64])
    maskw_d = dram_in(nc, "maskw", [128, 64])
    rowvec_d = dram_in(nc, "rowvec", [1, 7 * 768])
    psum = ctx["psum"]
    memT_d = scratch("memT_scr", [128, 4, NT], BF16); memT_res = kb.res("memT_d")
    naT_d = scratch("naT_scr", [128, 4, NT], BF16); naT_res = kb.res("naT_d")
    ctx["memT_d"], ctx["memT_res"], ctx["naT_d"], ctx["naT_res"] = memT_d, memT_res, naT_d, naT_res
    w_in_v = w_in.rearrange("(kc p) n -> p kc n", p=128)

    with ExitStack() as pm:
        hT = Tile(kb, pm, "m_hT", [128, 8, NT], BF16)
        kb.dma(hT[:], hT_d[:, :, 0:NT], r=[hT_res], w=[hT.r])
        w_q = Tile(kb, pm, "m_wq", [128, 8, 512], BF16)
        kb.dma(w_q[:], w_in_v[:, :, 2048:2560], w=[w_q.r], q="pool")
        w_kv = Tile(kb, pm, "m_wkv", [128, 8, 1024], BF16)
        kb.dma(w_kv[:], w_mkv_d.rearrange("(kc p) n -> p kc n", p=128), w=[w_kv.r], q="pool")
        gm = Tile(kb, pm, "m_g", [128, D], F32)
        kb.dma(gm[:], gvec[1:2, :].to_broadcast([128, D]), w=[gm.r])
        memT = Tile(kb, pm, "m_memT", [128, 8, 256], BF16)
        xt = Tile(kb, pm, "m_xt", [128, D], F32); xs = Tile(kb, pm, "m_xs", [128, D], BF16)
        junk = Tile(kb, pm, "m_junk", [128, D], BF16); ss = Tile(kb, pm, "m_ss", [128, 4], F32)
        for i in range(2):
            kb.dma(xt[:], mem_d[i * 128:(i + 1) * 128, :], w=[xt.r])
            norm_rows(ctx, pm, xt, gm, xs, ss, junk)
            for half in range(2):
                ps = next_ps()
                for k in range(4):
                    kc = half * 4 + k
                    kb.op("pe", lambda: PE.matmul(ps[:, k * 128:(k + 1) * 128], lhsT=xs[:, kc * 128:(kc + 1) * 128], rhs=ident[:], start=True, stop=True),
                          r=[xs.r, ident.r], w=[ps.r], signal=(k == 3))
                kb.op("act", lambda: A.copy(out=memT[:, half * 4:(half + 1) * 4, i * 128:(i + 1) * 128], in_=ps[:].rearrange("p (k t) -> p k t", k=4)),
                      r=[ps.r], w=[memT.r])
        kTm = Tile(kb, pm, "m_kT", [128, 4, 256], BF16)
        vm = Tile(kb, pm, "m_v", [128, 2, 512], BF16)
        for h in range(4):
            ps = next_ps()
            for kc in range(8):
                kb.op("pe", lambda: PE.matmul(ps[:, 0:256], lhsT=w_kv[:, kc, h * 128:(h + 1) * 128], rhs=memT[:, kc, :], start=(kc == 0), stop=(kc == 7)),
                      r=[w_kv.r, memT.r], w=[ps.r], signal=(kc == 7))
            kb.op("act", lambda: A.copy(out=kTm[:, h, :], in_=ps[:, 0:256]), r=[ps.r], w=[kTm.r])
        for mc in range(2):
            ps = next_ps()
            for kc in range(8):
                kb.op("pe", lambda: PE.matmul(ps[:, :], lhsT=memT[:, kc, mc * 128:(mc + 1) * 128], rhs=w_kv[:, kc, 512:1024], start=(kc == 0), stop=(kc == 7)),
                      r=[w_kv.r, memT.r], w=[ps.r], signal=(kc == 7))
            kb.op("act", lambda: A.copy(out=vm[:, mc, :], in_=ps[:, :]), r=[ps.r], w=[vm.r])
        qT = Tile(kb, pm, "m_qT", [128, 4, NT], BF16)
        for h in range(4):
            for tt in range(4):
                ps = next_ps()
                for kc in range(8):
                    kb.op("pe", lambda: PE.matmul(ps[:, :], lhsT=w_q[:, kc, h * 128:(h + 1) * 128], rhs=hT[:, kc, tt * 512:(tt + 1) * 512], start=(kc == 0), stop=(kc == 7)),
                          r=[w_q.r, hT.r], w=[ps.r], signal=(kc == 7))
                if tt % 2 == 0:
                    kb.op("act", lambda: A.copy(out=qT[:, h, tt * 512:(tt + 1) * 512], in_=ps[:, :]), r=[ps.r], w=[qT.r])
                else:
                    kb.op("dve", lambda: V.tensor_copy(out=qT[:, h, tt * 512:(tt + 1) * 512], in_=ps[:, :]), r=[ps.r], w=[qT.r])
        sc = 1.0 / math.sqrt(128.0)
        MB = 8
        Pn = [Tile(kb, pm, f"m_P{i}", [128, 256], F32) for i in range(MB)]
        Pb = [Tile(kb, pm, f"m_Pb{i}", [128, 256], BF16) for i in range(MB)]
        PT = [Tile(kb, pm, f"m_PT{i}", [128, 256], BF16) for i in range(MB)]
        st = [Tile(kb, pm, f"m_st{i}", [128, 4], F32) for i in range(MB)]
        oT = Tile(kb, pm, "m_oT", [128, 4, NT], BF16)
        mpools = {"qk": [0, [0, 1]], "t": [0, [2, 3, 4]], "pv": [0, [5, 6, 7]]}

        def mps(name):
            p = mpools[name]
            b = ctx["psum"][p[1][p[0] % len(p[1])]]
            p[0] += 1
            return b
        munits = [(h, sub) for h in range(4) for sub in range(16)]
        mqk, mtb = {}, {}

        def m1(u):
            h, sub = munits[u]
            s_ = st[u % MB]
            ps = mps("qk"); mqk[u] = ps
            kb.op("pe", lambda: PE.matmul(ps[:, 0:256], lhsT=qT[:, h, sub * 128:(sub + 1) * 128], rhs=kTm[:, h, :], start=True, stop=True),
                  r=[qT.r, kTm.r], w=[ps.r])
            kb.op("dve", lambda: V.reduce_max(out=s_[:, 0:1], in_=ps[:, 0:256], axis=AX.X), r=[ps.r], w=[s_.r])
            kb.op("dve", lambda: V.tensor_scalar(out=s_[:, 1:2], in0=s_[:, 0:1], scalar1=-sc, scalar2=None, op0=ALU.mult), r=[s_.r], w=[s_.r])

        def m2(u):
            p_ = Pn[u % MB]; pb_ = Pb[u % MB]; s_ = st[u % MB]
            ps = mqk.pop(u)
            kb.op("act", lambda: A.activation(out=p_[:], in_=ps[:, 0:256], func=AF.Exp, scale=sc, bias=s_[:, 1:2], accum_out=s_[:, 2:3]),
                  r=[ps.r, s_.r], w=[p_.r, s_.r])
            kb.op("dve", lambda: V.reciprocal(out=s_[:, 3:4], in_=s_[:, 2:3]), r=[s_.r], w=[s_.r])
            kb.op("dve", lambda: V.tensor_scalar(out=pb_[:], in0=p_[:], scalar1=s_[:, 3:4], scalar2=None, op0=ALU.mult), r=[p_.r, s_.r], w=[pb_.r])

        def m3(u):
            pb_ = Pb[u % MB]
            ps2 = mps("t"); mtb[u] = ps2
            for mc in range(2):
                kb.op("pe", lambda: PE.matmul(ps2[:, mc * 128:(mc + 1) * 128], lhsT=pb_[:, mc * 128:(mc + 1) * 128], rhs=ident[:], start=True, stop=True),
                      r=[pb_.r, ident.r], w=[ps2.r], signal=(mc == 1))

        def m4(u):
            pt_ = PT[u % MB]
            ps2 = mtb.pop(u)
            kb.op("act", lambda: A.copy(out=pt_[:], in_=ps2[:, 0:256]), r=[ps2.r], w=[pt_.r])

        def m5(u):
            h, sub = munits[u]
            pt_ = PT[u % MB]
            ps3 = mps("pv")
            for mc in range(2):
                kb.op("pe", lambda: PE.matmul(ps3[:, 0:128], lhsT=vm[:, mc, h * 128:(h + 1) * 128], rhs=pt_[:, mc * 128:(mc + 1) * 128], start=(mc == 0), stop=(mc == 1)),
                      r=[vm.r, pt_.r], w=[ps3.r], signal=(mc == 1))
            kb.op("dve", lambda: V.tensor_copy(out=oT[:, h, sub * 128:(sub + 1) * 128], in_=ps3[:, 0:128]), r=[ps3.r], w=[oT.r])

        swpipe(len(munits), [m1, m2, m3, m4, m5], lag=1)
        kb.dma(memT_d, oT[:], r=[oT.r], w=[memT_res])
    kb.barrier()

    with ExitStack() as pn:
        NKV_T = 2560
        bext8 = Tile(kb, pn, "n_bext8", [128, 4, 15, 64], BF16)
        rowv = Tile(kb, pn, "n_rowv", [1, 7 * 768], BF16)
        kb.dma(rowv[:], rowvec_d, w=[rowv.r], q="pool")
        ones1 = Tile(kb, pn, "n_ones1", [1, 128], BF16)
        kb.op("dve", lambda: V.memset(ones1[:], 1.0), w=[ones1.r])
        qT = Tile(kb, pn, "n_qT", [128, 4, NT], BF16)
        kT = Tile(kb, pn, "n_kT", [128, 4, NKV_T], BF16)
        Vt = Tile(kb, pn, "n_V", [128, 40, 512], BF16)
        yna = Tile(kb, pn, "n_y", [128, 8, 512], BF16)
        ob = [Tile(kb, pn, f"n_ob{i}", [128, 512], BF16) for i in range(2)]
        pn2 = ExitStack()
        bext = Tile(kb, pn2, "n_bext", [128, 4, 15, 64], F32)
        kb.dma(bext[:].rearrange("p h r k -> p (h r k)"), bext_d, w=[bext.r])
        maskw = Tile(kb, pn2, "n_maskw", [128, 64], F32)
        kb.dma(maskw[:], maskw_d, w=[maskw.r])
        kb.op("dve", lambda: V.tensor_tensor(out=bext[:].rearrange("p h r k -> p (h r) k"), in0=bext[:].rearrange("p h r k -> p (h r) k"),
                                              in1=maskw[:].unsqueeze(1).to_broadcast([128, 60, 64]), op=ALU.add), r=[bext.r, maskw.r], w=[bext.r])
        kb.op("dve", lambda: V.tensor_scalar(out=bext8[:].rearrange("p h r k -> p (h r k)"), in0=bext[:].rearrange("p h r k -> p (h r k)"),
                                              scalar1=8.0, scalar2=None, op0=ALU.mult), r=[bext.r], w=[bext8.r])
        hk = Tile(kb, pn2, "n_hT", [128, 8, NKV_T], BF16)
        kb.dma(hk[:, :, 0:256], hT_d[:, :, 2 * NT:2 * NT + 256], r=[hT_res], w=[hk.r])
        kb.dma(hk[:, :, 256:256 + NT], hT_d[:, :, 0:NT], r=[hT_res], w=[hk.r])
        kb.dma(hk[:, :, 256 + NT:NKV_T], hT_d[:, :, 2 * NT + 256:2 * NT + 512], r=[hT_res], w=[hk.r])
        wqkv = Tile(kb, pn2, "n_w", [128, 8, 1536], BF16)
        kb.dma(wqkv[:], w_in_v[:, :, 0:1536], w=[wqkv.r], q="pool")
        n = 0
        for c4 in range(4):
            for tt in range(4):
                ps = next_ps()
                for kc in range(8):
                    kb.op("pe", lambda: PE.matmul(ps[:, :], lhsT=wqkv[:, kc, c4 * 128:(c4 + 1) * 128], rhs=hk[:, kc, 256 + tt * 512:256 + (tt + 1) * 512], start=(kc == 0), stop=(kc == 7)),
                          r=[wqkv.r, hk.r], w=[ps.r], signal=(kc == 7))
                eng = "act" if n % 2 == 0 else "dve"; n += 1
                kb.op(eng, lambda: (A.copy if eng == "act" else V.tensor_copy)(out=qT[:, c4, tt * 512:(tt + 1) * 512], in_=ps[:, :]), r=[ps.r], w=[qT.r])
            for tt in range(5):
                ps = next_ps()
                for kc in range(8):
                    kb.op("pe", lambda: PE.matmul(ps[:, :], lhsT=wqkv[:, kc, 512 + c4 * 128:512 + (c4 + 1) * 128], rhs=hk[:, kc, tt * 512:(tt + 1) * 512], start=(kc == 0), stop=(kc == 7)),
                          r=[wqkv.r, hk.r], w=[ps.r], signal=(kc == 7))
                eng = "act" if n % 2 == 0 else "dve"; n += 1
                kb.op(eng, lambda: (A.copy if eng == "act" else V.tensor_copy)(out=kT[:, c4, tt * 512:(tt + 1) * 512], in_=ps[:, :]), r=[ps.r], w=[kT.r])
        for row in range(0, 40, 2):
            ps = next_ps()
            for kc in range(8):
                kb.op("pe", lambda: PE.matmul(ps[:, :], lhsT=hk[:, kc, row * 64:row * 64 + 128], rhs=wqkv[:, kc, 1024:1536], start=(kc == 0), stop=(kc == 7)),
                      r=[wqkv.r, hk.r], w=[ps.r], signal=(kc == 7))
            eng = "act" if (row // 2) % 2 == 0 else "dve"
            kb.op(eng, lambda: (A.copy if eng == "act" else V.tensor_copy)(out=Vt[:, row, :], in_=ps[:, :]), r=[ps.r], w=[Vt.r])
        kb.dma(Vt[0:64, 1:39:2, :], Vt[64:128, 0:38:2, :], r=[Vt.r], w=[Vt.r], semres=Vt.r)
        kb.dma(Vt[64:128, 1:39:2, :], Vt[0:64, 2:40:2, :], r=[Vt.r], w=[Vt.r], semres=Vt.r)
        pn2.close()
        kb.barrier()
        Ps = [Tile(kb, pn, f"n_P{i}", [128, 768], BF16) for i in range(8)]
        PTs = [Tile(kb, pn, f"n_PT{i}", [128, 6, 128], BF16) for i in range(8)]
        sts = [Tile(kb, pn, f"n_st{i}", [128, 8], F32) for i in range(8)]
        ctx["right_es"] = ExitStack()
        ctx["wg"] = Tile(kb, ctx["right_es"], "e_wg", [128, 8, 3072], BF16, side="right")
        kb.dma(ctx["wg"][:], ctx["w_gate_d"].rearrange("(kc p) n -> p kc n", p=128), w=[ctx["wg"].r], q="pool")
        NB = 8
        pvbank = {}
        pools = {"qk": [0, [0, 1, 2, 3]], "t": [0, [4, 5]], "pv": [0, [6, 7]]}

        def pool_ps(name):
            p = pools[name]
            b = psum[p[1][p[0] % len(p[1])]]
            p[0] += 1
            return b
        units = []
        for rl in range(32):
            if rl <= 3:
                start, nrows, bs, edge = rl, 12, 3, rl
            elif rl >= 29:
                start, nrows, bs, edge = rl - 3, 12, 0, 4 + (rl - 29)
            else:
                start, nrows, bs, edge = rl, 8, 3, None
            for c4 in range(4):
                units.append((rl, c4, start, nrows, bs, edge))
        if "naInterior" in debug:
            units = [x for x in units if 4 <= x[0] < 12]
        if "naEdge" in debug:
            units = [x for x in units if x[0] < 2]
        for fl in debug:
            if fl.startswith("naR"):
                a, b = fl[3:].split("-")
                units = [x for x in units if int(a) <= x[0] < int(b)]
        ito = [0]

        def _u(u):
            rl, c4, start, nrows, bs, edge = units[u]
            parts = [(0, 8)] + ([(8, nrows - 8)] if nrows > 8 else [])
            return rl, c4, start, nrows, bs, edge, parts

        qkb = {}

        def na_s1(u):
            rl, c4, start, nrows, bs, edge, parts = _u(u)
            s_ = sts[u % NB]
            qkb[u] = []
            for pi, (r0, nr) in enumerate(parts):
                ps = pool_ps("qk")
                qkb[u].append(ps)
                for hp in range(2):
                    kb.op("pe", lambda: PE.matmul(ps[hp * 64:(hp + 1) * 64, 0:nr * 64], lhsT=qT[hp * 64:(hp + 1) * 64, c4, rl * 64:(rl + 1) * 64],
                                                  rhs=kT[hp * 64:(hp + 1) * 64, c4, (start + r0) * 64:(start + r0 + nr) * 64], start=True, stop=False,
                                                  skip_group_check=True),
                          r=[qT.r, kT.r], w=[ps.r], signal=False)
                kb.op("pe", lambda: PE.matmul(ps[:, 0:nr * 64], lhsT=ident[:], rhs=bext8[:, c4, bs + r0:bs + r0 + nr, :].rearrange("p r k -> p (r k)"),
                                              start=False, stop=(edge is None), skip_group_check=True),
                      r=[ident.r, bext8.r], w=[ps.r], signal=(edge is None))
                if edge is not None:
                    kb.op("pe", lambda: PE.matmul(ps[:, 0:nr * 64], lhsT=ones1[0:1, :], rhs=rowv[0:1, edge * 768 + r0 * 64:edge * 768 + (r0 + nr) * 64],
                                                  start=False, stop=True, skip_group_check=True),
                          r=[ones1.r, rowv.r], w=[ps.r])
                kb.op("dve", lambda: V.reduce_max(out=s_[:, 4 * pi:4 * pi + 1], in_=ps[:, 0:nr * 64], axis=AX.X), r=[ps.r], w=[s_.r])
            if len(parts) == 2:
                kb.op("dve", lambda: V.tensor_tensor(out=s_[:, 0:1], in0=s_[:, 0:1], in1=s_[:, 4:5], op=ALU.max), r=[s_.r], w=[s_.r])
            kb.op("dve", lambda: V.tensor_scalar(out=s_[:, 1:2], in0=s_[:, 0:1], scalar1=-0.125, scalar2=None, op0=ALU.mult), r=[s_.r], w=[s_.r])

        def na_s2(u):
            rl, c4, start, nrows, bs, edge, parts = _u(u)
            P_ = Ps[u % NB]; s_ = sts[u % NB]
            for pi, ((r0, nr), ps) in enumerate(zip(parts, qkb.pop(u))):
                kb.op("act", lambda: A.activation(out=P_[:, r0 * 64:(r0 + nr) * 64], in_=ps[:, 0:nr * 64], func=AF.Exp, scale=0.125, bias=s_[:, 1:2],
                                                  accum_out=s_[:, 2 + 4 * pi:3 + 4 * pi]), r=[ps.r, s_.r], w=[P_.r, s_.r])
            if len(parts) == 2:
                kb.op("dve", lambda: V.tensor_tensor(out=s_[:, 2:3], in0=s_[:, 2:3], in1=s_[:, 6:7], op=ALU.add), r=[s_.r], w=[s_.r])
            kb.op("dve", lambda: V.reciprocal(out=s_[:, 3:4], in_=s_[:, 2:3]), r=[s_.r], w=[s_.r])

        tbank = {}

        def na_s3(u):
            rl, c4, start, nrows, bs, edge, parts = _u(u)
            P_ = Ps[u % NB]
            ps = pool_ps("t")
            tbank[u] = ps
            psb = ps[:].bitcast(BF16)
            npair = nrows // 2
            for m in range(npair):
                kb.op("pe", lambda: PE.transpose(out=psb[:, m * 128:(m + 1) * 128], in_=P_[:, m * 128:(m + 1) * 128], identity=ident[:]),
                      r=[P_.r, ident.r], w=[ps.r], signal=(m == npair - 1))

        def na_s4(u):
            rl, c4, start, nrows, bs, edge, parts = _u(u)
            PT_ = PTs[u % NB]
            ps = tbank.pop(u)
            psb = ps[:].bitcast(BF16)
            npair = nrows // 2
            kb.op("act", lambda: A.copy(out=PT_[:, 0:npair, :], in_=psb[:, 0:npair * 128].rearrange("p (r k) -> p r k", k=128)), r=[ps.r], w=[PT_.r])

        def na_s5(u):
            rl, c4, start, nrows, bs, edge, parts = _u(u)
            PT_ = PTs[u % NB]; s_ = sts[u % NB]
            pvb = pool_ps("pv"); pvc = 0; pvr = pvb.r
            npair = nrows // 2
            for m in range(npair):
                kb.op("pe", lambda: PE.matmul(pvb[:, pvc:pvc + 128], lhsT=PT_[:, m, :], rhs=Vt[:, start + 2 * m, c4 * 128:(c4 + 1) * 128], start=(m == 0), stop=(m == npair - 1)),
                      r=[PT_.r, Vt.r], w=[pvr], signal=(m == npair - 1))
            kb.op("dve", lambda: V.tensor_scalar(out=yna[0:64, rl % 8, c4 * 128:c4 * 128 + 64], in0=pvb[0:64, pvc:pvc + 64], scalar1=s_[0:64, 3:4], scalar2=None, op0=ALU.mult),
                  r=[pvr, s_.r], w=[yna.r])
            kb.op("act", lambda: A.activation(out=yna[64:128, rl % 8, c4 * 128 + 64:c4 * 128 + 128], in_=pvb[64:128, pvc + 64:pvc + 128], func=AF.Copy, scale=s_[64:128, 3:4]),
                  r=[pvr, s_.r], w=[yna.r])
            if rl % 8 == 7 and c4 == 3:
                r8 = rl // 8
                for c4b in range(4):
                    ps = pool_ps("pv")
                    o_ = ob[ito[0] % 2]; ito[0] += 1
                    for rr_ in range(8):
                        for hp in range(2):
                            lo, hi = hp * 64, (hp + 1) * 64
                            kb.op("pe", lambda: PE.matmul(ps[lo:hi, rr_ * 64:(rr_ + 1) * 64], lhsT=yna[lo:hi, rr_, c4b * 128 + lo:c4b * 128 + hi], rhs=ident[lo:hi, lo:hi], start=True, stop=True),
                                  r=[yna.r, ident.r], w=[ps.r], signal=(rr_ == 7 and hp == 1))
                    kb.op("act", lambda: A.copy(out=o_[:], in_=ps[:, :]), r=[ps.r], w=[o_.r])
                    kb.dma(naT_d[:, c4b, r8 * 512:(r8 + 1) * 512], o_[:], r=[o_.r], w=[naT_res], semres=naT_res)

        swpipe(len(units), [na_s1, na_s2, na_s3, na_s4, na_s5], lag=NA_LAG)
    kb.barrier()
    return [memT_res, naT_res]


def phase_merge_ffn(ctx):
    nc, kb, es = ctx["nc"], ctx["kb"], ctx["es"]
    scratch, debug = ctx["scratch"], ctx["debug"]
    ident, next_ps = ctx["ident"], ctx["next_ps"]
    hT_d, hT_res, x_all, out = ctx["hT_d"], ctx["hT_res"], ctx["x_all"], ctx["out"]
    gvec = ctx["gvec"]
    V, G, A, PE = nc.vector, nc.gpsimd, nc.scalar, nc.tensor
    w_gate_d = ctx["w_gate_d"]
    b_gate_d = dram_in(nc, "b_gate", [128, 24])
    w_br_d = dram_in(nc, "w_branch", [3, 512, D])
    w_o_d = dram_in(nc, "w_o", [D, D])
    w1_d = dram_in(nc, "w_ffn1", [D, DFF]); w3_d = dram_in(nc, "w_ffn3", [D, DFF]); w2_d = dram_in(nc, "w_ffn2", [DFF, D])
    x1_d = scratch("x1_scr", [NT, D], F32); x1_res = kb.res("x1_d")
    ys_d = [(ctx["naT_d"], ctx["naT_res"]), (ctx["s5T_d"], ctx["s5T_res"]), (ctx["memT_d"], ctx["memT_res"])]

    pw = ExitStack()
    w1 = Tile(kb, pw, "f_w1", [128, 8, DFF], BF16)
    with ExitStack() as pe_:
        wg = ctx["wg"]
        wb = Tile(kb, pe_, "e_wb", [128, 3, 4, D], BF16)
        for n_ in range(3):
            kb.dma(wb[:, n_, :, :], w_br_d[n_].rearrange("(kc p) n -> p kc n", p=128), w=[wb.r], q="pool")
        wo = Tile(kb, pe_, "e_wo", [128, 8, D], BF16)
        kb.dma(wo[:], w_o_d.rearrange("(kc p) n -> p kc n", p=128), w=[wo.r], q="pool")
        bg = Tile(kb, pe_, "e_bg", [128, 24], F32)
        kb.dma(bg[:], b_gate_d, w=[bg.r])
        for kc in range(8):
            kb.dma(w1[:, kc, :], w1_d[kc * 128:(kc + 1) * 128, :], w=[w1.r], q="pool")
        hTs = [Tile(kb, pe_, f"e_hT{i}", [128, 8, 512], BF16) for i in range(2)]
        yss = [[Tile(kb, pe_, f"e_ys{i}_{n_}", [128, 4, 512], BF16) for n_ in range(3)] for i in range(2)]
        mT = Tile(kb, pe_, "e_mT", [128, 8, 512], BF16)
        gs = [Tile(kb, pe_, f"e_gs{i}", [128, 512], F32) for i in range(2)]
        macc = Tile(kb, pe_, "e_macc", [128, 512], F32)
        mtmp = Tile(kb, pe_, "e_mtmp", [128, 512], F32)
        xts = [Tile(kb, pe_, f"e_xt{i}", [128, D], F32) for i in range(2)]
        it = 0; ix = 0
        def load_tile(tt):
            hT = hTs[tt % 2]; ys = yss[tt % 2]
            kb.dma(hT[:], hT_d[:, :, tt * 512:(tt + 1) * 512], r=[hT_res], w=[hT.r])
            for n_ in range(3):
                kb.dma(ys[n_][:], ys_d[n_][0][:, :, tt * 512:(tt + 1) * 512], r=[ys_d[n_][1]], w=[ys[n_].r])
        load_tile(0)
        for tt in range(4):
            hT = hTs[tt % 2]; ys = yss[tt % 2]
            if tt + 1 < 4:
                load_tile(tt + 1)
            for dc in range(8):
                for n_ in range(3):
                    g_ = gs[it % 2]; it += 1
                    ps = next_ps()
                    for kc in range(8):
                        kb.op("pe", lambda: PE.matmul(ps[:, :], lhsT=wg[:, kc, n_ * 1024 + dc * 128:n_ * 1024 + (dc + 1) * 128], rhs=hT[:, kc, :], start=(kc == 0), stop=(kc == 7)),
                              r=[wg.r, hT.r], w=[ps.r], signal=(kc == 7))
                    kb.op("act", lambda: A.activation(out=g_[:], in_=ps[:, :], func=AF.Sigmoid, bias=bg[:, n_ * 8 + dc:n_ * 8 + dc + 1]), r=[ps.r, bg.r], w=[g_.r])
                    ps2 = next_ps()
                    for kc in range(4):
                        kb.op("pe", lambda: PE.matmul(ps2[:, :], lhsT=wb[:, n_, kc, dc * 128:(dc + 1) * 128], rhs=ys[n_][:, kc, :], start=(kc == 0), stop=(kc == 3)),
                              r=[wb.r, ys[n_].r], w=[ps2.r], signal=(kc == 3))
                    if n_ == 0:
                        kb.op("dve", lambda: V.tensor_tensor(out=macc[:], in0=g_[:], in1=ps2[:, :], op=ALU.mult), r=[g_.r, ps2.r], w=[macc.r])
                    else:
                        kb.op("dve", lambda: V.tensor_tensor(out=mtmp[:], in0=g_[:], in1=ps2[:, :], op=ALU.mult), r=[g_.r, ps2.r], w=[mtmp.r])
                        if n_ == 1:
                            kb.op("dve", lambda: V.tensor_tensor(out=macc[:], in0=macc[:], in1=mtmp[:], op=ALU.add), r=[macc.r, mtmp.r], w=[macc.r])
                        else:
                            kb.op("dve", lambda: V.tensor_tensor(out=mT[:, dc, :], in0=macc[:], in1=mtmp[:], op=ALU.add), r=[macc.r, mtmp.r], w=[mT.r])
            for sub in range(4):
                x_ = xts[ix % 2]; ix += 1
                t0 = tt * 512 + sub * 128
                kb.dma(x_[:], x_all[t0:t0 + 128, :], w=[x_.r])
                for half in range(2):
                    ps = next_ps()
                    for dc in range(8):
                        kb.op("pe", lambda: PE.matmul(ps[:, :], lhsT=mT[:, dc, sub * 128:(sub + 1) * 128], rhs=wo[:, dc, half * 512:(half + 1) * 512], start=(dc == 0), stop=(dc == 7)),
                              r=[mT.r, wo.r], w=[ps.r], signal=(dc == 7))
                    kb.op("dve", lambda: V.tensor_tensor(out=x_[:, half * 512:(half + 1) * 512], in0=x_[:, half * 512:(half + 1) * 512], in1=ps[:, :], op=ALU.add),
                          r=[x_.r, ps.r], w=[x_.r])
                kb.dma(x1_d[t0:t0 + 128, :], x_[:], r=[x_.r], w=[x1_res], semres=x_.r)
    ctx["right_es"].close()
    kb.barrier()

    out_res = kb.res("out_d")
    with ExitStack() as pf:
        w3 = Tile(kb, pf, "f_w3", [128, 8, DFF], BF16)
        w2 = Tile(kb, pf, "f_w2", [128, NFC, D], BF16)
        for kc in range(8):
            kb.dma(w3[:, kc, :], w3_d[kc * 128:(kc + 1) * 128, :], w=[w3.r], q="pool")
        kb.dma(w2[:], w2_d.rearrange("(fc p) n -> p fc n", p=128), w=[w2.r], q="pool")
        gf = Tile(kb, pf, "f_gf", [128, D], F32)
        kb.dma(gf[:], gvec[2:3, :].to_broadcast([128, D]), w=[gf.r])
        gfin = Tile(kb, pf, "f_gfin", [128, D], F32)
        kb.dma(gfin[:], gvec[3:4, :].to_broadcast([128, D]), w=[gfin.r])
        x1 = Tile(kb, pf, "f_x1", [128, 4, D], F32)
        xl = Tile(kb, pf, "f_xl", [128, D], F32)
        xs = [Tile(kb, pf, f"f_xs{i}", [128, D], BF16) for i in range(4)]
        ss = [Tile(kb, pf, f"f_ss{i}", [128, 4], F32) for i in range(4)]
        h2Ts = [Tile(kb, pf, f"f_h2T{i}", [128, 8, 512], BF16) for i in range(2)]
        aT = Tile(kb, pf, "f_aT", [128, NFC, 512], BF16)
        sl = [Tile(kb, pf, f"f_sl{i}", [128, 512], BF16) for i in range(1)]

        class _Sub:
            def __init__(self, t, i):
                self.t, self.i, self.r = t, i, t.r

            def __getitem__(self, idx):
                return self.t[:, self.i, :][idx]

        def prepA(tt):
            for sub in range(4):
                t0 = tt * 512 + sub * 128
                kb.dma(xl[:], x1_d[t0:t0 + 128, :], r=[x1_res], w=[xl.r])
                norm_rows(ctx, pf, xl, gf, xs[sub], ss[sub], xs[sub])

        def prepB(tt):
            h2T = h2Ts[tt % 2]
            pb = {}

            def p2(sub):
                xs_ = xs[sub]
                pb[sub] = []
                for half in range(2):
                    ps = next_ps()
                    pb[sub].append(ps)
                    for k in range(4):
                        kc = half * 4 + k
                        kb.op("pe", lambda: PE.matmul(ps[:, k * 128:(k + 1) * 128], lhsT=xs_[:, kc * 128:(kc + 1) * 128], rhs=ident[:], start=True, stop=True),
                              r=[xs_.r, ident.r], w=[ps.r], signal=(k == 3))

            def p3(sub):
                for half, ps in enumerate(pb.pop(sub)):
                    kb.op("act", lambda: A.copy(out=h2T[:, half * 4:(half + 1) * 4, sub * 128:(sub + 1) * 128], in_=ps[:].rearrange("p (k t) -> p k t", k=4)),
                          r=[ps.r], w=[h2T.r])
            swpipe(4, [p2, p3], lag=1)

        itc = [0]

        def ffn1(tt):
            h2T = h2Ts[tt % 2]
            kb.dma(x1[:], x1_d[tt * 512:(tt + 1) * 512, :].rearrange("(s p) n -> p s n", p=128), r=[x1_res], w=[x1.r])
            for fc in range(NFC):
                s_ = sl[0]; itc[0] += 1
                ps1 = next_ps()
                for kc in range(8):
                    kb.op("pe", lambda: PE.matmul(ps1[:, :], lhsT=w1[:, kc, fc * 128:(fc + 1) * 128], rhs=h2T[:, kc, :], start=(kc == 0), stop=(kc == 7)),
                          r=[w1.r, h2T.r], w=[ps1.r], signal=(kc == 7))
                ps3 = next_ps()
                for kc in range(8):
                    kb.op("pe", lambda: PE.matmul(ps3[:, :], lhsT=w3[:, kc, fc * 128:(fc + 1) * 128], rhs=h2T[:, kc, :], start=(kc == 0), stop=(kc == 7)),
                          r=[w3.r, h2T.r], w=[ps3.r], signal=(kc == 7))
                kb.op("act", lambda: A.activation(out=s_[:], in_=ps1[:, :], func=AF.Silu), r=[ps1.r], w=[s_.r])
                kb.op("dve", lambda: V.tensor_tensor(out=aT[:, fc, :], in0=s_[:], in1=ps3[:, :], op=ALU.mult), r=[s_.r, ps3.r], w=[aT.r])

        def ffn2(tt, sub):
            s_ = ss[sub % 4]
            junk = xl
            for half in range(2):
                ps = next_ps()
                for fc in range(NFC):
                    kb.op("pe", lambda: PE.matmul(ps[:, :], lhsT=aT[:, fc, sub * 128:(sub + 1) * 128], rhs=w2[:, fc, half * 512:(half + 1) * 512], start=(fc == 0), stop=(fc == NFC - 1)),
                          r=[aT.r, w2.r], w=[ps.r], signal=(fc == NFC - 1))
                kb.op("dve", lambda: V.tensor_tensor(out=x1[:, sub, half * 512:(half + 1) * 512], in0=x1[:, sub, half * 512:(half + 1) * 512], in1=ps[:, :], op=ALU.add),
                      r=[x1.r, ps.r], w=[x1.r])
            xsub = _Sub(x1, sub)
            kb.op("act", lambda: A.activation(out=junk[:], in_=xsub[:], func=AF.Square, accum_out=s_[:, 0:1]), r=[x1.r], w=[junk.r, s_.r])
            kb.op("act", lambda: A.activation(out=s_[:, 1:2], in_=s_[:, 0:1], func=AF.Sqrt, scale=1.0 / D, bias=ctx["epsb"][:, 0:1]), r=[s_.r, ctx["epsb"].r], w=[s_.r])
            kb.op("dve", lambda: V.reciprocal(out=s_[:, 2:3], in_=s_[:, 1:2]), r=[s_.r], w=[s_.r])
            kb.op("dve", lambda: V.scalar_tensor_tensor(out=xsub[:], in0=xsub[:], scalar=s_[:, 2:3], in1=gfin[:], op0=ALU.mult, op1=ALU.mult),
                  r=[x1.r, s_.r, gfin.r], w=[x1.r])
            t0 = tt * 512 + sub * 128
            kb.dma(out[t0:t0 + 128, :], xsub[:], r=[x1.r], w=[out_res], semres=out_res)

        prepA(0)
        prepB(0)
        for tt in range(4):
            ffn1(tt)
            if tt + 1 < 4:
                prepA(tt + 1)
            ffn2(tt, 0)
            ffn2(tt, 1)
            ffn2(tt, 2)
            if tt + 1 < 4:
                prepB(tt + 1)
            ffn2(tt, 3)
    pw.close()
    return [out_res]


def s5_host(inp):
    f32 = np.float32

    def P2(a):
        return np.asarray(a, f32).reshape(16, 2, 64).transpose(1, 2, 0).reshape(128, 16)

    def P3(a):
        return np.asarray(a, f32).reshape(16, 2, 64, 16).transpose(1, 2, 0, 3).reshape(128, 256)
    small = np.zeros((128, 2, 3, 16), f32)
    bc = np.zeros((128, 2, 4, 256), f32)
    for d in range(2):
        small[:, d, 0] = P2(inp["a_re"][0, d])
        small[:, d, 1] = P2(inp["a_im"][0, d])
        small[:, d, 2] = P2(np.broadcast_to(np.asarray(inp["log_dt"][0, d])[:, None], (32, 64)))
        bc[:, d, 0] = P3(inp["b_re"][0, d])
        bc[:, d, 1] = P3(inp["b_im"][0, d])
        bc[:, d, 2] = P3(np.asarray(inp["c_re"][0, d]).transpose(0, 2, 1))
        bc[:, d, 3] = P3(np.asarray(inp["c_im"][0, d]).transpose(0, 2, 1))
    kvv, _ = kv_layout()
    ip = np.arange(128) // 16
    ii = np.arange(256) // 16
    mask = np.zeros((128, 2, 2, 256), f32)
    for kc2 in range(2):
        ia = kc2 * 8 + ip
        mask[:, 0, kc2, :] = (ii[None, :] >= ia[:, None])
        mask[:, 1, kc2, :] = (ii[None, :] <= ia[:, None])
    return {
        "s5_small": small, "s5_bc": bc,
        "s5_dbc": np.broadcast_to(np.asarray(inp["s5_d"][0], f32), (128, 512)).copy(),
        "kv": np.broadcast_to(kvv, (128, len(kvv))).copy(),
        "s5_mask": mask,
        "w_glu": np.ascontiguousarray(inp["w_glu"][0], f32),
    }


def make_in_maps(inp):
    x = np.asarray(inp["x"], np.float32)
    maps = []
    gvec = np.stack([inp["g_mix"][0], inp["g_mem"][0], inp["g_ffn"][0], inp["g_final"]]).astype(np.float32)
    s5h = s5_host(inp)
    rpb = np.asarray(inp["rpb"][0], np.float32)
    cq = np.arange(64)[:, None]; ck = np.arange(64)[None, :]
    relc = np.clip(ck - cq + 15, 0, 30)
    bg_ = rpb[:, :, relc].reshape(4, 2, 15, 64, 64)
    bext = np.ascontiguousarray(bg_.transpose(1, 3, 0, 2, 4).reshape(128, 4 * 15 * 64))
    cs = np.clip(np.arange(64) - 8, 0, 48)
    inwin = (ck >= cs[:, None]) & (ck < cs[:, None] + 16)
    maskw = np.tile(np.where(inwin, 0.0, NEG).astype(np.float32), (2, 1))
    rowmasks = []
    for half in range(2):
        rm = np.full((7, 12), NEG, np.float32)
        for e, rl in enumerate([0, 1, 2, 3, 29, 30, 31]):
            r = half * 32 + rl
            rs = min(max(r - 4, 0), 56)
            base = (rl - 4) if rl <= 3 else (rl - 7)
            nrows = 12 if rl <= 3 else 11
            for k in range(nrows):
                ra = half * 32 + base + k
                if rs <= ra < rs + 8:
                    rm[e, k] = 0.0
        rowmasks.append(np.ascontiguousarray((np.repeat(rm[:, :, None], 64, axis=2) * 1.0).reshape(1, 7 * 768)))
    for core in range(8):
        b, half = core // 2, core % 2
        own = x[b, half * NT:(half + 1) * NT]
        oth = x[b, (1 - half) * NT:(2 - half) * NT]
        halo = np.zeros((NHALO, D), np.float32)
        if half == 0:
            halo[256:512] = x[b, NT:NT + 256]
        else:
            halo[0:256] = x[b, NT - 256:NT]
        m = dict(s5h)
        m["flags"] = np.broadcast_to(np.array([1.0 if half == 1 else 0.0, 1.0 if half == 0 else 0.0], np.float32), (128, 2)).copy()
        m.update({
            "x_all": np.ascontiguousarray(np.concatenate([own, oth, halo], 0)),
            "mem": np.ascontiguousarray(inp["mem"][b]),
            "gvec": gvec,
            "w_in": np.ascontiguousarray(inp["w_in"][0]),
            "ident": np.eye(128, dtype=np.float32),
            "w_mem_kv": np.ascontiguousarray(inp["w_mem_kv"][0], np.float32),
            "bext": bext, "maskw": maskw, "rowvec": rowmasks[half],
            "w_gate": np.ascontiguousarray(inp["w_gate"][0], np.float32),
            "b_gate": np.ascontiguousarray(np.asarray(inp["b_gate"][0], np.float32).reshape(24, 128).T),
            "w_branch": np.ascontiguousarray(inp["w_branch"][0], np.float32),
            "w_o": np.ascontiguousarray(inp["w_o"][0], np.float32),
            "w_ffn1": np.ascontiguousarray(inp["w_ffn1"][0], np.float32),
            "w_ffn3": np.ascontiguousarray(inp["w_ffn3"][0], np.float32),
            "w_ffn2": np.ascontiguousarray(inp["w_ffn2"][0], np.float32),
        })
        maps.append(m)
    return maps


def kernel(**inp):
    nc = build_nc()
    maps = make_in_maps(inp)
    res = run_bass_kernel_spmd(nc, maps, core_ids=list(range(8)))
    outp = np.zeros((4, 4096, D), np.float32)
    for core in range(8):
        b, half = core // 2, core % 2
        outp[b, half * NT:(half + 1) * NT] = res.results[core]["out"]
    return outp
```

```python
import math
import numpy as np
import ml_dtypes
from contextlib import ExitStack
import concourse.bass as bass
import concourse.mybir as mybir
from concourse.bass_utils import run_bass_kernel_spmd

F32 = mybir.dt.float32
BF16 = mybir.dt.bfloat16
I32 = mybir.dt.int32
ALU = mybir.AluOpType
AF = mybir.ActivationFunctionType
AX = mybir.AxisListType

D = 1024
NT = 2048
NHALO = 512
NALL = NT + NT + NHALO
DFF = 2816
NFC = DFF // 128
EPS = 1e-6
L = 16
TWO_PI = 6.28318
NEG = -1.0e30

DEBUG = {}
NA_LAG = 1
NO_SELF_SYNC = ()


class Res:
    __slots__ = ("lw", "rd", "dsem", "dtot", "name")

    def __init__(self, name):
        self.lw = None
        self.rd = {}
        self.dsem = None
        self.dtot = 0
        self.name = name


class KB:
    ENG = ("pe", "act", "dve", "pool", "sp")

    def __init__(self, nc, es):
        self.nc = nc
        self.es = es
        self.eng = {"pe": nc.tensor, "act": nc.scalar, "dve": nc.vector, "pool": nc.gpsimd, "sp": nc.sync}
        self.sem = {}
        self.cnt = {}
        self.waited = {}
        self.pending = {}
        for e in self.ENG:
            self.sem[e] = es.enter_context(nc.semaphore("sem_" + e))
            self.cnt[e] = 0
            self.waited[e] = {}
            self.pending[e] = []
        self.semtot = {}
        self.nres = 0
        self.nd = 0
        self.dres = []
        self.ev = {e: [] for e in self.ENG}

    def res(self, name=None):
        self.nres += 1
        return Res(name or f"r{self.nres}")

    def _need(self, deps, mark):
        if mark is None:
            return
        key, val = mark
        if deps.get(key, 0) < val:
            deps[key] = val

    def _wait_all(self, engname, r, w):
        deps = {}
        for x in r:
            self._need(deps, x.lw)
        for x in w:
            self._need(deps, x.lw)
            for k, v in x.rd.items():
                self._need(deps, (k, v))
        eng = self.eng[engname]
        wt = self.waited[engname]
        for key, val in deps.items():
            kind, obj = key
            if kind == "e":
                if obj == engname and (engname == "pe" or engname in NO_SELF_SYNC):
                    continue
                semh = self.sem[obj]
            else:
                semh = obj.dsem
                val = obj.dtot
            if wt.get(key, 0) >= val:
                continue
            eng.wait_ge(semh, val)
            wt[key] = val
            self.ev[engname].append(("w", key, val, None))

    def op(self, engname, fn, r=(), w=(), signal=True):
        self._wait_all(engname, r, w)
        ins = fn()
        self.pending[engname].append((tuple(r), tuple(w)))
        if signal:
            self.cnt[engname] += 1
            ins.then_inc(self.sem[engname], 1)
            mark = (("e", engname), self.cnt[engname])
            import sys as _s
            self.ev[engname].append(("i", mark[0], 1, _s._getframe(1).f_lineno))
            for (rr, ww) in self.pending[engname]:
                for x in rr:
                    x.rd[mark[0]] = mark[1]
                for x in ww:
                    x.lw = mark
                    x.rd = {}
            self.pending[engname] = []
        return ins

    def dma(self, out, in_, r=(), w=(), q="sp", semres=None):
        self._wait_all(q, r, w)
        sr = semres or (w[0] if w else r[0])
        if sr.dsem is None:
            self.nd += 1
            sr.dsem = self.es.enter_context(self.nc.semaphore(f"dsem{self.nd}"))
            self.dres.append(sr)
        ins = self.eng[q].dma_start(out=out, in_=in_)
        sr.dtot += 16
        ins.then_inc(sr.dsem, 16)
        mark = (("d", sr), sr.dtot)
        self.ev[q].append(("i", mark[0], 16, None))
        for x in r:
            x.rd[mark[0]] = mark[1]
        for x in w:
            x.lw = mark
            x.rd = {}
        return ins

    def barrier(self):
        sp = self.eng["sp"]
        wt = self.waited["sp"]
        for e in ("pe", "act", "dve", "pool"):
            key = ("e", e)
            if self.cnt[e] > wt.get(key, 0):
                sp.wait_ge(self.sem[e], self.cnt[e])
                wt[key] = self.cnt[e]
                self.ev["sp"].append(("w", key, self.cnt[e], None))
        for r in self.dres:
            key = ("d", r)
            if r.dtot > wt.get(key, 0):
                sp.wait_ge(r.dsem, r.dtot)
                wt[key] = r.dtot
                self.ev["sp"].append(("w", key, r.dtot, None))
        self.cnt["sp"] += 1
        sp.nop().then_inc(self.sem["sp"], 1)
        self.ev["sp"].append(("i", ("e", "sp"), 1, None))
        for e in ("pe", "act", "dve", "pool"):
            self.eng[e].wait_ge(self.sem["sp"], self.cnt["sp"])
            self.waited[e][("e", "sp")] = self.cnt["sp"]
            self.ev[e].append(("w", ("e", "sp"), self.cnt["sp"], None))

    def check_deadlock(self):
        val = {}
        ptr = {e: 0 for e in self.ENG}
        while True:
            prog = False
            for e in self.ENG:
                evs = self.ev[e]
                while ptr[e] < len(evs):
                    kind, key, v, ln = evs[ptr[e]]
                    if kind == "w":
                        if val.get(key, 0) < v:
                            break
                    else:
                        val[key] = val.get(key, 0) + v
                    ptr[e] += 1
                    prog = True
            if not prog:
                break
        stuck = {e: (ptr[e], len(self.ev[e])) for e in self.ENG if ptr[e] < len(self.ev[e])}
        if stuck:
            msg = []
            for e, (p, n) in stuck.items():
                kind, key, v, ln = self.ev[e][p]
                nxt = [x for x in self.ev[e][p:p + 6] if x[0] == "i"][:1]
                msg.append(f"{e}: at {p}/{n} waits {key[0]}:{key[1] if key[0]=='e' else key[1].name} >= {v} (have {val.get(key, 0)}) next-inc-line {nxt[0][3] if nxt else None}")
            raise RuntimeError("DEADLOCK in program:\n" + "\n".join(msg))

    def finish(self, resources):
        self._wait_all("sp", list(resources), [])


class Tile:
    def __init__(self, kb, es, name, shape, dtype, psum=False, nsub=0, side=None):
        nc = kb.nc
        if psum:
            self.t = es.enter_context(nc.psum_tensor(name, list(shape), dtype))
        elif side is not None:
            self.t = es.enter_context(nc.sbuf_tensor(name, list(shape), dtype, side=side))
        else:
            self.t = es.enter_context(nc.sbuf_tensor(name, list(shape), dtype))
        self.r = kb.res(name)
        self.sub = [kb.res(f"{name}.{i}") for i in range(nsub)]

    def __getitem__(self, idx):
        return self.t[idx]


def swpipe(n, stages, lag=1):
    S = len(stages)
    for t in range(n + (S - 1) * lag):
        for k in range(S):
            i = t - k * lag
            if 0 <= i < n:
                stages[k](i)


def swpipe_gen(n, stages, lag=1):
    S = len(stages)
    for t in range(n + (S - 1) * lag):
        for k in range(S):
            i = t - k * lag
            if 0 <= i < n:
                stages[k](i)
        yield


def bcast_free(ap2d, n):
    return ap2d.unsqueeze(1).to_broadcast([ap2d.shape[0], n, ap2d.shape[1]])


def build_nc(debug=()):
    nc = bass.Bass("TRN2", target_bir_lowering=False)
    es = ExitStack()
    with es:
        kb = KB(nc, es)
        _build(nc, kb, es, debug)
        kb.check_deadlock()
    return nc


def dram_in(nc, name, shape, dtype=F32):
    return nc.dram_tensor(name, list(shape), dtype, kind="ExternalInput").ap()


def _build(nc, kb, es, debug):
    x_all = dram_in(nc, "x_all", [NALL, D])
    mem = dram_in(nc, "mem", [256, D])
    gvec = dram_in(nc, "gvec", [4, D])
    w_in = dram_in(nc, "w_in", [D, 2560])
    out = nc.dram_tensor("out", [NT, D], F32, kind="ExternalOutput").ap()
    ident_d = dram_in(nc, "ident", [128, 128])

    def scratch(name, shape, dtype):
        kind = "ExternalOutput" if name in debug else "Internal"
        return nc.dram_tensor(name, list(shape), dtype, kind=kind).ap()

    hT_d = scratch("hT_scr", [128, 8, NALL], BF16 if "hT_scr" not in debug else F32)
    hT_res = kb.res("hT_d")

    ident = Tile(kb, es, "ident_sb", [128, 128], BF16)
    kb.dma(ident[:], ident_d, w=[ident.r], q="pool")
    gbc = None
    epsb = Tile(kb, es, "epsb", [128, 1], F32)
    kb.op("dve", lambda: nc.vector.memset(epsb[:], EPS), w=[epsb.r])
    psum = [Tile(kb, es, f"ps{i}", [128, 512], F32, psum=True) for i in range(8)]
    psi = [0]

    def next_ps(nb=8):
        p = psum[psi[0] % nb]
        psi[0] += 1
        return p

    pa = ExitStack()
    if True:
        def TileR(kb_, es_, name, shape, dtype, **kw):
            return Tile(kb_, es_, name, shape, dtype, side="right", **kw)
        gbc = TileR(kb, pa, "gbc", [128, 1, D], F32)
        kb.dma(gbc[:], gvec[0:1, :].unsqueeze(0).to_broadcast([128, 1, D]), w=[gbc.r])
        xt = [TileR(kb, pa, f"xt{i}", [128, D], F32) for i in range(4)]
        xs = [TileR(kb, pa, f"xs{i}", [128, D], BF16) for i in range(3)]
        junk = TileR(kb, pa, "junk", [128, D], BF16)
        ss = [TileR(kb, pa, f"ss{i}", [128, 4], F32) for i in range(3)]
        hst = [TileR(kb, pa, f"hst{i}", [128, 8, 128], F32 if "hT_scr" in debug else BF16) for i in range(3)]
        ntile = NALL // 128
        for i in range(2):
            kb.dma(xt[i][:], x_all[i * 128:(i + 1) * 128, :], w=[xt[i].r])
        abank = {}

        def a1(i):
            if i + 2 < ntile:
                nx = xt[(i + 2) % 4]
                kb.dma(nx[:], x_all[(i + 2) * 128:(i + 3) * 128, :], w=[nx.r])
            x_ = xt[i % 4]; s_ = ss[i % 3]; xs_ = xs[i % 3]
            kb.op("act", lambda: nc.scalar.activation(out=junk[:], in_=x_[:], func=AF.Square, accum_out=s_[:, 0:1]),
                  r=[x_.r], w=[junk.r, s_.r])
            kb.op("act", lambda: nc.scalar.activation(out=s_[:, 1:2], in_=s_[:, 0:1], func=AF.Sqrt, scale=1.0 / D, bias=epsb[:, 0:1]),
                  r=[s_.r, epsb.r], w=[s_.r])
            kb.op("dve", lambda: nc.vector.reciprocal(out=s_[:, 2:3], in_=s_[:, 1:2]), r=[s_.r], w=[s_.r])
            kb.op("dve", lambda: nc.vector.scalar_tensor_tensor(out=xs_[:], in0=x_[:], scalar=s_[:, 2:3], in1=gbc[:, 0, :],
                                                                op0=ALU.mult, op1=ALU.mult),
                  r=[x_.r, s_.r, gbc.r], w=[xs_.r])

        def a2(i):
            xs_ = xs[i % 3]
            abank[i] = []
            for half in range(2):
                ps = next_ps()
                abank[i].append(ps)
                for k in range(4):
                    kc = half * 4 + k
                    kb.op("pe", lambda: nc.tensor.matmul(ps[:, k * 128:(k + 1) * 128], lhsT=xs_[:, kc * 128:(kc + 1) * 128],
                                                         rhs=ident[:], start=True, stop=True),
                          r=[xs_.r, ident.r], w=[ps.r], signal=(k == 3))

        def a3(i):
            h_ = hst[i % 3]
            for half, ps in enumerate(abank.pop(i)):
                kb.op("act", lambda: nc.scalar.copy(out=h_[:, half * 4:(half + 1) * 4, :],
                                                    in_=ps[:].rearrange("p (k t) -> p k t", k=4)),
                      r=[ps.r], w=[h_.r])
            kb.dma(hT_d[:, :, i * 128:(i + 1) * 128], h_[:], r=[h_.r], w=[hT_res], semres=hT_res)

        agen = swpipe_gen(ntile, [a1, a2, a3], lag=1)

    def pump(n=1):
        for _ in range(n):
            try:
                next(agen)
            except StopIteration:
                return

    def finish_A():
        pump(10 ** 6)
        if pa is not None:
            pa.close()
        kb.barrier()
    pump(3)
    ctx = dict(nc=nc, kb=kb, es=es, debug=debug, pump=pump, finish_A=finish_A, scratch=scratch, hT_d=hT_d, hT_res=hT_res, ident=ident, gbc=gbc,
               next_ps=next_ps, w_in=w_in, x_all=x_all, out=out, epsb=epsb)
    fin = [hT_res]
    ctx["psum"] = psum
    ctx["mem"] = mem
    ctx["gvec"] = gvec
    if "stopA" in debug:
        finish_A()
    if "stopA" not in debug:
        fin += phase_s5(ctx)
    if "stopS" not in debug and "stopA" not in debug and "stopS1" not in debug and "stopS2" not in debug:
        fin += phase_attn(ctx)
        if "stopN" not in debug:
            fin += phase_merge_ffn(ctx)
    kb.finish(fin)
    if "right_es" in ctx:
        ctx["right_es"].close()


def kv_layout():
    off = {}
    vecs = []

    def add(name, v):
        off[name] = (sum(len(x) for x in vecs), len(v))
        vecs.append(np.asarray(v, np.float32))
    i16 = np.arange(16)
    t = np.arange(129)
    add("i", i16)
    add("ni", -i16)
    add("preF_a", 15 - 16 * t)
    add("c15", np.full(129, 15.0))
    add("preB_a", -16 * t)
    add("c0", np.zeros(129))
    add("postF_a", 1 + 16 * t)
    add("c1", np.ones(129))
    add("postB_a", 16 + 16 * t)
    add("c16", np.full(129, 16.0))
    add("c2048", np.full(129, 2048.0))
    return np.concatenate(vecs), off


NKV = len(kv_layout()[0])


def phase_s5(ctx):
    nc, kb, es = ctx["nc"], ctx["kb"], ctx["es"]
    scratch, debug = ctx["scratch"], ctx["debug"]
    ident, next_ps = ctx["ident"], ctx["next_ps"]
    hT_d, hT_res, w_in = ctx["hT_d"], ctx["hT_res"], ctx["w_in"]
    V, G, A, PE = nc.vector, nc.gpsimd, nc.scalar, nc.tensor
    _, koff = kv_layout()
    pump, finish_A = ctx["pump"], ctx["finish_A"]

    s5_small = dram_in(nc, "s5_small", [128, 2, 3, 16])
    s5_bc = dram_in(nc, "s5_bc", [128, 2, 4, 256])
    s5_dbc = dram_in(nc, "s5_dbc", [128, 512])
    kv_d = dram_in(nc, "kv", [128, NKV])
    msk_d = dram_in(nc, "s5_mask", [128, 2, 2, 256])
    flags_d = dram_in(nc, "flags", [128, 2])
    w_glu_d = dram_in(nc, "w_glu", [512, 512])
    dbgf = lambda n: BF16
    Q_d = scratch("Q_scr", [128, 16 * 2 * 2 * 256], dbgf("Q_scr")); Q_res = kb.res("Q_d")
    T_d = scratch("T_scr", [128, 32 * 2 * 256], dbgf("T_scr")); T_res = kb.res("T_d")
    Ug_d = scratch("Ug_scr", [128, 32 * 2 * 128], BF16); Ug_res = kb.res("Ug_d")
    Xp_d = scratch("Xp_scr", [128, 2 * 2 * 16 * 128], dbgf("Xp_scr")); Xp_res = kb.res("Xp_d")
    s5T_d = scratch("s5T_scr", [128, 4, NT], dbgf("s5T_scr")); s5T_res = kb.res("s5T_d")
    ctx["s5T_d"], ctx["s5T_res"] = s5T_d, s5T_res

    dumps = []

    def dump(name, ap, shape, res):
        if name in debug:
            dd = nc.dram_tensor(name, list(shape), F32, kind="ExternalOutput").ap()
            rr = kb.res(name)
            kb.dma(dd, ap, r=[res], w=[rr], q="pool")
            dumps.append(rr)

    with ExitStack() as p5:
        sm = Tile(kb, p5, "s5sm", [128, 2, 3, 16], F32)
        kb.dma(sm[:], s5_small, w=[sm.r])
        kv = Tile(kb, p5, "kv_sb", [128, NKV], F32)
        kb.dma(kv[:], kv_d, w=[kv.r])
        flags = Tile(kb, p5, "flags_sb", [128, 2], F32)
        kb.dma(flags[:], flags_d, w=[flags.r])
        par = Tile(kb, p5, "s5par", [128, 2, 8, 16], F32)
        Eend = Tile(kb, p5, "s5Eend", [128, 2, 2, 16], F32)
        KkT = Tile(kb, p5, "s5KkT", [128, 16 * 2 * 2 * 2 * 128], BF16)

        def kvv(name, T):
            o, n = koff[name]
            return kv[:, o:o + T]

        for d in range(2):
            pr = par
            kb.op("act", lambda: A.activation(out=pr[:, d, 0, :], in_=sm[:, d, 2, :], func=AF.Exp), r=[sm.r], w=[pr.r])
            kb.op("dve", lambda: V.tensor_tensor(out=pr[:, d, 1, :], in0=sm[:, d, 0, :], in1=pr[:, d, 0, :], op=ALU.mult), r=[sm.r, pr.r], w=[pr.r])
            kb.op("dve", lambda: V.tensor_tensor(out=pr[:, d, 2, :], in0=sm[:, d, 1, :], in1=pr[:, d, 0, :], op=ALU.mult), r=[sm.r, pr.r], w=[pr.r])
            kb.op("dve", lambda: V.tensor_scalar(out=pr[:, d, 2, :], in0=pr[:, d, 2, :], scalar1=1.0 / (2.0 * math.pi), scalar2=None, op0=ALU.mult), r=[pr.r], w=[pr.r])
            kb.op("act", lambda: A.activation(out=pr[:, d, 3, :], in_=pr[:, d, 1, :], func=AF.Exp, scale=16.0), r=[pr.r], w=[pr.r])

        def tab(st, dre, dim, d, kang, kmag, T, wres):
            tr, ti, tf = st
            n = 16 * T
            r3 = tr[:, 0:n].rearrange("p (g t) -> p g t", g=16)
            i3 = ti[:, 0:n].rearrange("p (g t) -> p g t", g=16)
            f3 = tf[:, 0:n].rearrange("p (g t) -> p g t", g=16)
            thb = par[:, d, 2, :].unsqueeze(2).to_broadcast([128, 16, T])
            arb = par[:, d, 1, :].unsqueeze(2).to_broadcast([128, 16, T])
            kab = kang.unsqueeze(1).to_broadcast([128, 16, T])
            kmb = kmag.unsqueeze(1).to_broadcast([128, 16, T])
            kb.op("dve", lambda: V.tensor_tensor(out=r3, in0=thb, in1=kab, op=ALU.mult), r=[par.r, kv.r], w=[tr.r])
            for which, dst in ((0, dim), (1, dre)):
                if which == 1:
                    kb.op("dve", lambda: V.tensor_scalar_add(out=r3, in0=r3, scalar1=0.25), r=[tr.r], w=[tr.r])
                kb.op("dve", lambda: V.tensor_copy(out=i3, in_=r3), r=[tr.r], w=[ti.r])
                kb.op("dve", lambda: V.tensor_copy(out=f3, in_=i3), r=[ti.r], w=[tf.r])
                kb.op("dve", lambda: V.tensor_tensor(out=f3, in0=r3, in1=f3, op=ALU.subtract), r=[tr.r, tf.r], w=[tf.r])
                kb.op("act", lambda: A.activation(out=dst, in_=f3, func=AF.Sin, scale=TWO_PI), r=[tf.r], w=[wres])
            kb.op("dve", lambda: V.tensor_tensor(out=f3, in0=arb, in1=kmb, op=ALU.mult), r=[par.r, kv.r], w=[tf.r])
            kb.op("act", lambda: A.activation(out=f3, in_=f3, func=AF.Exp), r=[tf.r], w=[tf.r])
            kb.op("dve", lambda: V.tensor_tensor(out=dre, in0=dre, in1=f3, op=ALU.mult), r=[tf.r, wres], w=[wres])
            kb.op("dve", lambda: V.tensor_tensor(out=dim, in0=dim, in1=f3, op=ALU.mult), r=[tf.r, wres], w=[wres])
            pump(2)

        with ExitStack() as s1:
            bc = Tile(kb, s1, "s5bc", [128, 2, 4, 256], F32)
            kb.dma(bc[:], s5_bc, w=[bc.r])
            dbc = Tile(kb, s1, "s5dbc", [128, 512], F32)
            kb.dma(dbc[:], s5_dbc, w=[dbc.r])
            Et = Tile(kb, s1, "s5E", [128, 2, 2, 2, 256], F32)
            Bk = Tile(kb, s1, "s5Bk", [128, 2, 2, 256], F32)
            st = (Tile(kb, s1, "tb_r", [128, 16 * 16], F32), Tile(kb, s1, "tb_i", [128, 16 * 16], I32),
                  Tile(kb, s1, "tb_f", [128, 16 * 16], F32))
            for d in range(2):
                for sg, nm in ((0, "i"), (1, "ni")):
                    v3 = lambda part: Et[:, d, sg, part, :].rearrange("p (g t) -> p g t", g=16)
                    tab(st, v3(0), v3(1), d, kvv(nm, 16), kvv(nm, 16), 16, Et.r)
            for d in range(2):
                pr = par
                E3 = lambda part: Et[:, d, 0, part, :].rearrange("p (g t) -> p g t", g=16)
                lr, li = E3(0)[:, :, 1], E3(1)[:, :, 1]
                ar_, ai_ = sm[:, d, 0, :], sm[:, d, 1, :]
                t6, t7, kr, ki = pr[:, d, 6, :], pr[:, d, 7, :], pr[:, d, 4, :], pr[:, d, 5, :]
                o = lambda fn: kb.op("dve", fn, r=[pr.r, sm.r, Et.r], w=[pr.r])
                o(lambda: V.tensor_scalar_add(out=t6, in0=lr, scalar1=-1.0))
                o(lambda: V.tensor_tensor(out=kr, in0=t6, in1=ar_, op=ALU.mult))
                o(lambda: V.tensor_tensor(out=t7, in0=li, in1=ai_, op=ALU.mult))
                o(lambda: V.tensor_tensor(out=kr, in0=kr, in1=t7, op=ALU.add))
                o(lambda: V.tensor_tensor(out=ki, in0=li, in1=ar_, op=ALU.mult))
                o(lambda: V.tensor_tensor(out=t7, in0=t6, in1=ai_, op=ALU.mult))
                o(lambda: V.tensor_tensor(out=ki, in0=ki, in1=t7, op=ALU.subtract))
                o(lambda: V.tensor_tensor(out=t6, in0=ar_, in1=ar_, op=ALU.mult))
                o(lambda: V.tensor_tensor(out=t7, in0=ai_, in1=ai_, op=ALU.mult))
                o(lambda: V.tensor_tensor(out=t6, in0=t6, in1=t7, op=ALU.add))
                o(lambda: V.reciprocal(out=t6, in_=t6))
                o(lambda: V.tensor_tensor(out=kr, in0=kr, in1=t6, op=ALU.mult))
                o(lambda: V.tensor_tensor(out=ki, in0=ki, in1=t6, op=ALU.mult))
                Br = bc[:, d, 0, :].rearrange("p (g c) -> p g c", g=16)
                Bi = bc[:, d, 1, :].rearrange("p (g c) -> p g c", g=16)
                krb = kr.unsqueeze(2).to_broadcast([128, 16, 16])
                kib = ki.unsqueeze(2).to_broadcast([128, 16, 16])
                t1 = st[0][:, 0:256].rearrange("p (g c) -> p g c", g=16)
                t2 = st[2][:, 0:256].rearrange("p (g c) -> p g c", g=16)
                Bkr = Bk[:, d, 0, :].rearrange("p (g c) -> p g c", g=16)
                Bki = Bk[:, d, 1, :].rearrange("p (g c) -> p g c", g=16)
                o2 = lambda fn: kb.op("dve", fn, r=[pr.r, bc.r, st[0].r, st[2].r, Bk.r], w=[st[0].r, st[2].r, Bk.r])
                o2(lambda: V.tensor_tensor(out=t1, in0=Br, in1=krb, op=ALU.mult))
                o2(lambda: V.tensor_tensor(out=t2, in0=Bi, in1=kib, op=ALU.mult))
                o2(lambda: V.tensor_tensor(out=Bkr, in0=t1, in1=t2, op=ALU.subtract))
                o2(lambda: V.tensor_tensor(out=t1, in0=Bi, in1=krb, op=ALU.mult))
                o2(lambda: V.tensor_tensor(out=t2, in0=Br, in1=kib, op=ALU.mult))
                o2(lambda: V.tensor_tensor(out=Bki, in0=t1, in1=t2, op=ALU.add))

            Kk = Tile(kb, s1, "s5Kk", [128, 16, 2, 2, 256], BF16)
            Qt = Tile(kb, s1, "s5Q", [128, 16, 2, 2, 256], BF16)
            s1o = ExitStack()
            o1 = Tile(kb, s1o, "s5o1", [128, 4096], F32)
            o2t = Tile(kb, s1o, "s5o2", [128, 4096], F32)

            o1r = [kb.res("o1a"), kb.res("o1b")]
            o2r = [kb.res("o2a"), kb.res("o2b")]
            GS = 12

            def outer(dst, d, Mr, Mi, sg, neg_im):
                Er = Et[:, d, sg, 0, :].rearrange("p (g t) -> p g t", g=16).unsqueeze(3).to_broadcast([128, 16, 16, 16])
                Ei = Et[:, d, sg, 1, :].rearrange("p (g t) -> p g t", g=16).unsqueeze(3).to_broadcast([128, 16, 16, 16])
                Mrb = Mr.rearrange("p (g c) -> p g c", g=16).unsqueeze(2).to_broadcast([128, 16, 16, 16])
                Mib = Mi.rearrange("p (g c) -> p g c", g=16).unsqueeze(2).to_broadcast([128, 16, 16, 16])
                a1 = o1[:, :].rearrange("p (g i c) -> p g i c", g=16, i=16)
                a2 = o2t[:, :].rearrange("p (g i c) -> p g i c", g=16, i=16)
                dre = dst[:, :, d, 0, :].rearrange("p g (i c) -> p g i c", i=16)
                dimm = dst[:, :, d, 1, :].rearrange("p g (i c) -> p g i c", i=16)
                rr = [Et.r, Bk.r, bc.r]

                def mul2(o, orr, x, y):
                    kb.op("dve", lambda: V.tensor_tensor(out=o[:, 0:GS], in0=x[:, 0:GS], in1=y[:, 0:GS], op=ALU.mult), r=rr, w=[orr[0]])
                    kb.op("pool", lambda: G.tensor_tensor(out=o[:, GS:16], in0=x[:, GS:16], in1=y[:, GS:16], op=ALU.mult), r=rr, w=[orr[1]])
                mul2(a1, o1r, Mrb, Er)
                pump(1)
                mul2(a2, o2r, Mib, Ei)
                pump(1)
                kb.op("dve", lambda: V.tensor_tensor(out=dre, in0=a1, in1=a2, op=ALU.subtract), r=o1r + o2r, w=[dst.r])
                pump(1)
                mul2(a1, o1r, Mrb, Ei)
                pump(1)
                mul2(a2, o2r, Mib, Er)
                pump(1)
                if neg_im:
                    kb.op("dve", lambda: V.scalar_tensor_tensor(out=dimm, in0=a1, scalar=-1.0, in1=a2, op0=ALU.mult, op1=ALU.subtract),
                          r=o1r + o2r, w=[dst.r])
                else:
                    kb.op("dve", lambda: V.tensor_tensor(out=dimm, in0=a1, in1=a2, op=ALU.add), r=o1r + o2r, w=[dst.r])

            outer(Kk, 0, Bk[:, 0, 0, :], Bk[:, 0, 1, :], 1, False)
            outer(Kk, 1, Bk[:, 1, 0, :], Bk[:, 1, 1, :], 0, False)
            outer(Qt, 0, bc[:, 0, 2, :], bc[:, 0, 3, :], 0, True)
            outer(Qt, 1, bc[:, 1, 2, :], bc[:, 1, 3, :], 1, True)
            if True:
                kb.dma(Q_d, Qt[:].rearrange("p g d t n -> p (g d t n)"), r=[Qt.r], w=[Q_res])

            s1o.close()
            kb.barrier()
            n = 0
            ps = None
            for g2 in range(16):
                for d in range(2):
                    for part in range(2):
                        for kc2 in range(2):
                            if n % 4 == 0:
                                ps = next_ps()
                            kb.op("pe", lambda: PE.matmul(ps[:, (n % 4) * 128:(n % 4 + 1) * 128], lhsT=Kk[:, g2, d, part, kc2 * 128:(kc2 + 1) * 128],
                                                          rhs=ident[:], start=True, stop=True), r=[Kk.r, ident.r], w=[ps.r], signal=(n % 4 == 3))
                            if n % 4 == 3:
                                kb.op("act", lambda: A.copy(out=KkT[:, (n - 3) * 128:(n + 1) * 128], in_=ps[:]), r=[ps.r], w=[KkT.r])
                                if n % 16 == 15:
                                    pump(1)
                            n += 1

            msk = Tile(kb, s1, "s5msk", [128, 2, 2, 256], F32)
            kb.dma(msk[:], msk_d, w=[msk.r])
            Tts = [Tile(kb, s1, f"s5T{i}", [128, 256], BF16) for i in range(4)]
            Dm = [Tile(kb, s1, f"s5Dm{i}", [128, 128], BF16) for i in range(4)]
            pfs = [Tile(kb, s1, f"s5pf{i}", [128, 512], F32) for i in range(4)]
            tq = [Tile(kb, s1, f"s5tq{i}", [128, 512], F32) for i in range(4)]
            tgb = {}

            def tg1(it):
                g, kc2 = it // 2, it % 2
                gp, g2 = g % 2, g // 2
                dm = Dm[g % 4]
                if kc2 == 0:
                    kb.op("pool", lambda: G.tensor_tensor(out=dm[:].rearrange("p (i c) -> p i c", i=8), in0=ident[:].rearrange("p (i c) -> p i c", i=8),
                                                           in1=dbc[:, g * 16:(g + 1) * 16].unsqueeze(1).to_broadcast([128, 8, 16]), op=ALU.mult),
                          r=[ident.r, dbc.r], w=[dm.r])
                ps = next_ps()
                tgb[it] = ps
                mms = []
                for d in range(2):
                    for hh in range(2):
                        with_d = (d == 1 and hh == kc2)
                        for part in range(2):
                            mms.append((ps[:, d * 256 + hh * 128:d * 256 + (hh + 1) * 128],
                                        Kk[gp * 64:(gp + 1) * 64, g2, d, part, kc2 * 128:(kc2 + 1) * 128],
                                        Qt[gp * 64:(gp + 1) * 64, g2, d, part, hh * 128:(hh + 1) * 128],
                                        part == 0, part == 1 and not with_d))
                        if with_d:
                            mms.append((ps[:, 256 + kc2 * 128:256 + (kc2 + 1) * 128], ident[:], dm[:], False, True))
                for mi, (o_, l_, r_, st_, sp_) in enumerate(mms):
                    kb.op("pe", lambda: PE.matmul(o_, lhsT=l_, rhs=r_, start=st_, stop=sp_), r=[Kk.r, Qt.r, ident.r, dm.r], w=[ps.r],
                          signal=(mi == len(mms) - 1))

            def tg2(it):
                pf = pfs[it % 4]
                ps = tgb.pop(it)
                kb.op("act", lambda: A.copy(out=pf[:], in_=ps[:]), r=[ps.r], w=[pf.r])

            def tg3(it):
                g, kc2 = it // 2, it % 2
                pf = pfs[it % 4]; tt = tq[it % 4]
                kb.op("dve", lambda: V.tensor_tensor(out=tt[:].rearrange("p (a n) -> p a n", a=2), in0=pf[:].rearrange("p (a n) -> p a n", a=2),
                                                      in1=msk[:, :, kc2, :], op=ALU.mult), r=[pf.r, msk.r], w=[tt.r])
                Tt = Tts[it % 4]
                kb.op("dve", lambda: V.tensor_tensor(out=Tt[:], in0=tt[:, 0:256], in1=tt[:, 256:512], op=ALU.add), r=[tt.r], w=[Tt.r])
                kb.dma(T_d[:, it * 256:(it + 1) * 256], Tt[:], r=[Tt.r], w=[T_res], semres=T_res)
                if it % 6 == 5:
                    pump(1)

            swpipe(64, [tg1, tg2, tg3], lag=1)
        kb.barrier()
        if "stopS1" in debug:
            return [Q_res, T_res]
        def do_half(hidx, own):
            off = 0 if own else NT
            with ExitStack() as s2:
                Ug = Tile(kb, s2, f"s5Ug{hidx}", [128, 32 * 2 * 128], BF16)
                with ExitStack() as s2a:
                    Up = Tile(kb, s2a, f"s5Up{hidx}", [128, 32 * 256], BF16)
                    hT = Tile(kb, s2a, f"s5hT{hidx}", [128, 8, NT], BF16)
                    kb.dma(hT[:], hT_d[:, :, off:off + NT], r=[hT_res], w=[hT.r])
                    Up4 = Up[:, :].rearrange("p (g i c) -> p g i c", g=32, i=16)
                    for i in range(16):
                        ps = next_ps()
                        for kc in range(8):
                            kb.op("pe", lambda: PE.matmul(ps[:, :], lhsT=hT[:, kc, i:NT:16], rhs=w_s5[:, kc, :], start=(kc == 0), stop=(kc == 7)),
                                  r=[hT.r, w_s5.r], w=[ps.r], signal=(kc == 7))
                        src = ps[:, :].rearrange("p (g c) -> p g c", g=32)
                        if i % 2 == 0:
                            kb.op("act", lambda: A.copy(out=Up4[:, :, i, :], in_=src), r=[ps.r], w=[Up.r])
                        else:
                            kb.op("dve", lambda: V.tensor_copy(out=Up4[:, :, i, :], in_=src), r=[ps.r], w=[Up.r])
                    n = 0
                    for g in range(32):
                        for kc2 in range(2):
                            if n % 4 == 0:
                                ps = next_ps()
                            kb.op("pe", lambda: PE.matmul(ps[:, (n % 4) * 128:(n % 4 + 1) * 128], lhsT=Up[:, g * 256 + kc2 * 128:g * 256 + (kc2 + 1) * 128],
                                                          rhs=ident[:], start=True, stop=True), r=[Up.r, ident.r], w=[ps.r], signal=(n % 4 == 3))
                            if n % 4 == 3:
                                if (n // 4) % 2 == 0:
                                    kb.op("act", lambda: A.copy(out=Ug[:, (n - 3) * 128:(n + 1) * 128], in_=ps[:]), r=[ps.r], w=[Ug.r])
                                else:
                                    kb.op("dve", lambda: V.tensor_copy(out=Ug[:, (n - 3) * 128:(n + 1) * 128], in_=ps[:]), r=[ps.r], w=[Ug.r])
                            n += 1
                kb.barrier()
                if own:
                    kb.dma(Ug_d, Ug[:], r=[Ug.r], w=[Ug_res])
                    dump("dbg_Ug", Ug[:], [128, 8192], Ug.r)
                    dump("dbg_KkT", KkT[:], [128, 16384], KkT.r)
                    dump("dbg_ws5", w_s5[:], [128, 8, 512], w_s5.r)
                Ssb = Tile(kb, s2, f"s5Ssb{hidx}", [128, 2, 16, 128], F32)
                d1 = Tile(kb, s2, f"s5d1{hidx}", [128, 2, 16, 129], F32)
                d0 = Tile(kb, s2, f"s5d0{hidx}", [128, 16, 129], F32)
                Wt = Tile(kb, s2, f"s5W{hidx}", [128, 2, 16, 129], F32)
                Xp = Tile(kb, s2, f"s5Xp{hidx}", [128, 2, 2, 16, 128], BF16) if own else None
                for d in range(2):
                    tre, tim = PRE[d]
                    for part in range(2):
                        for b4 in range(4):
                            ps = next_ps()
                            for q in range(4):
                                g2 = b4 * 4 + q
                                for gp in range(2):
                                    g = 2 * g2 + gp
                                    for kc2 in range(2):
                                        base = (((g2 * 2 + d) * 2 + part) * 2 + kc2) * 128
                                        kb.op("pe", lambda: PE.matmul(ps[gp * 64:(gp + 1) * 64, q * 128:(q + 1) * 128],
                                                                      lhsT=KkT[:, base + gp * 64:base + (gp + 1) * 64],
                                                                      rhs=Ug[:, (g * 2 + kc2) * 128:(g * 2 + kc2 + 1) * 128],
                                                                      start=(kc2 == 0), stop=(kc2 == 1)),
                                              r=[KkT.r, Ug.r], w=[ps.r], signal=(q == 3 and gp == 1 and kc2 == 1))
                            kb.op("act", lambda: A.copy(out=Ssb[:, part, b4 * 4:(b4 + 1) * 4, :], in_=ps[:, :].rearrange("p (q j) -> p q j", q=4)),
                                  r=[ps.r], w=[Ssb.r])
                    Sre = Ssb[:, 0, :, :] if d == 0 else Ssb[:, 0, :, ::-1]
                    Sim = Ssb[:, 1, :, :] if d == 0 else Ssb[:, 1, :, ::-1]
                    pr_, pi_ = tre[:, :, 1:129], tim[:, :, 1:129]
                    ta, tb = Wt[:, 0, :, 0:128], Wt[:, 1, :, 0:128]
                    o_re, o_im = d1[:, 0, :, 1:129], d1[:, 1, :, 1:129]
                    rr = [Ssb.r, tre.r]
                    kb.op("dve", lambda: V.tensor_tensor(out=o_re, in0=pr_, in1=Sre, op=ALU.mult), r=rr, w=[d1.r])
                    kb.op("pool", lambda: G.tensor_tensor(out=ta, in0=pi_, in1=Sim, op=ALU.mult), r=rr, w=[Wt.r])
                    kb.op("dve", lambda: V.tensor_tensor(out=o_im, in0=pr_, in1=Sim, op=ALU.mult), r=rr, w=[d1.r])
                    kb.op("pool", lambda: G.tensor_tensor(out=tb, in0=pi_, in1=Sre, op=ALU.mult), r=rr, w=[Wt.r])
                    kb.op("dve", lambda: V.tensor_tensor(out=o_re, in0=o_re, in1=ta, op=ALU.subtract), r=[Wt.r, d1.r], w=[d1.r])
                    kb.op("dve", lambda: V.tensor_tensor(out=o_im, in0=o_im, in1=tb, op=ALU.add), r=[Wt.r, d1.r], w=[d1.r])
                    for part in range(2):
                        if own:
                            kb.op("dve", lambda: V.tensor_scalar(out=d1[:, part, :, 0], in0=Eend[:, d, part, :], scalar1=flags[:, d:d + 1], scalar2=None, op0=ALU.mult),
                                  r=[Eend.r, flags.r], w=[d1.r])
                        else:
                            kb.op("dve", lambda: V.memset(d1[:, part, :, 0], 0.0), w=[d1.r])
                    kb.op("dve", lambda: V.tensor_copy(out=d0[:, :, 1:129], in_=par[:, d, 3, :].unsqueeze(2).to_broadcast([128, 16, 128])), r=[par.r], w=[d0.r])
                    kb.op("dve", lambda: V.memset(d0[:, :, 0:1], 0.0), w=[d0.r])
                    for part in range(2):
                        kb.op("dve", lambda: V.tensor_tensor_scan(out=Wt[:, part, :, :].rearrange("p g t -> p (g t)"), data0=d0[:].rearrange("p g t -> p (g t)"),
                                                                   data1=d1[:, part, :, :].rearrange("p g t -> p (g t)"), initial=0.0, op0=ALU.mult, op1=ALU.add),
                              r=[d0.r, d1.r], w=[Wt.r])
                    if own:
                        tre, tim = POST[d]
                        Wr, Wi = Wt[:, 0, :, 0:128], Wt[:, 1, :, 0:128]
                        pr_, pi_ = tre[:, :, 0:128], tim[:, :, 0:128]
                        oR = Xp[:, d, 0, :, :] if d == 0 else Xp[:, d, 0, :, ::-1]
                        oI = Xp[:, d, 1, :, :] if d == 0 else Xp[:, d, 1, :, ::-1]
                        t1a, t1b = d1[:, 0, :, 0:128], d1[:, 1, :, 0:128]
                        t2a, t2b = Ssb[:, 0, :, :], Ssb[:, 1, :, :]
                        rr = [Wt.r, tre.r]
                        kb.op("dve", lambda: V.tensor_tensor(out=t1a, in0=pr_, in1=Wr, op=ALU.mult), r=rr, w=[d1.r])
                        kb.op("pool", lambda: G.tensor_tensor(out=t2a, in0=pi_, in1=Wi, op=ALU.mult), r=rr, w=[Ssb.r])
                        kb.op("dve", lambda: V.tensor_tensor(out=t1b, in0=pr_, in1=Wi, op=ALU.mult), r=rr, w=[d1.r])
                        kb.op("pool", lambda: G.tensor_tensor(out=t2b, in0=pi_, in1=Wr, op=ALU.mult), r=rr, w=[Ssb.r])
                        kb.op("dve", lambda: V.tensor_tensor(out=oR, in0=t1a, in1=t2a, op=ALU.subtract), r=[d1.r, Ssb.r], w=[Xp.r])
                        kb.op("dve", lambda: V.tensor_tensor(out=oI, in0=t1b, in1=t2b, op=ALU.add), r=[d1.r, Ssb.r], w=[Xp.r])
                    else:
                        tre, tim = POSTE[d]
                        Wr, Wi = Wt[:, 0, :, 128], Wt[:, 1, :, 128]
                        pr_, pi_ = tre[:, :, 0], tim[:, :, 0]
                        a1, a2 = d1[:, 0, :, 1], d1[:, 1, :, 1]
                        rr = [Wt.r, tre.r, d1.r]
                        kb.op("dve", lambda: V.tensor_tensor(out=a1, in0=pr_, in1=Wr, op=ALU.mult), r=rr, w=[d1.r])
                        kb.op("dve", lambda: V.tensor_tensor(out=a2, in0=pi_, in1=Wi, op=ALU.mult), r=rr, w=[d1.r])
                        kb.op("dve", lambda: V.tensor_tensor(out=Eend[:, d, 0, :], in0=a1, in1=a2, op=ALU.subtract), r=rr, w=[Eend.r])
                        kb.op("dve", lambda: V.tensor_tensor(out=a1, in0=pr_, in1=Wi, op=ALU.mult), r=rr, w=[d1.r])
                        kb.op("dve", lambda: V.tensor_tensor(out=a2, in0=pi_, in1=Wr, op=ALU.mult), r=rr, w=[d1.r])
                        kb.op("dve", lambda: V.tensor_tensor(out=Eend[:, d, 1, :], in0=a1, in1=a2, op=ALU.add), r=rr, w=[Eend.r])
                if own:
                    kb.dma(Xp_d, Xp[:].rearrange("p d t g j -> p (d t g j)"), r=[Xp.r], w=[Xp_res])

        with ExitStack() as s2t:
            w_s5 = Tile(kb, s2t, "w_s5", [128, 8, 512], BF16)
            kb.dma(w_s5[:], w_in.rearrange("(kc p) n -> p kc n", p=128)[:, :, 1536:2048], w=[w_s5.r], q="pool")
            PRE, POST, POSTE = [], [], []
            for d in range(2):
                PRE.append((Tile(kb, s2t, f"s5pre_r{d}", [128, 16, 129], F32), Tile(kb, s2t, f"s5pre_i{d}", [128, 16, 129], F32)))
                POST.append((Tile(kb, s2t, f"s5post_r{d}", [128, 16, 128], F32), Tile(kb, s2t, f"s5post_i{d}", [128, 16, 128], F32)))
                POSTE.append((Tile(kb, s2t, f"s5poste_r{d}", [128, 16, 1], F32), Tile(kb, s2t, f"s5poste_i{d}", [128, 16, 1], F32)))
            with ExitStack() as s2tt:
                st = (Tile(kb, s2tt, "tc_r", [128, 16 * 129], F32), Tile(kb, s2tt, "tc_i", [128, 16 * 129], I32),
                      Tile(kb, s2tt, "tc_f", [128, 16 * 129], F32))
                for d in range(2):
                    tab(st, PRE[d][0][:, :, :], PRE[d][1][:, :, :], d, kvv("preF_a" if d == 0 else "preB_a", 129), kvv("c15" if d == 0 else "c0", 129), 129, PRE[d][0].r)
                    tab(st, POST[d][0][:, :, :], POST[d][1][:, :, :], d, kvv("postF_a" if d == 0 else "postB_a", 128), kvv("c1" if d == 0 else "c16", 128), 128, POST[d][0].r)
                    tab(st, POSTE[d][0][:, :, :], POSTE[d][1][:, :, :], d, kvv("c2048", 1), kvv("c0", 1), 1, POSTE[d][0].r)
            finish_A()
            do_half(0, False)
            kb.barrier()
            do_half(1, True)
            kb.barrier()
        kb.barrier()
        if "stopS2" in debug:
            return [Q_res, T_res, Xp_res, Ug_res] + dumps

        with ExitStack() as s4:
            Tt = Tile(kb, s4, "s4T", [128, 32, 2, 256], BF16)
            kb.dma(Tt[:].rearrange("p g k n -> p (g k n)"), T_d, r=[T_res], w=[Tt.r])
            Qt = Tile(kb, s4, "s4Q", [128, 16, 2, 2, 256], BF16)
            kb.dma(Qt[:].rearrange("p g d t n -> p (g d t n)"), Q_d, r=[Q_res], w=[Qt.r])
            Ug = Tile(kb, s4, "s4Ug", [128, 32 * 2 * 128], BF16)
            kb.dma(Ug[:], Ug_d, r=[Ug_res], w=[Ug.r])
            Xp = Tile(kb, s4, "s4Xp", [128, 2, 2, 16, 128], BF16)
            kb.dma(Xp[:].rearrange("p d t g j -> p (d t g j)"), Xp_d, r=[Xp_res], w=[Xp.r])
            zp = Tile(kb, s4, "s4zp", [128, 16, 512], BF16)
            w_glu = Tile(kb, s4, "w_glu_sb", [128, 4, 512], BF16)
            kb.dma(w_glu[:], w_glu_d.rearrange("(kc p) n -> p kc n", p=128), w=[w_glu.r], q="pool")
            for g in range(32):
                gp, g2 = g % 2, g // 2
                ps = next_ps()
                reg = ps[:, 0:256]
                for kc2 in range(2):
                    kb.op("pe", lambda: PE.matmul(reg, lhsT=Ug[:, (g * 2 + kc2) * 128:(g * 2 + kc2 + 1) * 128], rhs=Tt[:, g, kc2, :], start=(kc2 == 0), stop=False),
                          r=[Ug.r, Tt.r], w=[ps.r], signal=False)
                for d in range(2):
                    for part in range(2):
                        last = (d == 1 and part == 1)
                        kb.op("pe", lambda: PE.matmul(reg, lhsT=Xp[gp * 64:(gp + 1) * 64, d, part, g2, :], rhs=Qt[gp * 64:(gp + 1) * 64, g2, d, part, :],
                                                      start=False, stop=last), r=[Xp.r, Qt.r], w=[ps.r], signal=last)
                kb.op("act", lambda: A.activation(out=zp[:, :, g * 16:(g + 1) * 16], in_=reg.rearrange("p (i c) -> p i c", i=16), func=AF.Gelu_apprx_tanh),
                      r=[ps.r], w=[zp.r])
            zT = Tile(kb, s4, "s4zT", [128, 4, NT], BF16)
            for ch in range(4):
                zv = zT[:, ch, :].rearrange("p (j i) -> p i j", i=16)
                for i in range(16):
                    if i % 4 == 0:
                        ps = next_ps()
                    kb.op("pe", lambda: PE.matmul(ps[:, (i % 4) * 128:(i % 4 + 1) * 128], lhsT=zp[:, i, ch * 128:(ch + 1) * 128], rhs=ident[:], start=True, stop=True),
                          r=[zp.r, ident.r], w=[ps.r], signal=(i % 4 == 3))
                    if i % 4 == 3:
                        kb.op("dve", lambda: V.tensor_copy(out=zv[:, i - 3:i + 1, :], in_=ps[:, :].rearrange("p (i j) -> p i j", i=4)), r=[ps.r], w=[zT.r])
            sg = [Tile(kb, s4, f"s4sg{i}", [128, 512], BF16) for i in range(2)]
            so = [Tile(kb, s4, f"s4so{i}", [128, 512], BF16) for i in range(2)]
            it = 0
            for ch in range(4):
                for tt in range(4):
                    ps = next_ps()
                    s_ = sg[it % 2]; o_ = so[it % 2]; it += 1
                    for kc in range(4):
                        kb.op("pe", lambda: PE.matmul(ps[:, :], lhsT=w_glu[:, kc, ch * 128:(ch + 1) * 128], rhs=zT[:, kc, tt * 512:(tt + 1) * 512], start=(kc == 0), stop=(kc == 3)),
                              r=[w_glu.r, zT.r], w=[ps.r], signal=(kc == 3))
                    kb.op("act", lambda: A.activation(out=s_[:], in_=ps[:, :], func=AF.Sigmoid), r=[ps.r], w=[s_.r])
                    kb.op("dve", lambda: V.tensor_tensor(out=o_[:], in0=zT[:, ch, tt * 512:(tt + 1) * 512], in1=s_[:], op=ALU.mult),
                          r=[zT.r, s_.r], w=[o_.r])
                    kb.dma(s5T_d[:, ch, tt * 512:(tt + 1) * 512], o_[:], r=[o_.r], w=[s5T_res], semres=s5T_res)
        kb.barrier()
    kb.barrier()
    return [s5T_res]

def norm_rows(ctx, pool, x_, gtile, xs_, s_, junk):
    nc, kb, epsb = ctx["nc"], ctx["kb"], ctx["epsb"]
    A, V = nc.scalar, nc.vector
    kb.op("act", lambda: A.activation(out=junk[:], in_=x_[:], func=AF.Square, accum_out=s_[:, 0:1]), r=[x_.r], w=[junk.r, s_.r])
    kb.op("act", lambda: A.activation(out=s_[:, 1:2], in_=s_[:, 0:1], func=AF.Sqrt, scale=1.0 / D, bias=epsb[:, 0:1]), r=[s_.r, epsb.r], w=[s_.r])
    kb.op("dve", lambda: V.reciprocal(out=s_[:, 2:3], in_=s_[:, 1:2]), r=[s_.r], w=[s_.r])
    kb.op("dve", lambda: V.scalar_tensor_tensor(out=xs_[:], in0=x_[:], scalar=s_[:, 2:3], in1=gtile[:], op0=ALU.mult, op1=ALU.mult),
          r=[x_.r, s_.r, gtile.r], w=[xs_.r])


def phase_attn(ctx):
    nc, kb, es = ctx["nc"], ctx["kb"], ctx["es"]
    scratch, debug = ctx["scratch"], ctx["debug"]
    ident, next_ps = ctx["ident"], ctx["next_ps"]
    hT_d, hT_res, w_in = ctx["hT_d"], ctx["hT_res"], ctx["w_in"]
    V, G, A, PE = nc.vector, nc.gpsimd, nc.scalar, nc.tensor
    mem_d = ctx["mem"]
    gvec = ctx["gvec"]
    w_mkv_d = dram_in(nc, "w_mem_kv", [D, 1024])
    ctx["w_gate_d"] = dram_in(nc, "w_gate", [D, 3072])
    bext_d = dram_in(nc, "bext", [128, 4 * 15 * 64])
    maskw_d = dram_in(nc, "maskw", [128, 64])
    rowvec_d = dram_in(nc, "rowvec", [1, 7 * 768])
    psum = ctx["psum"]
    memT_d = scratch("memT_scr", [128, 4, NT], BF16); memT_res = kb.res("memT_d")
    naT_d = scratch("naT_scr", [128, 4, NT], BF16); naT_res = kb.res("naT_d")
    ctx["memT_d"], ctx["memT_res"], ctx["naT_d"], ctx["naT_res"] = memT_d, memT_res, naT_d, naT_res
    w_in_v = w_in.rearrange("(kc p) n -> p kc n", p=128)

    with ExitStack() as pm:
        hT = Tile(kb, pm, "m_hT", [128, 8, NT], BF16)
        kb.dma(hT[:], hT_d[:, :, 0:NT], r=[hT_res], w=[hT.r])
        w_q = Tile(kb, pm, "m_wq", [128, 8, 512], BF16)
        kb.dma(w_q[:], w_in_v[:, :, 2048:2560], w=[w_q.r], q="pool")
        w_kv = Tile(kb, pm, "m_wkv", [128, 8, 1024], BF16)
        kb.dma(w_kv[:], w_mkv_d.rearrange("(kc p) n -> p kc n", p=128), w=[w_kv.r], q="pool")
        gm = Tile(kb, pm, "m_g", [128, D], F32)
        kb.dma(gm[:], gvec[1:2, :].to_broadcast([128, D]), w=[gm.r])
        memT = Tile(kb, pm, "m_memT", [128, 8, 256], BF16)
        xt = Tile(kb, pm, "m_xt", [128, D], F32); xs = Tile(kb, pm, "m_xs", [128, D], BF16)
        junk = Tile(kb, pm, "m_junk", [128, D], BF16); ss = Tile(kb, pm, "m_ss", [128, 4], F32)
        for i in range(2):
            kb.dma(xt[:], mem_d[i * 128:(i + 1) * 128, :], w=[xt.r])
            norm_rows(ctx, pm, xt, gm, xs, ss, junk)
            for half in range(2):
                ps = next_ps()
                for k in range(4):
                    kc = half * 4 + k
                    kb.op("pe", lambda: PE.matmul(ps[:, k * 128:(k + 1) * 128], lhsT=xs[:, kc * 128:(kc + 1) * 128], rhs=ident[:], start=True, stop=True),
                          r=[xs.r, ident.r], w=[ps.r], signal=(k == 3))
                kb.op("act", lambda: A.copy(out=memT[:, half * 4:(half + 1) * 4, i * 128:(i + 1) * 128], in_=ps[:].rearrange("p (k t) -> p k t", k=4)),
                      r=[ps.r], w=[memT.r])
        kTm = Tile(kb, pm, "m_kT", [128, 4, 256], BF16)
        vm = Tile(kb, pm, "m_v", [128, 2, 512], BF16)
        for h in range(4):
            ps = next_ps()
            for kc in range(8):
                kb.op("pe", lambda: PE.matmul(ps[:, 0:256], lhsT=w_kv[:, kc, h * 128:(h + 1) * 128], rhs=memT[:, kc, :], start=(kc == 0), stop=(kc == 7)),
                      r=[w_kv.r, memT.r], w=[ps.r], signal=(kc == 7))
            kb.op("act", lambda: A.copy(out=kTm[:, h, :], in_=ps[:, 0:256]), r=[ps.r], w=[kTm.r])
        for mc in range(2):
            ps = next_ps()
            for kc in range(8):
                kb.op("pe", lambda: PE.matmul(ps[:, :], lhsT=memT[:, kc, mc * 128:(mc + 1) * 128], rhs=w_kv[:, kc, 512:1024], start=(kc == 0), stop=(kc == 7)),
                      r=[w_kv.r, memT.r], w=[ps.r], signal=(kc == 7))
            kb.op("act", lambda: A.copy(out=vm[:, mc, :], in_=ps[:, :]), r=[ps.r], w=[vm.r])
        qT = Tile(kb, pm, "m_qT", [128, 4, NT], BF16)
        for h in range(4):
            for tt in range(4):
                ps = next_ps()
                for kc in range(8):
                    kb.op("pe", lambda: PE.matmul(ps[:, :], lhsT=w_q[:, kc, h * 128:(h + 1) * 128], rhs=hT[:, kc, tt * 512:(tt + 1) * 512], start=(kc == 0), stop=(kc == 7)),
                          r=[w_q.r, hT.r], w=[ps.r], signal=(kc == 7))
                if tt % 2 == 0:
                    kb.op("act", lambda: A.copy(out=qT[:, h, tt * 512:(tt + 1) * 512], in_=ps[:, :]), r=[ps.r], w=[qT.r])
                else:
                    kb.op("dve", lambda: V.tensor_copy(out=qT[:, h, tt * 512:(tt + 1) * 512], in_=ps[:, :]), r=[ps.r], w=[qT.r])
        sc = 1.0 / math.sqrt(128.0)
        MB = 8
        Pn = [Tile(kb, pm, f"m_P{i}", [128, 256], F32) for i in range(MB)]
        Pb = [Tile(kb, pm, f"m_Pb{i}", [128, 256], BF16) for i in range(MB)]
        PT = [Tile(kb, pm, f"m_PT{i}", [128, 256], BF16) for i in range(MB)]
        st = [Tile(kb, pm, f"m_st{i}", [128, 4], F32) for i in range(MB)]
        oT = Tile(kb, pm, "m_oT", [128, 4, NT], BF16)
        mpools = {"qk": [0, [0, 1]], "t": [0, [2, 3, 4]], "pv": [0, [5, 6, 7]]}

        def mps(name):
            p = mpools[name]
            b = ctx["psum"][p[1][p[0] % len(p[1])]]
            p[0] += 1
            return b
        munits = [(h, sub) for h in range(4) for sub in range(16)]
        mqk, mtb = {}, {}

        def m1(u):
            h, sub = munits[u]
            s_ = st[u % MB]
            ps = mps("qk"); mqk[u] = ps
            kb.op("pe", lambda: PE.matmul(ps[:, 0:256], lhsT=qT[:, h, sub * 128:(sub + 1) * 128], rhs=kTm[:, h, :], start=True, stop=True),
                  r=[qT.r, kTm.r], w=[ps.r])
            kb.op("dve", lambda: V.reduce_max(out=s_[:, 0:1], in_=ps[:, 0:256], axis=AX.X), r=[ps.r], w=[s_.r])
            kb.op("dve", lambda: V.tensor_scalar(out=s_[:, 1:2], in0=s_[:, 0:1], scalar1=-sc, scalar2=None, op0=ALU.mult), r=[s_.r], w=[s_.r])

        def m2(u):
            p_ = Pn[u % MB]; pb_ = Pb[u % MB]; s_ = st[u % MB]
            ps = mqk.pop(u)
            kb.op("act", lambda: A.activation(out=p_[:], in_=ps[:, 0:256], func=AF.Exp, scale=sc, bias=s_[:, 1:2], accum_out=s_[:, 2:3]),
                  r=[ps.r, s_.r], w=[p_.r, s_.r])
            kb.op("dve", lambda: V.reciprocal(out=s_[:, 3:4], in_=s_[:, 2:3]), r=[s_.r], w=[s_.r])
            kb.op("dve", lambda: V.tensor_scalar(out=pb_[:], in0=p_[:], scalar1=s_[:, 3:4], scalar2=None, op0=ALU.mult), r=[p_.r, s_.r], w=[pb_.r])

        def m3(u):
            pb_ = Pb[u % MB]
            ps2 = mps("t"); mtb[u] = ps2
            for mc in range(2):
                kb.op("pe", lambda: PE.matmul(ps2[:, mc * 128:(mc + 1) * 128], lhsT=pb_[:, mc * 128:(mc + 1) * 128], rhs=ident[:], start=True, stop=True),
                      r=[pb_.r, ident.r], w=[ps2.r], signal=(mc == 1))

        def m4(u):
            pt_ = PT[u % MB]
            ps2 = mtb.pop(u)
            kb.op("act", lambda: A.copy(out=pt_[:], in_=ps2[:, 0:256]), r=[ps2.r], w=[pt_.r])

        def m5(u):
            h, sub = munits[u]
            pt_ = PT[u % MB]
            ps3 = mps("pv")
            for mc in range(2):
                kb.op("pe", lambda: PE.matmul(ps3[:, 0:128], lhsT=vm[:, mc, h * 128:(h + 1) * 128], rhs=pt_[:, mc * 128:(mc + 1) * 128], start=(mc == 0), stop=(mc == 1)),
                      r=[vm.r, pt_.r], w=[ps3.r], signal=(mc == 1))
            kb.op("dve", lambda: V.tensor_copy(out=oT[:, h, sub * 128:(sub + 1) * 128], in_=ps3[:, 0:128]), r=[ps3.r], w=[oT.r])

        swpipe(len(munits), [m1, m2, m3, m4, m5], lag=1)
        kb.dma(memT_d, oT[:], r=[oT.r], w=[memT_res])
    kb.barrier()

    with ExitStack() as pn:
        NKV_T = 2560
        bext8 = Tile(kb, pn, "n_bext8", [128, 4, 15, 64], BF16)
        rowv = Tile(kb, pn, "n_rowv", [1, 7 * 768], BF16)
        kb.dma(rowv[:], rowvec_d, w=[rowv.r], q="pool")
        ones1 = Tile(kb, pn, "n_ones1", [1, 128], BF16)
        kb.op("dve", lambda: V.memset(ones1[:], 1.0), w=[ones1.r])
        qT = Tile(kb, pn, "n_qT", [128, 4, NT], BF16)
        kT = Tile(kb, pn, "n_kT", [128, 4, NKV_T], BF16)
        Vt = Tile(kb, pn, "n_V", [128, 40, 512], BF16)
        yna = Tile(kb, pn, "n_y", [128, 8, 512], BF16)
        ob = [Tile(kb, pn, f"n_ob{i}", [128, 512], BF16) for i in range(2)]
        pn2 = ExitStack()
        bext = Tile(kb, pn2, "n_bext", [128, 4, 15, 64], F32)
        kb.dma(bext[:].rearrange("p h r k -> p (h r k)"), bext_d, w=[bext.r])
        maskw = Tile(kb, pn2, "n_maskw", [128, 64], F32)
        kb.dma(maskw[:], maskw_d, w=[maskw.r])
        kb.op("dve", lambda: V.tensor_tensor(out=bext[:].rearrange("p h r k -> p (h r) k"), in0=bext[:].rearrange("p h r k -> p (h r) k"),
                                              in1=maskw[:].unsqueeze(1).to_broadcast([128, 60, 64]), op=ALU.add), r=[bext.r, maskw.r], w=[bext.r])
        kb.op("dve", lambda: V.tensor_scalar(out=bext8[:].rearrange("p h r k -> p (h r k)"), in0=bext[:].rearrange("p h r k -> p (h r k)"),
                                              scalar1=8.0, scalar2=None, op0=ALU.mult), r=[bext.r], w=[bext8.r])
        hk = Tile(kb, pn2, "n_hT", [128, 8, NKV_T], BF16)
        kb.dma(hk[:, :, 0:256], hT_d[:, :, 2 * NT:2 * NT + 256], r=[hT_res], w=[hk.r])
        kb.dma(hk[:, :, 256:256 + NT], hT_d[:, :, 0:NT], r=[hT_res], w=[hk.r])
        kb.dma(hk[:, :, 256 + NT:NKV_T], hT_d[:, :, 2 * NT + 256:2 * NT + 512], r=[hT_res], w=[hk.r])
        wqkv = Tile(kb, pn2, "n_w", [128, 8, 1536], BF16)
        kb.dma(wqkv[:], w_in_v[:, :, 0:1536], w=[wqkv.r], q="pool")
        n = 0
        for c4 in range(4):
            for tt in range(4):
                ps = next_ps()
                for kc in range(8):
                    kb.op("pe", lambda: PE.matmul(ps[:, :], lhsT=wqkv[:, kc, c4 * 128:(c4 + 1) * 128], rhs=hk[:, kc, 256 + tt * 512:256 + (tt + 1) * 512], start=(kc == 0), stop=(kc == 7)),
                          r=[wqkv.r, hk.r], w=[ps.r], signal=(kc == 7))
                eng = "act" if n % 2 == 0 else "dve"; n += 1
                kb.op(eng, lambda: (A.copy if eng == "act" else V.tensor_copy)(out=qT[:, c4, tt * 512:(tt + 1) * 512], in_=ps[:, :]), r=[ps.r], w=[qT.r])
            for tt in range(5):
                ps = next_ps()
                for kc in range(8):
                    kb.op("pe", lambda: PE.matmul(ps[:, :], lhsT=wqkv[:, kc, 512 + c4 * 128:512 + (c4 + 1) * 128], rhs=hk[:, kc, tt * 512:(tt + 1) * 512], start=(kc == 0), stop=(kc == 7)),
                          r=[wqkv.r, hk.r], w=[ps.r], signal=(kc == 7))
                eng = "act" if n % 2 == 0 else "dve"; n += 1
                kb.op(eng, lambda: (A.copy if eng == "act" else V.tensor_copy)(out=kT[:, c4, tt * 512:(tt + 1) * 512], in_=ps[:, :]), r=[ps.r], w=[kT.r])
        for row in range(0, 40, 2):
            ps = next_ps()
            for kc in range(8):
                kb.op("pe", lambda: PE.matmul(ps[:, :], lhsT=hk[:, kc, row * 64:row * 64 + 128], rhs=wqkv[:, kc, 1024:1536], start=(kc == 0), stop=(kc == 7)),
                      r=[wqkv.r, hk.r], w=[ps.r], signal=(kc == 7))
            eng = "act" if (row // 2) % 2 == 0 else "dve"
            kb.op(eng, lambda: (A.copy if eng == "act" else V.tensor_copy)(out=Vt[:, row, :], in_=ps[:, :]), r=[ps.r], w=[Vt.r])
        kb.dma(Vt[0:64, 1:39:2, :], Vt[64:128, 0:38:2, :], r=[Vt.r], w=[Vt.r], semres=Vt.r)
        kb.dma(Vt[64:128, 1:39:2, :], Vt[0:64, 2:40:2, :], r=[Vt.r], w=[Vt.r], semres=Vt.r)
        pn2.close()
        kb.barrier()
        Ps = [Tile(kb, pn, f"n_P{i}", [128, 768], BF16) for i in range(8)]
        PTs = [Tile(kb, pn, f"n_PT{i}", [128, 6, 128], BF16) for i in range(8)]
        sts = [Tile(kb, pn, f"n_st{i}", [128, 8], F32) for i in range(8)]
        ctx["right_es"] = ExitStack()
        ctx["wg"] = Tile(kb, ctx["right_es"], "e_wg", [128, 8, 3072], BF16, side="right")
        kb.dma(ctx["wg"][:], ctx["w_gate_d"].rearrange("(kc p) n -> p kc n", p=128), w=[ctx["wg"].r], q="pool")
        NB = 8
        pvbank = {}
        pools = {"qk": [0, [0, 1, 2, 3]], "t": [0, [4, 5]], "pv": [0, [6, 7]]}

        def pool_ps(name):
            p = pools[name]
            b = psum[p[1][p[0] % len(p[1])]]
            p[0] += 1
            return b
        units = []
        for rl in range(32):
            if rl <= 3:
                start, nrows, bs, edge = rl, 12, 3, rl
            elif rl >= 29:
                start, nrows, bs, edge = rl - 3, 12, 0, 4 + (rl - 29)
            else:
                start, nrows, bs, edge = rl, 8, 3, None
            for c4 in range(4):
                units.append((rl, c4, start, nrows, bs, edge))
        if "naInterior" in debug:
            units = [x for x in units if 4 <= x[0] < 12]
        if "naEdge" in debug:
            units = [x for x in units if x[0] < 2]
        for fl in debug:
            if fl.startswith("naR"):
                a, b = fl[3:].split("-")
                units = [x for x in units if int(a) <= x[0] < int(b)]
        ito = [0]

        def _u(u):
            rl, c4, start, nrows, bs, edge = units[u]
            parts = [(0, 8)] + ([(8, nrows - 8)] if nrows > 8 else [])
            return rl, c4, start, nrows, bs, edge, parts

        qkb = {}

        def na_s1(u):
            rl, c4, start, nrows, bs, edge, parts = _u(u)
            s_ = sts[u % NB]
            qkb[u] = []
            for pi, (r0, nr) in enumerate(parts):
                ps = pool_ps("qk")
                qkb[u].append(ps)
                for hp in range(2):
                    kb.op("pe", lambda: PE.matmul(ps[hp * 64:(hp + 1) * 64, 0:nr * 64], lhsT=qT[hp * 64:(hp + 1) * 64, c4, rl * 64:(rl + 1) * 64],
                                                  rhs=kT[hp * 64:(hp + 1) * 64, c4, (start + r0) * 64:(start + r0 + nr) * 64], start=True, stop=False,
                                                  skip_group_check=True),
                          r=[qT.r, kT.r], w=[ps.r], signal=False)
                kb.op("pe", lambda: PE.matmul(ps[:, 0:nr * 64], lhsT=ident[:], rhs=bext8[:, c4, bs + r0:bs + r0 + nr, :].rearrange("p r k -> p (r k)"),
                                              start=False, stop=(edge is None), skip_group_check=True),
                      r=[ident.r, bext8.r], w=[ps.r], signal=(edge is None))
                if edge is not None:
                    kb.op("pe", lambda: PE.matmul(ps[:, 0:nr * 64], lhsT=ones1[0:1, :], rhs=rowv[0:1, edge * 768 + r0 * 64:edge * 768 + (r0 + nr) * 64],
                                                  start=False, stop=True, skip_group_check=True),
                          r=[ones1.r, rowv.r], w=[ps.r])
                kb.op("dve", lambda: V.reduce_max(out=s_[:, 4 * pi:4 * pi + 1], in_=ps[:, 0:nr * 64], axis=AX.X), r=[ps.r], w=[s_.r])
            if len(parts) == 2:
                kb.op("dve", lambda: V.tensor_tensor(out=s_[:, 0:1], in0=s_[:, 0:1], in1=s_[:, 4:5], op=ALU.max), r=[s_.r], w=[s_.r])
            kb.op("dve", lambda: V.tensor_scalar(out=s_[:, 1:2], in0=s_[:, 0:1], scalar1=-0.125, scalar2=None, op0=ALU.mult), r=[s_.r], w=[s_.r])

        def na_s2(u):
            rl, c4, start, nrows, bs, edge, parts = _u(u)
            P_ = Ps[u % NB]; s_ = sts[u % NB]
            for pi, ((r0, nr), ps) in enumerate(zip(parts, qkb.pop(u))):
                kb.op("act", lambda: A.activation(out=P_[:, r0 * 64:(r0 + nr) * 64], in_=ps[:, 0:nr * 64], func=AF.Exp, scale=0.125, bias=s_[:, 1:2],
                                                  accum_out=s_[:, 2 + 4 * pi:3 + 4 * pi]), r=[ps.r, s_.r], w=[P_.r, s_.r])
            if len(parts) == 2:
                kb.op("dve", lambda: V.tensor_tensor(out=s_[:, 2:3], in0=s_[:, 2:3], in1=s_[:, 6:7], op=ALU.add), r=[s_.r], w=[s_.r])
            kb.op("dve", lambda: V.reciprocal(out=s_[:, 3:4], in_=s_[:, 2:3]), r=[s_.r], w=[s_.r])

        tbank = {}

        def na_s3(u):
            rl, c4, start, nrows, bs, edge, parts = _u(u)
            P_ = Ps[u % NB]
            ps = pool_ps("t")
            tbank[u] = ps
            psb = ps[:].bitcast(BF16)
            npair = nrows // 2
            for m in range(npair):
                kb.op("pe", lambda: PE.transpose(out=psb[:, m * 128:(m + 1) * 128], in_=P_[:, m * 128:(m + 1) * 128], identity=ident[:]),
                      r=[P_.r, ident.r], w=[ps.r], signal=(m == npair - 1))

        def na_s4(u):
            rl, c4, start, nrows, bs, edge, parts = _u(u)
            PT_ = PTs[u % NB]
            ps = tbank.pop(u)
            psb = ps[:].bitcast(BF16)
            npair = nrows // 2
            kb.op("act", lambda: A.copy(out=PT_[:, 0:npair, :], in_=psb[:, 0:npair * 128].rearrange("p (r k) -> p r k", k=128)), r=[ps.r], w=[PT_.r])

        def na_s5(u):
            rl, c4, start, nrows, bs, edge, parts = _u(u)
            PT_ = PTs[u % NB]; s_ = sts[u % NB]
            pvb = pool_ps("pv"); pvc = 0; pvr = pvb.r
            npair = nrows // 2
            for m in range(npair):
                kb.op("pe", lambda: PE.matmul(pvb[:, pvc:pvc + 128], lhsT=PT_[:, m, :], rhs=Vt[:, start + 2 * m, c4 * 128:(c4 + 1) * 128], start=(m == 0), stop=(m == npair - 1)),
                      r=[PT_.r, Vt.r], w=[pvr], signal=(m == npair - 1))
            kb.op("dve", lambda: V.tensor_scalar(out=yna[0:64, rl % 8, c4 * 128:c4 * 128 + 64], in0=pvb[0:64, pvc:pvc + 64], scalar1=s_[0:64, 3:4], scalar2=None, op0=ALU.mult),
                  r=[pvr, s_.r], w=[yna.r])
            kb.op("dve", lambda: V.tensor_scalar(out=yna[64:128, rl % 8, c4 * 128 + 64:c4 * 128 + 128], in0=pvb[64:128, pvc + 64:pvc + 128], scalar1=s_[64:128, 3:4], scalar2=None, op0=ALU.mult),
                  r=[pvr, s_.r], w=[yna.r])
            if rl % 8 == 7 and c4 == 3:
                r8 = rl // 8
                for c4b in range(4):
                    ps = pool_ps("pv")
                    o_ = ob[ito[0] % 2]; ito[0] += 1
                    for rr_ in range(8):
                        for hp in range(2):
                            lo, hi = hp * 64, (hp + 1) * 64
                            kb.op("pe", lambda: PE.matmul(ps[lo:hi, rr_ * 64:(rr_ + 1) * 64], lhsT=yna[lo:hi, rr_, c4b * 128 + lo:c4b * 128 + hi], rhs=ident[lo:hi, lo:hi], start=True, stop=True),
                                  r=[yna.r, ident.r], w=[ps.r], signal=(rr_ == 7 and hp == 1))
                    kb.op("act", lambda: A.copy(out=o_[:], in_=ps[:, :]), r=[ps.r], w=[o_.r])
                    kb.dma(naT_d[:, c4b, r8 * 512:(r8 + 1) * 512], o_[:], r=[o_.r], w=[naT_res], semres=naT_res)

        swpipe(len(units), [na_s1, na_s2, na_s3, na_s4, na_s5], lag=NA_LAG)
    kb.barrier()
    return [memT_res, naT_res]


def phase_merge_ffn(ctx):
    nc, kb, es = ctx["nc"], ctx["kb"], ctx["es"]
    scratch, debug = ctx["scratch"], ctx["debug"]
    ident, next_ps = ctx["ident"], ctx["next_ps"]
    hT_d, hT_res, x_all, out = ctx["hT_d"], ctx["hT_res"], ctx["x_all"], ctx["out"]
    gvec = ctx["gvec"]
    V, G, A, PE = nc.vector, nc.gpsimd, nc.scalar, nc.tensor
    w_gate_d = ctx["w_gate_d"]
    b_gate_d = dram_in(nc, "b_gate", [128, 24])
    w_br_d = dram_in(nc, "w_branch", [3, 512, D])
    w_o_d = dram_in(nc, "w_o", [D, D])
    w1_d = dram_in(nc, "w_ffn1", [D, DFF]); w3_d = dram_in(nc, "w_ffn3", [D, DFF]); w2_d = dram_in(nc, "w_ffn2", [DFF, D])
    x1_d = scratch("x1_scr", [NT, D], F32); x1_res = kb.res("x1_d")
    ys_d = [(ctx["naT_d"], ctx["naT_res"]), (ctx["s5T_d"], ctx["s5T_res"]), (ctx["memT_d"], ctx["memT_res"])]

    pw = ExitStack()
    w1 = Tile(kb, pw, "f_w1", [128, 8, DFF], BF16)
    with ExitStack() as pe_:
        wg = ctx["wg"]
        wb = Tile(kb, pe_, "e_wb", [128, 3, 4, D], BF16)
        for n_ in range(3):
            kb.dma(wb[:, n_, :, :], w_br_d[n_].rearrange("(kc p) n -> p kc n", p=128), w=[wb.r], q="pool")
        wo = Tile(kb, pe_, "e_wo", [128, 8, D], BF16)
        kb.dma(wo[:], w_o_d.rearrange("(kc p) n -> p kc n", p=128), w=[wo.r], q="pool")
        bg = Tile(kb, pe_, "e_bg", [128, 24], F32)
        kb.dma(bg[:], b_gate_d, w=[bg.r])
        for kc in range(8):
            kb.dma(w1[:, kc, :], w1_d[kc * 128:(kc + 1) * 128, :], w=[w1.r], q="pool")
        hTs = [Tile(kb, pe_, f"e_hT{i}", [128, 8, 512], BF16) for i in range(2)]
        yss = [[Tile(kb, pe_, f"e_ys{i}_{n_}", [128, 4, 512], BF16) for n_ in range(3)] for i in range(2)]
        mT = Tile(kb, pe_, "e_mT", [128, 8, 512], BF16)
        gs = [Tile(kb, pe_, f"e_gs{i}", [128, 512], F32) for i in range(2)]
        macc = Tile(kb, pe_, "e_macc", [128, 512], F32)
        mtmp = Tile(kb, pe_, "e_mtmp", [128, 512], F32)
        xts = [Tile(kb, pe_, f"e_xt{i}", [128, D], F32) for i in range(2)]
        it = 0; ix = 0
        def load_tile(tt):
            hT = hTs[tt % 2]; ys = yss[tt % 2]
            kb.dma(hT[:], hT_d[:, :, tt * 512:(tt + 1) * 512], r=[hT_res], w=[hT.r])
            for n_ in range(3):
                kb.dma(ys[n_][:], ys_d[n_][0][:, :, tt * 512:(tt + 1) * 512], r=[ys_d[n_][1]], w=[ys[n_].r])
        load_tile(0)
        for tt in range(4):
            hT = hTs[tt % 2]; ys = yss[tt % 2]
            if tt + 1 < 4:
                load_tile(tt + 1)
            for dc in range(8):
                for n_ in range(3):
                    g_ = gs[it % 2]; it += 1
                    ps = next_ps()
                    for kc in range(8):
                        kb.op("pe", lambda: PE.matmul(ps[:, :], lhsT=wg[:, kc, n_ * 1024 + dc * 128:n_ * 1024 + (dc + 1) * 128], rhs=hT[:, kc, :], start=(kc == 0), stop=(kc == 7)),
                              r=[wg.r, hT.r], w=[ps.r], signal=(kc == 7))
                    kb.op("act", lambda: A.activation(out=g_[:], in_=ps[:, :], func=AF.Sigmoid, bias=bg[:, n_ * 8 + dc:n_ * 8 + dc + 1]), r=[ps.r, bg.r], w=[g_.r])
                    ps2 = next_ps()
                    for kc in range(4):
                        kb.op("pe", lambda: PE.matmul(ps2[:, :], lhsT=wb[:, n_, kc, dc * 128:(dc + 1) * 128], rhs=ys[n_][:, kc, :], start=(kc == 0), stop=(kc == 3)),
                              r=[wb.r, ys[n_].r], w=[ps2.r], signal=(kc == 3))
                    if n_ == 0:
                        kb.op("dve", lambda: V.tensor_tensor(out=macc[:], in0=g_[:], in1=ps2[:, :], op=ALU.mult), r=[g_.r, ps2.r], w=[macc.r])
                    else:
                        kb.op("dve", lambda: V.tensor_tensor(out=mtmp[:], in0=g_[:], in1=ps2[:, :], op=ALU.mult), r=[g_.r, ps2.r], w=[mtmp.r])
                        if n_ == 1:
                            kb.op("dve", lambda: V.tensor_tensor(out=macc[:], in0=macc[:], in1=mtmp[:], op=ALU.add), r=[macc.r, mtmp.r], w=[macc.r])
                        else:
                            kb.op("dve", lambda: V.tensor_tensor(out=mT[:, dc, :], in0=macc[:], in1=mtmp[:], op=ALU.add), r=[macc.r, mtmp.r], w=[mT.r])
            for sub in range(4):
                x_ = xts[ix % 2]; ix += 1
                t0 = tt * 512 + sub * 128
                kb.dma(x_[:], x_all[t0:t0 + 128, :], w=[x_.r])
                for half in range(2):
                    ps = next_ps()
                    for dc in range(8):
                        kb.op("pe", lambda: PE.matmul(ps[:, :], lhsT=mT[:, dc, sub * 128:(sub + 1) * 128], rhs=wo[:, dc, half * 512:(half + 1) * 512], start=(dc == 0), stop=(dc == 7)),
                              r=[mT.r, wo.r], w=[ps.r], signal=(dc == 7))
                    kb.op("dve", lambda: V.tensor_tensor(out=x_[:, half * 512:(half + 1) * 512], in0=x_[:, half * 512:(half + 1) * 512], in1=ps[:, :], op=ALU.add),
                          r=[x_.r, ps.r], w=[x_.r])
                kb.dma(x1_d[t0:t0 + 128, :], x_[:], r=[x_.r], w=[x1_res], semres=x_.r)
    ctx["right_es"].close()
    kb.barrier()

    out_res = kb.res("out_d")
    with ExitStack() as pf:
        w3 = Tile(kb, pf, "f_w3", [128, 8, DFF], BF16)
        w2 = Tile(kb, pf, "f_w2", [128, NFC, D], BF16)
        for kc in range(8):
            kb.dma(w3[:, kc, :], w3_d[kc * 128:(kc + 1) * 128, :], w=[w3.r], q="pool")
        kb.dma(w2[:], w2_d.rearrange("(fc p) n -> p fc n", p=128), w=[w2.r], q="pool")
        gf = Tile(kb, pf, "f_gf", [128, D], F32)
        kb.dma(gf[:], gvec[2:3, :].to_broadcast([128, D]), w=[gf.r])
        gfin = Tile(kb, pf, "f_gfin", [128, D], F32)
        kb.dma(gfin[:], gvec[3:4, :].to_broadcast([128, D]), w=[gfin.r])
        x1 = Tile(kb, pf, "f_x1", [128, 4, D], F32)
        xl = Tile(kb, pf, "f_xl", [128, D], F32)
        xs = [Tile(kb, pf, f"f_xs{i}", [128, D], BF16) for i in range(4)]
        ss = [Tile(kb, pf, f"f_ss{i}", [128, 4], F32) for i in range(4)]
        h2Ts = [Tile(kb, pf, f"f_h2T{i}", [128, 8, 512], BF16) for i in range(2)]
        aT = Tile(kb, pf, "f_aT", [128, NFC, 512], BF16)
        sl = [Tile(kb, pf, f"f_sl{i}", [128, 512], BF16) for i in range(1)]

        class _Sub:
            def __init__(self, t, i):
                self.t, self.i, self.r = t, i, t.r

            def __getitem__(self, idx):
                return self.t[:, self.i, :][idx]

        def prepA(tt):
            for sub in range(4):
                t0 = tt * 512 + sub * 128
                kb.dma(xl[:], x1_d[t0:t0 + 128, :], r=[x1_res], w=[xl.r])
                norm_rows(ctx, pf, xl, gf, xs[sub], ss[sub], xs[sub])

        def prepB(tt):
            h2T = h2Ts[tt % 2]
            pb = {}

            def p2(sub):
                xs_ = xs[sub]
                pb[sub] = []
                for half in range(2):
                    ps = next_ps()
                    pb[sub].append(ps)
                    for k in range(4):
                        kc = half * 4 + k
                        kb.op("pe", lambda: PE.matmul(ps[:, k * 128:(k + 1) * 128], lhsT=xs_[:, kc * 128:(kc + 1) * 128], rhs=ident[:], start=True, stop=True),
                              r=[xs_.r, ident.r], w=[ps.r], signal=(k == 3))

            def p3(sub):
                for half, ps in enumerate(pb.pop(sub)):
                    kb.op("act", lambda: A.copy(out=h2T[:, half * 4:(half + 1) * 4, sub * 128:(sub + 1) * 128], in_=ps[:].rearrange("p (k t) -> p k t", k=4)),
                          r=[ps.r], w=[h2T.r])
            swpipe(4, [p2, p3], lag=1)

        itc = [0]

        def ffn1(tt):
            h2T = h2Ts[tt % 2]
            kb.dma(x1[:], x1_d[tt * 512:(tt + 1) * 512, :].rearrange("(s p) n -> p s n", p=128), r=[x1_res], w=[x1.r])
            for fc in range(NFC):
                s_ = sl[0]; itc[0] += 1
                ps1 = next_ps()
                for kc in range(8):
                    kb.op("pe", lambda: PE.matmul(ps1[:, :], lhsT=w1[:, kc, fc * 128:(fc + 1) * 128], rhs=h2T[:, kc, :], start=(kc == 0), stop=(kc == 7)),
                          r=[w1.r, h2T.r], w=[ps1.r], signal=(kc == 7))
                ps3 = next_ps()
                for kc in range(8):
                    kb.op("pe", lambda: PE.matmul(ps3[:, :], lhsT=w3[:, kc, fc * 128:(fc + 1) * 128], rhs=h2T[:, kc, :], start=(kc == 0), stop=(kc == 7)),
                          r=[w3.r, h2T.r], w=[ps3.r], signal=(kc == 7))
                kb.op("act", lambda: A.activation(out=s_[:], in_=ps1[:, :], func=AF.Silu), r=[ps1.r], w=[s_.r])
                kb.op("dve", lambda: V.tensor_tensor(out=aT[:, fc, :], in0=s_[:], in1=ps3[:, :], op=ALU.mult), r=[s_.r, ps3.r], w=[aT.r])

        def ffn2(tt, sub):
            s_ = ss[sub % 4]
            junk = xl
            for half in range(2):
                ps = next_ps()
                for fc in range(NFC):
                    kb.op("pe", lambda: PE.matmul(ps[:, :], lhsT=aT[:, fc, sub * 128:(sub + 1) * 128], rhs=w2[:, fc, half * 512:(half + 1) * 512], start=(fc == 0), stop=(fc == NFC - 1)),
                          r=[aT.r, w2.r], w=[ps.r], signal=(fc == NFC - 1))
                kb.op("dve", lambda: V.tensor_tensor(out=x1[:, sub, half * 512:(half + 1) * 512], in0=x1[:, sub, half * 512:(half + 1) * 512], in1=ps[:, :], op=ALU.add),
                      r=[x1.r, ps.r], w=[x1.r])
            xsub = _Sub(x1, sub)
            kb.op("act", lambda: A.activation(out=junk[:], in_=xsub[:], func=AF.Square, accum_out=s_[:, 0:1]), r=[x1.r], w=[junk.r, s_.r])
            kb.op("act", lambda: A.activation(out=s_[:, 1:2], in_=s_[:, 0:1], func=AF.Sqrt, scale=1.0 / D, bias=ctx["epsb"][:, 0:1]), r=[s_.r, ctx["epsb"].r], w=[s_.r])
            kb.op("dve", lambda: V.reciprocal(out=s_[:, 2:3], in_=s_[:, 1:2]), r=[s_.r], w=[s_.r])
            kb.op("dve", lambda: V.scalar_tensor_tensor(out=xsub[:], in0=xsub[:], scalar=s_[:, 2:3], in1=gfin[:], op0=ALU.mult, op1=ALU.mult),
                  r=[x1.r, s_.r, gfin.r], w=[x1.r])
            t0 = tt * 512 + sub * 128
            kb.dma(out[t0:t0 + 128, :], xsub[:], r=[x1.r], w=[out_res], semres=out_res)

        prepA(0)
        prepB(0)
        for tt in range(4):
            ffn1(tt)
            if tt + 1 < 4:
                prepA(tt + 1)
            ffn2(tt, 0)
            ffn2(tt, 1)
            ffn2(tt, 2)
            if tt + 1 < 4:
                prepB(tt + 1)
            ffn2(tt, 3)
    pw.close()
    return [out_res]


def s5_host(inp):
    f32 = np.float32

    def P2(a):
        return np.asarray(a, f32).reshape(16, 2, 64).transpose(1, 2, 0).reshape(128, 16)

    def P3(a):
        return np.asarray(a, f32).reshape(16, 2, 64, 16).transpose(1, 2, 0, 3).reshape(128, 256)
    small = np.zeros((128, 2, 3, 16), f32)
    bc = np.zeros((128, 2, 4, 256), f32)
    for d in range(2):
        small[:, d, 0] = P2(inp["a_re"][0, d])
        small[:, d, 1] = P2(inp["a_im"][0, d])
        small[:, d, 2] = P2(np.broadcast_to(np.asarray(inp["log_dt"][0, d])[:, None], (32, 64)))
        bc[:, d, 0] = P3(inp["b_re"][0, d])
        bc[:, d, 1] = P3(inp["b_im"][0, d])
        bc[:, d, 2] = P3(np.asarray(inp["c_re"][0, d]).transpose(0, 2, 1))
        bc[:, d, 3] = P3(np.asarray(inp["c_im"][0, d]).transpose(0, 2, 1))
    kvv, _ = kv_layout()
    ip = np.arange(128) // 16
    ii = np.arange(256) // 16
    mask = np.zeros((128, 2, 2, 256), f32)
    for kc2 in range(2):
        ia = kc2 * 8 + ip
        mask[:, 0, kc2, :] = (ii[None, :] >= ia[:, None])
        mask[:, 1, kc2, :] = (ii[None, :] <= ia[:, None])
    return {
        "s5_small": small, "s5_bc": bc,
        "s5_dbc": np.broadcast_to(np.asarray(inp["s5_d"][0], f32), (128, 512)).copy(),
        "kv": np.broadcast_to(kvv, (128, len(kvv))).copy(),
        "s5_mask": mask,
        "w_glu": np.ascontiguousarray(inp["w_glu"][0], f32),
    }


def make_in_maps(inp):
    x = np.asarray(inp["x"], np.float32)
    maps = []
    gvec = np.stack([inp["g_mix"][0], inp["g_mem"][0], inp["g_ffn"][0], inp["g_final"]]).astype(np.float32)
    s5h = s5_host(inp)
    rpb = np.asarray(inp["rpb"][0], np.float32)
    cq = np.arange(64)[:, None]; ck = np.arange(64)[None, :]
    relc = np.clip(ck - cq + 15, 0, 30)
    bg_ = rpb[:, :, relc].reshape(4, 2, 15, 64, 64)
    bext = np.ascontiguousarray(bg_.transpose(1, 3, 0, 2, 4).reshape(128, 4 * 15 * 64))
    cs = np.clip(np.arange(64) - 8, 0, 48)
    inwin = (ck >= cs[:, None]) & (ck < cs[:, None] + 16)
    maskw = np.tile(np.where(inwin, 0.0, NEG).astype(np.float32), (2, 1))
    rowmasks = []
    for half in range(2):
        rm = np.full((7, 12), NEG, np.float32)
        for e, rl in enumerate([0, 1, 2, 3, 29, 30, 31]):
            r = half * 32 + rl
            rs = min(max(r - 4, 0), 56)
            base = (rl - 4) if rl <= 3 else (rl - 7)
            nrows = 12 if rl <= 3 else 11
            for k in range(nrows):
                ra = half * 32 + base + k
                if rs <= ra < rs + 8:
                    rm[e, k] = 0.0
        rowmasks.append(np.ascontiguousarray((np.repeat(rm[:, :, None], 64, axis=2) * 1.0).reshape(1, 7 * 768)))
    for core in range(8):
        b, half = core // 2, core % 2
        own = x[b, half * NT:(half + 1) * NT]
        oth = x[b, (1 - half) * NT:(2 - half) * NT]
        halo = np.zeros((NHALO, D), np.float32)
        if half == 0:
            halo[256:512] = x[b, NT:NT + 256]
        else:
            halo[0:256] = x[b, NT - 256:NT]
        m = dict(s5h)
        m["flags"] = np.broadcast_to(np.array([1.0 if half == 1 else 0.0, 1.0 if half == 0 else 0.0], np.float32), (128, 2)).copy()
        m.update({
            "x_all": np.ascontiguousarray(np.concatenate([own, oth, halo], 0)),
            "mem": np.ascontiguousarray(inp["mem"][b]),
            "gvec": gvec,
            "w_in": np.ascontiguousarray(inp["w_in"][0]),
            "ident": np.eye(128, dtype=np.float32),
            "w_mem_kv": np.ascontiguousarray(inp["w_mem_kv"][0], np.float32),
            "bext": bext, "maskw": maskw, "rowvec": rowmasks[half],
            "w_gate": np.ascontiguousarray(inp["w_gate"][0], np.float32),
            "b_gate": np.ascontiguousarray(np.asarray(inp["b_gate"][0], np.float32).reshape(24, 128).T),
            "w_branch": np.ascontiguousarray(inp["w_branch"][0], np.float32),
            "w_o": np.ascontiguousarray(inp["w_o"][0], np.float32),
            "w_ffn1": np.ascontiguousarray(inp["w_ffn1"][0], np.float32),
            "w_ffn3": np.ascontiguousarray(inp["w_ffn3"][0], np.float32),
            "w_ffn2": np.ascontiguousarray(inp["w_ffn2"][0], np.float32),
        })
        maps.append(m)
    return maps


def kernel(**inp):
    nc = build_nc()
    maps = make_in_maps(inp)
    res = run_bass_kernel_spmd(nc, maps, core_ids=list(range(8)))
    outp = np.zeros((4, 4096, D), np.float32)
    for core in range(8):
        b, half = core // 2, core % 2
        outp[b, half * NT:(half + 1) * NT] = res.results[core]["out"]
    return outp
```

```python
import math
import numpy as np
import ml_dtypes
from contextlib import ExitStack
import concourse.bass as bass
import concourse.mybir as mybir
from concourse.bass_utils import run_bass_kernel_spmd

F32 = mybir.dt.float32
BF16 = mybir.dt.bfloat16
I32 = mybir.dt.int32
ALU = mybir.AluOpType
AF = mybir.ActivationFunctionType
AX = mybir.AxisListType

D = 1024
NT = 2048
NHALO = 512
NALL = NT + NT + NHALO
DFF = 2816
NFC = DFF // 128
EPS = 1e-6
L = 16
TWO_PI = 6.28318
NEG = -1.0e30

DEBUG = {}
NA_LAG = 1
NO_SELF_SYNC = ()


class Res:
    __slots__ = ("lw", "rd", "dsem", "dtot", "name")

    def __init__(self, name):
        self.lw = None
        self.rd = {}
        self.dsem = None
        self.dtot = 0
        self.name = name


class KB:
    ENG = ("pe", "act", "dve", "pool", "sp")

    def __init__(self, nc, es):
        self.nc = nc
        self.es = es
        self.eng = {"pe": nc.tensor, "act": nc.scalar, "dve": nc.vector, "pool": nc.gpsimd, "sp": nc.sync}
        self.sem = {}
        self.cnt = {}
        self.waited = {}
        self.pending = {}
        for e in self.ENG:
            self.sem[e] = es.enter_context(nc.semaphore("sem_" + e))
            self.cnt[e] = 0
            self.waited[e] = {}
            self.pending[e] = []
        self.semtot = {}
        self.nres = 0
        self.nd = 0
        self.dres = []
        self.ev = {e: [] for e in self.ENG}

    def res(self, name=None):
        self.nres += 1
        return Res(name or f"r{self.nres}")

    def _need(self, deps, mark):
        if mark is None:
            return
        key, val = mark
        if deps.get(key, 0) < val:
            deps[key] = val

    def _wait_all(self, engname, r, w):
        deps = {}
        for x in r:
            self._need(deps, x.lw)
        for x in w:
            self._need(deps, x.lw)
            for k, v in x.rd.items():
                self._need(deps, (k, v))
        eng = self.eng[engname]
        wt = self.waited[engname]
        for key, val in deps.items():
            kind, obj = key
            if kind == "e":
                if obj == engname and (engname == "pe" or engname in NO_SELF_SYNC):
                    continue
                semh = self.sem[obj]
            else:
                semh = obj.dsem
                val = obj.dtot
            if wt.get(key, 0) >= val:
                continue
            eng.wait_ge(semh, val)
            wt[key] = val
            self.ev[engname].append(("w", key, val, None))

    def op(self, engname, fn, r=(), w=(), signal=True):
        self._wait_all(engname, r, w)
        ins = fn()
        self.pending[engname].append((tuple(r), tuple(w)))
        if signal:
            self.cnt[engname] += 1
            ins.then_inc(self.sem[engname], 1)
            mark = (("e", engname), self.cnt[engname])
            import sys as _s
            self.ev[engname].append(("i", mark[0], 1, _s._getframe(1).f_lineno))
            for (rr, ww) in self.pending[engname]:
                for x in rr:
                    x.rd[mark[0]] = mark[1]
                for x in ww:
                    x.lw = mark
                    x.rd = {}
            self.pending[engname] = []
        return ins

    def dma(self, out, in_, r=(), w=(), q="sp", semres=None):
        self._wait_all(q, r, w)
        sr = semres or (w[0] if w else r[0])
        if sr.dsem is None:
            self.nd += 1
            sr.dsem = self.es.enter_context(self.nc.semaphore(f"dsem{self.nd}"))
            self.dres.append(sr)
        ins = self.eng[q].dma_start(out=out, in_=in_)
        sr.dtot += 16
        ins.then_inc(sr.dsem, 16)
        mark = (("d", sr), sr.dtot)
        self.ev[q].append(("i", mark[0], 16, None))
        for x in r:
            x.rd[mark[0]] = mark[1]
        for x in w:
            x.lw = mark
            x.rd = {}
        return ins

    def barrier(self):
        sp = self.eng["sp"]
        wt = self.waited["sp"]
        for e in ("pe", "act", "dve", "pool"):
            key = ("e", e)
            if self.cnt[e] > wt.get(key, 0):
                sp.wait_ge(self.sem[e], self.cnt[e])
                wt[key] = self.cnt[e]
                self.ev["sp"].append(("w", key, self.cnt[e], None))
        for r in self.dres:
            key = ("d", r)
            if r.dtot > wt.get(key, 0):
                sp.wait_ge(r.dsem, r.dtot)
                wt[key] = r.dtot
                self.ev["sp"].append(("w", key, r.dtot, None))
        self.cnt["sp"] += 1
        sp.nop().then_inc(self.sem["sp"], 1)
        self.ev["sp"].append(("i", ("e", "sp"), 1, None))
        for e in ("pe", "act", "dve", "pool"):
            self.eng[e].wait_ge(self.sem["sp"], self.cnt["sp"])
            self.waited[e][("e", "sp")] = self.cnt["sp"]
            self.ev[e].append(("w", ("e", "sp"), self.cnt["sp"], None))

    def check_deadlock(self):
        val = {}
        ptr = {e: 0 for e in self.ENG}
        while True:
            prog = False
            for e in self.ENG:
                evs = self.ev[e]
                while ptr[e] < len(evs):
                    kind, key, v, ln = evs[ptr[e]]
                    if kind == "w":
                        if val.get(key, 0) < v:
                            break
                    else:
                        val[key] = val.get(key, 0) + v
                    ptr[e] += 1
                    prog = True
            if not prog:
                break
        stuck = {e: (ptr[e], len(self.ev[e])) for e in self.ENG if ptr[e] < len(self.ev[e])}
        if stuck:
            msg = []
            for e, (p, n) in stuck.items():
                kind, key, v, ln = self.ev[e][p]
                nxt = [x for x in self.ev[e][p:p + 6] if x[0] == "i"][:1]
                msg.append(f"{e}: at {p}/{n} waits {key[0]}:{key[1] if key[0]=='e' else key[1].name} >= {v} (have {val.get(key, 0)}) next-inc-line {nxt[0][3] if nxt else None}")
            raise RuntimeError("DEADLOCK in program:\n" + "\n".join(msg))

    def finish(self, resources):
        self._wait_all("sp", list(resources), [])


class Tile:
    def __init__(self, kb, es, name, shape, dtype, psum=False, nsub=0, side=None):
        nc = kb.nc
        if psum:
            self.t = es.enter_context(nc.psum_tensor(name, list(shape), dtype))
        elif side is not None:
            self.t = es.enter_context(nc.sbuf_tensor(name, list(shape), dtype, side=side))
        else:
            self.t = es.enter_context(nc.sbuf_tensor(name, list(shape), dtype))
        self.r = kb.res(name)
        self.sub = [kb.res(f"{name}.{i}") for i in range(nsub)]

    def __getitem__(self, idx):
        return self.t[idx]


def swpipe(n, stages, lag=1):
    S = len(stages)
    for t in range(n + (S - 1) * lag):
        for k in range(S):
            i = t - k * lag
            if 0 <= i < n:
                stages[k](i)


def swpipe_gen(n, stages, lag=1):
    S = len(stages)
    for t in range(n + (S - 1) * lag):
        for k in range(S):
            i = t - k * lag
            if 0 <= i < n:
                stages[k](i)
        yield


def bcast_free(ap2d, n):
    return ap2d.unsqueeze(1).to_broadcast([ap2d.shape[0], n, ap2d.shape[1]])


def build_nc(debug=()):
    nc = bass.Bass("TRN2", target_bir_lowering=False)
    es = ExitStack()
    with es:
        kb = KB(nc, es)
        _build(nc, kb, es, debug)
        kb.check_deadlock()
    return nc


def dram_in(nc, name, shape, dtype=F32):
    return nc.dram_tensor(name, list(shape), dtype, kind="ExternalInput").ap()


def _build(nc, kb, es, debug):
    x_all = dram_in(nc, "x_all", [NALL, D])
    mem = dram_in(nc, "mem", [256, D])
    gvec = dram_in(nc, "gvec", [4, D])
    w_in = dram_in(nc, "w_in", [D, 2560])
    out = nc.dram_tensor("out", [NT, D], F32, kind="ExternalOutput").ap()
    ident_d = dram_in(nc, "ident", [128, 128])

    def scratch(name, shape, dtype):
        kind = "ExternalOutput" if name in debug else "Internal"
        return nc.dram_tensor(name, list(shape), dtype, kind=kind).ap()

    hT_d = scratch("hT_scr", [128, 8, NALL], BF16 if "hT_scr" not in debug else F32)
    hT_res = kb.res("hT_d")

    ident = Tile(kb, es, "ident_sb", [128, 128], BF16)
    kb.dma(ident[:], ident_d, w=[ident.r], q="pool")
    gbc = None
    epsb = Tile(kb, es, "epsb", [128, 1], F32)
    kb.op("dve", lambda: nc.vector.memset(epsb[:], EPS), w=[epsb.r])
    psum = [Tile(kb, es, f"ps{i}", [128, 512], F32, psum=True) for i in range(8)]
    psi = [0]

    def next_ps(nb=8):
        p = psum[psi[0] % nb]
        psi[0] += 1
        return p

    pa = ExitStack()
    if True:
        def TileR(kb_, es_, name, shape, dtype, **kw):
            return Tile(kb_, es_, name, shape, dtype, side="right", **kw)
        gbc = TileR(kb, pa, "gbc", [128, 1, D], F32)
        kb.dma(gbc[:], gvec[0:1, :].unsqueeze(0).to_broadcast([128, 1, D]), w=[gbc.r])
        xt = [TileR(kb, pa, f"xt{i}", [128, D], F32) for i in range(4)]
        xs = [TileR(kb, pa, f"xs{i}", [128, D], BF16) for i in range(3)]
        junk = TileR(kb, pa, "junk", [128, D], BF16)
        ss = [TileR(kb, pa, f"ss{i}", [128, 4], F32) for i in range(3)]
        hst = [TileR(kb, pa, f"hst{i}", [128, 8, 128], F32 if "hT_scr" in debug else BF16) for i in range(3)]
        ntile = NALL // 128
        for i in range(2):
            kb.dma(xt[i][:], x_all[i * 128:(i + 1) * 128, :], w=[xt[i].r])
        abank = {}

        def a1(i):
            if i + 2 < ntile:
                nx = xt[(i + 2) % 4]
                kb.dma(nx[:], x_all[(i + 2) * 128:(i + 3) * 128, :], w=[nx.r])
            x_ = xt[i % 4]; s_ = ss[i % 3]; xs_ = xs[i % 3]
            kb.op("act", lambda: nc.scalar.activation(out=junk[:], in_=x_[:], func=AF.Square, accum_out=s_[:, 0:1]),
                  r=[x_.r], w=[junk.r, s_.r])
            kb.op("act", lambda: nc.scalar.activation(out=s_[:, 1:2], in_=s_[:, 0:1], func=AF.Sqrt, scale=1.0 / D, bias=epsb[:, 0:1]),
                  r=[s_.r, epsb.r], w=[s_.r])
            kb.op("dve", lambda: nc.vector.reciprocal(out=s_[:, 2:3], in_=s_[:, 1:2]), r=[s_.r], w=[s_.r])
            kb.op("dve", lambda: nc.vector.scalar_tensor_tensor(out=xs_[:], in0=x_[:], scalar=s_[:, 2:3], in1=gbc[:, 0, :],
                                                                op0=ALU.mult, op1=ALU.mult),
                  r=[x_.r, s_.r, gbc.r], w=[xs_.r])

        def a2(i):
            xs_ = xs[i % 3]
            abank[i] = []
            for half in range(2):
                ps = next_ps()
                abank[i].append(ps)
                for k in range(4):
                    kc = half * 4 + k
                    kb.op("pe", lambda: nc.tensor.matmul(ps[:, k * 128:(k + 1) * 128], lhsT=xs_[:, kc * 128:(kc + 1) * 128],
                                                         rhs=ident[:], start=True, stop=True),
                          r=[xs_.r, ident.r], w=[ps.r], signal=(k == 3))

        def a3(i):
            h_ = hst[i % 3]
            for half, ps in enumerate(abank.pop(i)):
                kb.op("act", lambda: nc.scalar.copy(out=h_[:, half * 4:(half + 1) * 4, :],
                                                    in_=ps[:].rearrange("p (k t) -> p k t", k=4)),
                      r=[ps.r], w=[h_.r])
            kb.dma(hT_d[:, :, i * 128:(i + 1) * 128], h_[:], r=[h_.r], w=[hT_res], semres=hT_res)

        agen = swpipe_gen(ntile, [a1, a2, a3], lag=1)

    def pump(n=1):
        for _ in range(n):
            try:
                next(agen)
            except StopIteration:
                return

    def finish_A():
        pump(10 ** 6)
        if pa is not None:
            pa.close()
    pump(3)
    ctx = dict(nc=nc, kb=kb, es=es, debug=debug, pump=pump, finish_A=finish_A, scratch=scratch, hT_d=hT_d, hT_res=hT_res, ident=ident, gbc=gbc,
               next_ps=next_ps, w_in=w_in, x_all=x_all, out=out, epsb=epsb)
    fin = [hT_res]
    ctx["psum"] = psum
    ctx["mem"] = mem
    ctx["gvec"] = gvec
    if "stopA" in debug:
        finish_A()
    if "stopA" not in debug:
        fin += phase_s5(ctx)
    if "stopS" not in debug and "stopA" not in debug and "stopS1" not in debug and "stopS2" not in debug:
        fin += phase_attn(ctx)
        if "stopN" not in debug:
            fin += phase_merge_ffn(ctx)
    kb.finish(fin)
    if "right_es" in ctx:
        ctx["right_es"].close()


def kv_layout():
    off = {}
    vecs = []

    def add(name, v):
        off[name] = (sum(len(x) for x in vecs), len(v))
        vecs.append(np.asarray(v, np.float32))
    i16 = np.arange(16)
    t = np.arange(129)
    add("i", i16)
    add("ni", -i16)
    add("preF_a", 15 - 16 * t)
    add("c15", np.full(129, 15.0))
    add("preB_a", -16 * t)
    add("c0", np.zeros(129))
    add("postF_a", 1 + 16 * t)
    add("c1", np.ones(129))
    add("postB_a", 16 + 16 * t)
    add("c16", np.full(129, 16.0))
    add("c2048", np.full(129, 2048.0))
    return np.concatenate(vecs), off


NKV = len(kv_layout()[0])


def phase_s5(ctx):
    nc, kb, es = ctx["nc"], ctx["kb"], ctx["es"]
    scratch, debug = ctx["scratch"], ctx["debug"]
    ident, next_ps = ctx["ident"], ctx["next_ps"]
    hT_d, hT_res, w_in = ctx["hT_d"], ctx["hT_res"], ctx["w_in"]
    V, G, A, PE = nc.vector, nc.gpsimd, nc.scalar, nc.tensor
    _, koff = kv_layout()
    pump, finish_A = ctx["pump"], ctx["finish_A"]

    s5_small = dram_in(nc, "s5_small", [128, 2, 3, 16])
    s5_bc = dram_in(nc, "s5_bc", [128, 2, 4, 256])
    s5_dbc = dram_in(nc, "s5_dbc", [128, 512])
    kv_d = dram_in(nc, "kv", [128, NKV])
    msk_d = dram_in(nc, "s5_mask", [128, 2, 2, 256])
    flags_d = dram_in(nc, "flags", [128, 2])
    w_glu_d = dram_in(nc, "w_glu", [512, 512])
    dbgf = lambda n: BF16
    Q_d = scratch("Q_scr", [128, 16 * 2 * 2 * 256], dbgf("Q_scr")); Q_res = kb.res("Q_d")
    T_d = scratch("T_scr", [128, 32 * 2 * 256], dbgf("T_scr")); T_res = kb.res("T_d")
    Ug_d = scratch("Ug_scr", [128, 32 * 2 * 128], BF16); Ug_res = kb.res("Ug_d")
    Xp_d = scratch("Xp_scr", [128, 2 * 2 * 16 * 128], dbgf("Xp_scr")); Xp_res = kb.res("Xp_d")
    s5T_d = scratch("s5T_scr", [128, 4, NT], dbgf("s5T_scr")); s5T_res = kb.res("s5T_d")
    ctx["s5T_d"], ctx["s5T_res"] = s5T_d, s5T_res

    dumps = []

    def dump(name, ap, shape, res):
        if name in debug:
            dd = nc.dram_tensor(name, list(shape), F32, kind="ExternalOutput").ap()
            rr = kb.res(name)
            kb.dma(dd, ap, r=[res], w=[rr], q="pool")
            dumps.append(rr)

    with ExitStack() as p5:
        sm = Tile(kb, p5, "s5sm", [128, 2, 3, 16], F32)
        kb.dma(sm[:], s5_small, w=[sm.r])
        kv = Tile(kb, p5, "kv_sb", [128, NKV], F32)
        kb.dma(kv[:], kv_d, w=[kv.r])
        flags = Tile(kb, p5, "flags_sb", [128, 2], F32)
        kb.dma(flags[:], flags_d, w=[flags.r])
        par = Tile(kb, p5, "s5par", [128, 2, 8, 16], F32)
        Eend = Tile(kb, p5, "s5Eend", [128, 2, 2, 16], F32)
        KkT = Tile(kb, p5, "s5KkT", [128, 16 * 2 * 2 * 2 * 128], BF16)

        def kvv(name, T):
            o, n = koff[name]
            return kv[:, o:o + T]

        for d in range(2):
            pr = par
            kb.op("act", lambda: A.activation(out=pr[:, d, 0, :], in_=sm[:, d, 2, :], func=AF.Exp), r=[sm.r], w=[pr.r])
            kb.op("dve", lambda: V.tensor_tensor(out=pr[:, d, 1, :], in0=sm[:, d, 0, :], in1=pr[:, d, 0, :], op=ALU.mult), r=[sm.r, pr.r], w=[pr.r])
            kb.op("dve", lambda: V.tensor_tensor(out=pr[:, d, 2, :], in0=sm[:, d, 1, :], in1=pr[:, d, 0, :], op=ALU.mult), r=[sm.r, pr.r], w=[pr.r])
            kb.op("dve", lambda: V.tensor_scalar(out=pr[:, d, 2, :], in0=pr[:, d, 2, :], scalar1=1.0 / (2.0 * math.pi), scalar2=None, op0=ALU.mult), r=[pr.r], w=[pr.r])
            kb.op("act", lambda: A.activation(out=pr[:, d, 3, :], in_=pr[:, d, 1, :], func=AF.Exp, scale=16.0), r=[pr.r], w=[pr.r])

        def tab(st, dre, dim, d, kang, kmag, T, wres):
            tr, ti, tf = st
            n = 16 * T
            r3 = tr[:, 0:n].rearrange("p (g t) -> p g t", g=16)
            i3 = ti[:, 0:n].rearrange("p (g t) -> p g t", g=16)
            f3 = tf[:, 0:n].rearrange("p (g t) -> p g t", g=16)
            thb = par[:, d, 2, :].unsqueeze(2).to_broadcast([128, 16, T])
            arb = par[:, d, 1, :].unsqueeze(2).to_broadcast([128, 16, T])
            kab = kang.unsqueeze(1).to_broadcast([128, 16, T])
            kmb = kmag.unsqueeze(1).to_broadcast([128, 16, T])
            kb.op("dve", lambda: V.tensor_tensor(out=r3, in0=thb, in1=kab, op=ALU.mult), r=[par.r, kv.r], w=[tr.r])
            for which, dst in ((0, dim), (1, dre)):
                if which == 1:
                    kb.op("dve", lambda: V.tensor_scalar_add(out=r3, in0=r3, scalar1=0.25), r=[tr.r], w=[tr.r])
                kb.op("dve", lambda: V.tensor_copy(out=i3, in_=r3), r=[tr.r], w=[ti.r])
                kb.op("dve", lambda: V.tensor_copy(out=f3, in_=i3), r=[ti.r], w=[tf.r])
                kb.op("dve", lambda: V.tensor_tensor(out=f3, in0=r3, in1=f3, op=ALU.subtract), r=[tr.r, tf.r], w=[tf.r])
                kb.op("act", lambda: A.activation(out=dst, in_=f3, func=AF.Sin, scale=TWO_PI), r=[tf.r], w=[wres])
            kb.op("dve", lambda: V.tensor_tensor(out=f3, in0=arb, in1=kmb, op=ALU.mult), r=[par.r, kv.r], w=[tf.r])
            kb.op("act", lambda: A.activation(out=f3, in_=f3, func=AF.Exp), r=[tf.r], w=[tf.r])
            kb.op("dve", lambda: V.tensor_tensor(out=dre, in0=dre, in1=f3, op=ALU.mult), r=[tf.r, wres], w=[wres])
            kb.op("dve", lambda: V.tensor_tensor(out=dim, in0=dim, in1=f3, op=ALU.mult), r=[tf.r, wres], w=[wres])
            pump(2)

        with ExitStack() as s1:
            bc = Tile(kb, s1, "s5bc", [128, 2, 4, 256], F32)
            kb.dma(bc[:], s5_bc, w=[bc.r])
            dbc = Tile(kb, s1, "s5dbc", [128, 512], F32)
            kb.dma(dbc[:], s5_dbc, w=[dbc.r])
            Et = Tile(kb, s1, "s5E", [128, 2, 2, 2, 256], F32)
            Bk = Tile(kb, s1, "s5Bk", [128, 2, 2, 256], F32)
            st = (Tile(kb, s1, "tb_r", [128, 16 * 16], F32), Tile(kb, s1, "tb_i", [128, 16 * 16], I32),
                  Tile(kb, s1, "tb_f", [128, 16 * 16], F32))
            for d in range(2):
                for sg, nm in ((0, "i"), (1, "ni")):
                    v3 = lambda part: Et[:, d, sg, part, :].rearrange("p (g t) -> p g t", g=16)
                    tab(st, v3(0), v3(1), d, kvv(nm, 16), kvv(nm, 16), 16, Et.r)
            for d in range(2):
                pr = par
                E3 = lambda part: Et[:, d, 0, part, :].rearrange("p (g t) -> p g t", g=16)
                lr, li = E3(0)[:, :, 1], E3(1)[:, :, 1]
                ar_, ai_ = sm[:, d, 0, :], sm[:, d, 1, :]
                t6, t7, kr, ki = pr[:, d, 6, :], pr[:, d, 7, :], pr[:, d, 4, :], pr[:, d, 5, :]
                o = lambda fn: kb.op("dve", fn, r=[pr.r, sm.r, Et.r], w=[pr.r])
                o(lambda: V.tensor_scalar_add(out=t6, in0=lr, scalar1=-1.0))
                o(lambda: V.tensor_tensor(out=kr, in0=t6, in1=ar_, op=ALU.mult))
                o(lambda: V.tensor_tensor(out=t7, in0=li, in1=ai_, op=ALU.mult))
                o(lambda: V.tensor_tensor(out=kr, in0=kr, in1=t7, op=ALU.add))
                o(lambda: V.tensor_tensor(out=ki, in0=li, in1=ar_, op=ALU.mult))
                o(lambda: V.tensor_tensor(out=t7, in0=t6, in1=ai_, op=ALU.mult))
                o(lambda: V.tensor_tensor(out=ki, in0=ki, in1=t7, op=ALU.subtract))
                o(lambda: V.tensor_tensor(out=t6, in0=ar_, in1=ar_, op=ALU.mult))
                o(lambda: V.tensor_tensor(out=t7, in0=ai_, in1=ai_, op=ALU.mult))
                o(lambda: V.tensor_tensor(out=t6, in0=t6, in1=t7, op=ALU.add))
                o(lambda: V.reciprocal(out=t6, in_=t6))
                o(lambda: V.tensor_tensor(out=kr, in0=kr, in1=t6, op=ALU.mult))
                o(lambda: V.tensor_tensor(out=ki, in0=ki, in1=t6, op=ALU.mult))
                Br = bc[:, d, 0, :].rearrange("p (g c) -> p g c", g=16)
                Bi = bc[:, d, 1, :].rearrange("p (g c) -> p g c", g=16)
                krb = kr.unsqueeze(2).to_broadcast([128, 16, 16])
                kib = ki.unsqueeze(2).to_broadcast([128, 16, 16])
                t1 = st[0][:, 0:256].rearrange("p (g c) -> p g c", g=16)
                t2 = st[2][:, 0:256].rearrange("p (g c) -> p g c", g=16)
                Bkr = Bk[:, d, 0, :].rearrange("p (g c) -> p g c", g=16)
                Bki = Bk[:, d, 1, :].rearrange("p (g c) -> p g c", g=16)
                o2 = lambda fn: kb.op("dve", fn, r=[pr.r, bc.r, st[0].r, st[2].r, Bk.r], w=[st[0].r, st[2].r, Bk.r])
                o2(lambda: V.tensor_tensor(out=t1, in0=Br, in1=krb, op=ALU.mult))
                o2(lambda: V.tensor_tensor(out=t2, in0=Bi, in1=kib, op=ALU.mult))
                o2(lambda: V.tensor_tensor(out=Bkr, in0=t1, in1=t2, op=ALU.subtract))
                o2(lambda: V.tensor_tensor(out=t1, in0=Bi, in1=krb, op=ALU.mult))
                o2(lambda: V.tensor_tensor(out=t2, in0=Br, in1=kib, op=ALU.mult))
                o2(lambda: V.tensor_tensor(out=Bki, in0=t1, in1=t2, op=ALU.add))

            Kk = Tile(kb, s1, "s5Kk", [128, 16, 2, 2, 256], BF16)
            Qt = Tile(kb, s1, "s5Q", [128, 16, 2, 2, 256], BF16)
            s1o = ExitStack()
            o1 = Tile(kb, s1o, "s5o1", [128, 4096], F32)
            o2t = Tile(kb, s1o, "s5o2", [128, 4096], F32)

            o1r = [kb.res("o1a"), kb.res("o1b")]
            o2r = [kb.res("o2a"), kb.res("o2b")]
            GS = 12

            def outer(dst, d, Mr, Mi, sg, neg_im):
                Er = Et[:, d, sg, 0, :].rearrange("p (g t) -> p g t", g=16).unsqueeze(3).to_broadcast([128, 16, 16, 16])
                Ei = Et[:, d, sg, 1, :].rearrange("p (g t) -> p g t", g=16).unsqueeze(3).to_broadcast([128, 16, 16, 16])
                Mrb = Mr.rearrange("p (g c) -> p g c", g=16).unsqueeze(2).to_broadcast([128, 16, 16, 16])
                Mib = Mi.rearrange("p (g c) -> p g c", g=16).unsqueeze(2).to_broadcast([128, 16, 16, 16])
                a1 = o1[:, :].rearrange("p (g i c) -> p g i c", g=16, i=16)
                a2 = o2t[:, :].rearrange("p (g i c) -> p g i c", g=16, i=16)
                dre = dst[:, :, d, 0, :].rearrange("p g (i c) -> p g i c", i=16)
                dimm = dst[:, :, d, 1, :].rearrange("p g (i c) -> p g i c", i=16)
                rr = [Et.r, Bk.r, bc.r]

                def mul2(o, orr, x, y):
                    kb.op("dve", lambda: V.tensor_tensor(out=o[:, 0:GS], in0=x[:, 0:GS], in1=y[:, 0:GS], op=ALU.mult), r=rr, w=[orr[0]])
                    kb.op("pool", lambda: G.tensor_tensor(out=o[:, GS:16], in0=x[:, GS:16], in1=y[:, GS:16], op=ALU.mult), r=rr, w=[orr[1]])
                mul2(a1, o1r, Mrb, Er)
                pump(1)
                mul2(a2, o2r, Mib, Ei)
                pump(1)
                kb.op("dve", lambda: V.tensor_tensor(out=dre, in0=a1, in1=a2, op=ALU.subtract), r=o1r + o2r, w=[dst.r])
                pump(1)
                mul2(a1, o1r, Mrb, Ei)
                pump(1)
                mul2(a2, o2r, Mib, Er)
                pump(1)
                if neg_im:
                    kb.op("dve", lambda: V.scalar_tensor_tensor(out=dimm, in0=a1, scalar=-1.0, in1=a2, op0=ALU.mult, op1=ALU.subtract),
                          r=o1r + o2r, w=[dst.r])
                else:
                    kb.op("dve", lambda: V.tensor_tensor(out=dimm, in0=a1, in1=a2, op=ALU.add), r=o1r + o2r, w=[dst.r])

            outer(Kk, 0, Bk[:, 0, 0, :], Bk[:, 0, 1, :], 1, False)
            outer(Kk, 1, Bk[:, 1, 0, :], Bk[:, 1, 1, :], 0, False)
            outer(Qt, 0, bc[:, 0, 2, :], bc[:, 0, 3, :], 0, True)
            outer(Qt, 1, bc[:, 1, 2, :], bc[:, 1, 3, :], 1, True)
            if True:
                kb.dma(Q_d, Qt[:].rearrange("p g d t n -> p (g d t n)"), r=[Qt.r], w=[Q_res])

            s1o.close()
            kb.barrier()
            n = 0
            ps = None
            for g2 in range(16):
                for d in range(2):
                    for part in range(2):
                        for kc2 in range(2):
                            if n % 4 == 0:
                                ps = next_ps()
                            kb.op("pe", lambda: PE.matmul(ps[:, (n % 4) * 128:(n % 4 + 1) * 128], lhsT=Kk[:, g2, d, part, kc2 * 128:(kc2 + 1) * 128],
                                                          rhs=ident[:], start=True, stop=True), r=[Kk.r, ident.r], w=[ps.r], signal=(n % 4 == 3))
                            if n % 4 == 3:
                                kb.op("act", lambda: A.copy(out=KkT[:, (n - 3) * 128:(n + 1) * 128], in_=ps[:]), r=[ps.r], w=[KkT.r])
                                if n % 16 == 15:
                                    pump(1)
                            n += 1

            msk = Tile(kb, s1, "s5msk", [128, 2, 2, 256], F32)
            kb.dma(msk[:], msk_d, w=[msk.r])
            Tts = [Tile(kb, s1, f"s5T{i}", [128, 256], BF16) for i in range(4)]
            Dm = [Tile(kb, s1, f"s5Dm{i}", [128, 128], BF16) for i in range(4)]
            pfs = [Tile(kb, s1, f"s5pf{i}", [128, 512], F32) for i in range(4)]
            tq = [Tile(kb, s1, f"s5tq{i}", [128, 512], F32) for i in range(4)]
            tgb = {}

            def tg1(it):
                g, kc2 = it // 2, it % 2
                gp, g2 = g % 2, g // 2
                dm = Dm[g % 4]
                if kc2 == 0:
                    kb.op("pool", lambda: G.tensor_tensor(out=dm[:].rearrange("p (i c) -> p i c", i=8), in0=ident[:].rearrange("p (i c) -> p i c", i=8),
                                                           in1=dbc[:, g * 16:(g + 1) * 16].unsqueeze(1).to_broadcast([128, 8, 16]), op=ALU.mult),
                          r=[ident.r, dbc.r], w=[dm.r])
                ps = next_ps()
                tgb[it] = ps
                mms = []
                for d in range(2):
                    for hh in range(2):
                        with_d = (d == 1 and hh == kc2)
                        for part in range(2):
                            mms.append((ps[:, d * 256 + hh * 128:d * 256 + (hh + 1) * 128],
                                        Kk[gp * 64:(gp + 1) * 64, g2, d, part, kc2 * 128:(kc2 + 1) * 128],
                                        Qt[gp * 64:(gp + 1) * 64, g2, d, part, hh * 128:(hh + 1) * 128],
                                        part == 0, part == 1 and not with_d))
                        if with_d:
                            mms.append((ps[:, 256 + kc2 * 128:256 + (kc2 + 1) * 128], ident[:], dm[:], False, True))
                for mi, (o_, l_, r_, st_, sp_) in enumerate(mms):
                    kb.op("pe", lambda: PE.matmul(o_, lhsT=l_, rhs=r_, start=st_, stop=sp_), r=[Kk.r, Qt.r, ident.r, dm.r], w=[ps.r],
                          signal=(mi == len(mms) - 1))

            def tg2(it):
                pf = pfs[it % 4]
                ps = tgb.pop(it)
                kb.op("act", lambda: A.copy(out=pf[:], in_=ps[:]), r=[ps.r], w=[pf.r])

            def tg3(it):
                g, kc2 = it // 2, it % 2
                pf = pfs[it % 4]; tt = tq[it % 4]
                kb.op("dve", lambda: V.tensor_tensor(out=tt[:].rearrange("p (a n) -> p a n", a=2), in0=pf[:].rearrange("p (a n) -> p a n", a=2),
                                                      in1=msk[:, :, kc2, :], op=ALU.mult), r=[pf.r, msk.r], w=[tt.r])
                Tt = Tts[it % 4]
                kb.op("dve", lambda: V.tensor_tensor(out=Tt[:], in0=tt[:, 0:256], in1=tt[:, 256:512], op=ALU.add), r=[tt.r], w=[Tt.r])
                kb.dma(T_d[:, it * 256:(it + 1) * 256], Tt[:], r=[Tt.r], w=[T_res], semres=T_res)
                if it % 6 == 5:
                    pump(1)

            swpipe(64, [tg1, tg2, tg3], lag=1)
        kb.barrier()
        if "stopS1" in debug:
            return [Q_res, T_res]
        def do_half(hidx, own):
            off = 0 if own else NT
            with ExitStack() as s2:
                Ug = Tile(kb, s2, f"s5Ug{hidx}", [128, 32 * 2 * 128], BF16)
                with ExitStack() as s2a:
                    Up = Tile(kb, s2a, f"s5Up{hidx}", [128, 32 * 256], BF16)
                    hT = Tile(kb, s2a, f"s5hT{hidx}", [128, 8, NT], BF16)
                    kb.dma(hT[:], hT_d[:, :, off:off + NT], r=[hT_res], w=[hT.r])
                    Up4 = Up[:, :].rearrange("p (g i c) -> p g i c", g=32, i=16)
                    for i in range(16):
                        ps = next_ps()
                        for kc in range(8):
                            kb.op("pe", lambda: PE.matmul(ps[:, :], lhsT=hT[:, kc, i:NT:16], rhs=w_s5[:, kc, :], start=(kc == 0), stop=(kc == 7)),
                                  r=[hT.r, w_s5.r], w=[ps.r], signal=(kc == 7))
                        src = ps[:, :].rearrange("p (g c) -> p g c", g=32)
                        if i % 2 == 0:
                            kb.op("act", lambda: A.copy(out=Up4[:, :, i, :], in_=src), r=[ps.r], w=[Up.r])
                        else:
                            kb.op("dve", lambda: V.tensor_copy(out=Up4[:, :, i, :], in_=src), r=[ps.r], w=[Up.r])
                    n = 0
                    for g in range(32):
                        for kc2 in range(2):
                            if n % 4 == 0:
                                ps = next_ps()
                            kb.op("pe", lambda: PE.matmul(ps[:, (n % 4) * 128:(n % 4 + 1) * 128], lhsT=Up[:, g * 256 + kc2 * 128:g * 256 + (kc2 + 1) * 128],
                                                          rhs=ident[:], start=True, stop=True), r=[Up.r, ident.r], w=[ps.r], signal=(n % 4 == 3))
                            if n % 4 == 3:
                                if (n // 4) % 2 == 0:
                                    kb.op("act", lambda: A.copy(out=Ug[:, (n - 3) * 128:(n + 1) * 128], in_=ps[:]), r=[ps.r], w=[Ug.r])
                                else:
                                    kb.op("dve", lambda: V.tensor_copy(out=Ug[:, (n - 3) * 128:(n + 1) * 128], in_=ps[:]), r=[ps.r], w=[Ug.r])
                            n += 1
                kb.barrier()
                if own:
                    kb.dma(Ug_d, Ug[:], r=[Ug.r], w=[Ug_res])
                    dump("dbg_Ug", Ug[:], [128, 8192], Ug.r)
                    dump("dbg_KkT", KkT[:], [128, 16384], KkT.r)
                    dump("dbg_ws5", w_s5[:], [128, 8, 512], w_s5.r)
                Ssb = Tile(kb, s2, f"s5Ssb{hidx}", [128, 2, 16, 128], F32)
                d1 = Tile(kb, s2, f"s5d1{hidx}", [128, 2, 16, 129], F32)
                d0 = Tile(kb, s2, f"s5d0{hidx}", [128, 16, 129], F32)
                Wt = Tile(kb, s2, f"s5W{hidx}", [128, 2, 16, 129], F32)
                Xp = Tile(kb, s2, f"s5Xp{hidx}", [128, 2, 2, 16, 128], BF16) if own else None
                for d in range(2):
                    tre, tim = PRE[d]
                    for part in range(2):
                        for b4 in range(4):
                            ps = next_ps()
                            for q in range(4):
                                g2 = b4 * 4 + q
                                for gp in range(2):
                                    g = 2 * g2 + gp
                                    for kc2 in range(2):
                                        base = (((g2 * 2 + d) * 2 + part) * 2 + kc2) * 128
                                        kb.op("pe", lambda: PE.matmul(ps[gp * 64:(gp + 1) * 64, q * 128:(q + 1) * 128],
                                                                      lhsT=KkT[:, base + gp * 64:base + (gp + 1) * 64],
                                                                      rhs=Ug[:, (g * 2 + kc2) * 128:(g * 2 + kc2 + 1) * 128],
                                                                      start=(kc2 == 0), stop=(kc2 == 1)),
                                              r=[KkT.r, Ug.r], w=[ps.r], signal=(q == 3 and gp == 1 and kc2 == 1))
                            kb.op("act", lambda: A.copy(out=Ssb[:, part, b4 * 4:(b4 + 1) * 4, :], in_=ps[:, :].rearrange("p (q j) -> p q j", q=4)),
                                  r=[ps.r], w=[Ssb.r])
                    Sre = Ssb[:, 0, :, :] if d == 0 else Ssb[:, 0, :, ::-1]
                    Sim = Ssb[:, 1, :, :] if d == 0 else Ssb[:, 1, :, ::-1]
                    pr_, pi_ = tre[:, :, 1:129], tim[:, :, 1:129]
                    ta, tb = Wt[:, 0, :, 0:128], Wt[:, 1, :, 0:128]
                    o_re, o_im = d1[:, 0, :, 1:129], d1[:, 1, :, 1:129]
                    rr = [Ssb.r, tre.r]
                    kb.op("dve", lambda: V.tensor_tensor(out=o_re, in0=pr_, in1=Sre, op=ALU.mult), r=rr, w=[d1.r])
                    kb.op("pool", lambda: G.tensor_tensor(out=ta, in0=pi_, in1=Sim, op=ALU.mult), r=rr, w=[Wt.r])
                    kb.op("dve", lambda: V.tensor_tensor(out=o_im, in0=pr_, in1=Sim, op=ALU.mult), r=rr, w=[d1.r])
                    kb.op("pool", lambda: G.tensor_tensor(out=tb, in0=pi_, in1=Sre, op=ALU.mult), r=rr, w=[Wt.r])
                    kb.op("dve", lambda: V.tensor_tensor(out=o_re, in0=o_re, in1=ta, op=ALU.subtract), r=[Wt.r, d1.r], w=[d1.r])
                    kb.op("dve", lambda: V.tensor_tensor(out=o_im, in0=o_im, in1=tb, op=ALU.add), r=[Wt.r, d1.r], w=[d1.r])
                    for part in range(2):
                        if own:
                            kb.op("dve", lambda: V.tensor_scalar(out=d1[:, part, :, 0], in0=Eend[:, d, part, :], scalar1=flags[:, d:d + 1], scalar2=None, op0=ALU.mult),
                                  r=[Eend.r, flags.r], w=[d1.r])
                        else:
                            kb.op("dve", lambda: V.memset(d1[:, part, :, 0], 0.0), w=[d1.r])
                    kb.op("dve", lambda: V.tensor_copy(out=d0[:, :, 1:129], in_=par[:, d, 3, :].unsqueeze(2).to_broadcast([128, 16, 128])), r=[par.r], w=[d0.r])
                    kb.op("dve", lambda: V.memset(d0[:, :, 0:1], 0.0), w=[d0.r])
                    for part in range(2):
                        kb.op("dve", lambda: V.tensor_tensor_scan(out=Wt[:, part, :, :].rearrange("p g t -> p (g t)"), data0=d0[:].rearrange("p g t -> p (g t)"),
                                                                   data1=d1[:, part, :, :].rearrange("p g t -> p (g t)"), initial=0.0, op0=ALU.mult, op1=ALU.add),
                              r=[d0.r, d1.r], w=[Wt.r])
                    if own:
                        tre, tim = POST[d]
                        Wr, Wi = Wt[:, 0, :, 0:128], Wt[:, 1, :, 0:128]
                        pr_, pi_ = tre[:, :, 0:128], tim[:, :, 0:128]
                        oR = Xp[:, d, 0, :, :] if d == 0 else Xp[:, d, 0, :, ::-1]
                        oI = Xp[:, d, 1, :, :] if d == 0 else Xp[:, d, 1, :, ::-1]
                        t1a, t1b = d1[:, 0, :, 0:128], d1[:, 1, :, 0:128]
                        t2a, t2b = Ssb[:, 0, :, :], Ssb[:, 1, :, :]
                        rr = [Wt.r, tre.r]
                        kb.op("dve", lambda: V.tensor_tensor(out=t1a, in0=pr_, in1=Wr, op=ALU.mult), r=rr, w=[d1.r])
                        kb.op("pool", lambda: G.tensor_tensor(out=t2a, in0=pi_, in1=Wi, op=ALU.mult), r=rr, w=[Ssb.r])
                        kb.op("dve", lambda: V.tensor_tensor(out=t1b, in0=pr_, in1=Wi, op=ALU.mult), r=rr, w=[d1.r])
                        kb.op("pool", lambda: G.tensor_tensor(out=t2b, in0=pi_, in1=Wr, op=ALU.mult), r=rr, w=[Ssb.r])
                        kb.op("dve", lambda: V.tensor_tensor(out=oR, in0=t1a, in1=t2a, op=ALU.subtract), r=[d1.r, Ssb.r], w=[Xp.r])
                        kb.op("dve", lambda: V.tensor_tensor(out=oI, in0=t1b, in1=t2b, op=ALU.add), r=[d1.r, Ssb.r], w=[Xp.r])
                    else:
                        tre, tim = POSTE[d]
                        Wr, Wi = Wt[:, 0, :, 128], Wt[:, 1, :, 128]
                        pr_, pi_ = tre[:, :, 0], tim[:, :, 0]
                        a1, a2 = d1[:, 0, :, 1], d1[:, 1, :, 1]
                        rr = [Wt.r, tre.r, d1.r]
                        kb.op("dve", lambda: V.tensor_tensor(out=a1, in0=pr_, in1=Wr, op=ALU.mult), r=rr, w=[d1.r])
                        kb.op("dve", lambda: V.tensor_tensor(out=a2, in0=pi_, in1=Wi, op=ALU.mult), r=rr, w=[d1.r])
                        kb.op("dve", lambda: V.tensor_tensor(out=Eend[:, d, 0, :], in0=a1, in1=a2, op=ALU.subtract), r=rr, w=[Eend.r])
                        kb.op("dve", lambda: V.tensor_tensor(out=a1, in0=pr_, in1=Wi, op=ALU.mult), r=rr, w=[d1.r])
                        kb.op("dve", lambda: V.tensor_tensor(out=a2, in0=pi_, in1=Wr, op=ALU.mult), r=rr, w=[d1.r])
                        kb.op("dve", lambda: V.tensor_tensor(out=Eend[:, d, 1, :], in0=a1, in1=a2, op=ALU.add), r=rr, w=[Eend.r])
                if own:
                    kb.dma(Xp_d, Xp[:].rearrange("p d t g j -> p (d t g j)"), r=[Xp.r], w=[Xp_res])

        with ExitStack() as s2t:
            w_s5 = Tile(kb, s2t, "w_s5", [128, 8, 512], BF16)
            kb.dma(w_s5[:], w_in.rearrange("(kc p) n -> p kc n", p=128)[:, :, 1536:2048], w=[w_s5.r], q="pool")
            PRE, POST, POSTE = [], [], []
            for d in range(2):
                PRE.append((Tile(kb, s2t, f"s5pre_r{d}", [128, 16, 129], F32), Tile(kb, s2t, f"s5pre_i{d}", [128, 16, 129], F32)))
                POST.append((Tile(kb, s2t, f"s5post_r{d}", [128, 16, 128], F32), Tile(kb, s2t, f"s5post_i{d}", [128, 16, 128], F32)))
                POSTE.append((Tile(kb, s2t, f"s5poste_r{d}", [128, 16, 1], F32), Tile(kb, s2t, f"s5poste_i{d}", [128, 16, 1], F32)))
            with ExitStack() as s2tt:
                st = (Tile(kb, s2tt, "tc_r", [128, 16 * 129], F32), Tile(kb, s2tt, "tc_i", [128, 16 * 129], I32),
                      Tile(kb, s2tt, "tc_f", [128, 16 * 129], F32))
                for d in range(2):
                    tab(st, PRE[d][0][:, :, :], PRE[d][1][:, :, :], d, kvv("preF_a" if d == 0 else "preB_a", 129), kvv("c15" if d == 0 else "c0", 129), 129, PRE[d][0].r)
                    tab(st, POST[d][0][:, :, :], POST[d][1][:, :, :], d, kvv("postF_a" if d == 0 else "postB_a", 128), kvv("c1" if d == 0 else "c16", 128), 128, POST[d][0].r)
                    tab(st, POSTE[d][0][:, :, :], POSTE[d][1][:, :, :], d, kvv("c2048", 1), kvv("c0", 1), 1, POSTE[d][0].r)
            finish_A()
            do_half(0, False)
            kb.barrier()
            do_half(1, True)
            kb.barrier()
        kb.barrier()
        if "stopS2" in debug:
            return [Q_res, T_res, Xp_res, Ug_res] + dumps

        with ExitStack() as s4:
            Tt = Tile(kb, s4, "s4T", [128, 32, 2, 256], BF16)
            kb.dma(Tt[:].rearrange("p g k n -> p (g k n)"), T_d, r=[T_res], w=[Tt.r])
            Qt = Tile(kb, s4, "s4Q", [128, 16, 2, 2, 256], BF16)
            kb.dma(Qt[:].rearrange("p g d t n -> p (g d t n)"), Q_d, r=[Q_res], w=[Qt.r])
            Ug = Tile(kb, s4, "s4Ug", [128, 32 * 2 * 128], BF16)
            kb.dma(Ug[:], Ug_d, r=[Ug_res], w=[Ug.r])
            Xp = Tile(kb, s4, "s4Xp", [128, 2, 2, 16, 128], BF16)
            kb.dma(Xp[:].rearrange("p d t g j -> p (d t g j)"), Xp_d, r=[Xp_res], w=[Xp.r])
            zp = Tile(kb, s4, "s4zp", [128, 16, 512], BF16)
            w_glu = Tile(kb, s4, "w_glu_sb", [128, 4, 512], BF16)
            kb.dma(w_glu[:], w_glu_d.rearrange("(kc p) n -> p kc n", p=128), w=[w_glu.r], q="pool")
            for g in range(32):
                gp, g2 = g % 2, g // 2
                ps = next_ps()
                reg = ps[:, 0:256]
                for kc2 in range(2):
                    kb.op("pe", lambda: PE.matmul(reg, lhsT=Ug[:, (g * 2 + kc2) * 128:(g * 2 + kc2 + 1) * 128], rhs=Tt[:, g, kc2, :], start=(kc2 == 0), stop=False),
                          r=[Ug.r, Tt.r], w=[ps.r], signal=False)
                for d in range(2):
                    for part in range(2):
                        last = (d == 1 and part == 1)
                        kb.op("pe", lambda: PE.matmul(reg, lhsT=Xp[gp * 64:(gp + 1) * 64, d, part, g2, :], rhs=Qt[gp * 64:(gp + 1) * 64, g2, d, part, :],
                                                      start=False, stop=last), r=[Xp.r, Qt.r], w=[ps.r], signal=last)
                kb.op("act", lambda: A.activation(out=zp[:, :, g * 16:(g + 1) * 16], in_=reg.rearrange("p (i c) -> p i c", i=16), func=AF.Gelu_apprx_tanh),
                      r=[ps.r], w=[zp.r])
            zT = Tile(kb, s4, "s4zT", [128, 4, NT], BF16)
            for ch in range(4):
                zv = zT[:, ch, :].rearrange("p (j i) -> p i j", i=16)
                for i in range(16):
                    if i % 4 == 0:
                        ps = next_ps()
                    kb.op("pe", lambda: PE.matmul(ps[:, (i % 4) * 128:(i % 4 + 1) * 128], lhsT=zp[:, i, ch * 128:(ch + 1) * 128], rhs=ident[:], start=True, stop=True),
                          r=[zp.r, ident.r], w=[ps.r], signal=(i % 4 == 3))
                    if i % 4 == 3:
                        kb.op("dve", lambda: V.tensor_copy(out=zv[:, i - 3:i + 1, :], in_=ps[:, :].rearrange("p (i j) -> p i j", i=4)), r=[ps.r], w=[zT.r])
            sg = [Tile(kb, s4, f"s4sg{i}", [128, 512], BF16) for i in range(2)]
            so = [Tile(kb, s4, f"s4so{i}", [128, 512], BF16) for i in range(2)]
            it = 0
            for ch in range(4):
                for tt in range(4):
                    ps = next_ps()
                    s_ = sg[it % 2]; o_ = so[it % 2]; it += 1
                    for kc in range(4):
                        kb.op("pe", lambda: PE.matmul(ps[:, :], lhsT=w_glu[:, kc, ch * 128:(ch + 1) * 128], rhs=zT[:, kc, tt * 512:(tt + 1) * 512], start=(kc == 0), stop=(kc == 3)),
                              r=[w_glu.r, zT.r], w=[ps.r], signal=(kc == 3))
                    kb.op("act", lambda: A.activation(out=s_[:], in_=ps[:, :], func=AF.Sigmoid), r=[ps.r], w=[s_.r])
                    kb.op("dve", lambda: V.tensor_tensor(out=o_[:], in0=zT[:, ch, tt * 512:(tt + 1) * 512], in1=s_[:], op=ALU.mult),
                          r=[zT.r, s_.r], w=[o_.r])
                    kb.dma(s5T_d[:, ch, tt * 512:(tt + 1) * 512], o_[:], r=[o_.r], w=[s5T_res], semres=s5T_res)
        kb.barrier()
    kb.barrier()
    return [s5T_res]

def norm_rows(ctx, pool, x_, gtile, xs_, s_, junk):
    nc, kb, epsb = ctx["nc"], ctx["kb"], ctx["epsb"]
    A, V = nc.scalar, nc.vector
    kb.op("act", lambda: A.activation(out=junk[:], in_=x_[:], func=AF.Square, accum_out=s_[:, 0:1]), r=[x_.r], w=[junk.r, s_.r])
    kb.op("act", lambda: A.activation(out=s_[:, 1:2], in_=s_[:, 0:1], func=AF.Sqrt, scale=1.0 / D, bias=epsb[:, 0:1]), r=[s_.r, epsb.r], w=[s_.r])
    kb.op("dve", lambda: V.reciprocal(out=s_[:, 2:3], in_=s_[:, 1:2]), r=[s_.r], w=[s_.r])
    kb.op("dve", lambda: V.scalar_tensor_tensor(out=xs_[:], in0=x_[:], scalar=s_[:, 2:3], in1=gtile[:], op0=ALU.mult, op1=ALU.mult),
          r=[x_.r, s_.r, gtile.r], w=[xs_.r])


def phase_attn(ctx):
    nc, kb, es = ctx["nc"], ctx["kb"], ctx["es"]
    scratch, debug = ctx["scratch"], ctx["debug"]
    ident, next_ps = ctx["ident"], ctx["next_ps"]
    hT_d, hT_res, w_in = ctx["hT_d"], ctx["hT_res"], ctx["w_in"]
    V, G, A, PE = nc.vector, nc.gpsimd, nc.scalar, nc.tensor
    mem_d = ctx["mem"]
    gvec = ctx["gvec"]
    w_mkv_d = dram_in(nc, "w_mem_kv", [D, 1024])
    ctx["w_gate_d"] = dram_in(nc, "w_gate", [D, 3072])
    bext_d = dram_in(nc, "bext", [128, 4 * 15 * 64])
    maskw_d = dram_in(nc, "maskw", [128, 64])
    rowvec_d = dram_in(nc, "rowvec", [1, 7 * 768])
    psum = ctx["psum"]
    memT_d = scratch("memT_scr", [128, 4, NT], BF16); memT_res = kb.res("memT_d")
    naT_d = scratch("naT_scr", [128, 4, NT], BF16); naT_res = kb.res("naT_d")
    ctx["memT_d"], ctx["memT_res"], ctx["naT_d"], ctx["naT_res"] = memT_d, memT_res, naT_d, naT_res
    w_in_v = w_in.rearrange("(kc p) n -> p kc n", p=128)

    with ExitStack() as pm:
        hT = Tile(kb, pm, "m_hT", [128, 8, NT], BF16)
        kb.dma(hT[:], hT_d[:, :, 0:NT], r=[hT_res], w=[hT.r])
        w_q = Tile(kb, pm, "m_wq", [128, 8, 512], BF16)
        kb.dma(w_q[:], w_in_v[:, :, 2048:2560], w=[w_q.r], q="pool")
        w_kv = Tile(kb, pm, "m_wkv", [128, 8, 1024], BF16)
        kb.dma(w_kv[:], w_mkv_d.rearrange("(kc p) n -> p kc n", p=128), w=[w_kv.r], q="pool")
        gm = Tile(kb, pm, "m_g", [128, D], F32)
        kb.dma(gm[:], gvec[1:2, :].to_broadcast([128, D]), w=[gm.r])
        memT = Tile(kb, pm, "m_memT", [128, 8, 256], BF16)
        xt = Tile(kb, pm, "m_xt", [128, D], F32); xs = Tile(kb, pm, "m_xs", [128, D], BF16)
        junk = Tile(kb, pm, "m_junk", [128, D], BF16); ss = Tile(kb, pm, "m_ss", [128, 4], F32)
        for i in range(2):
            kb.dma(xt[:], mem_d[i * 128:(i + 1) * 128, :], w=[xt.r])
            norm_rows(ctx, pm, xt, gm, xs, ss, junk)
            for half in range(2):
                ps = next_ps()
                for k in range(4):
                    kc = half * 4 + k
                    kb.op("pe", lambda: PE.matmul(ps[:, k * 128:(k + 1) * 128], lhsT=xs[:, kc * 128:(kc + 1) * 128], rhs=ident[:], start=True, stop=True),
                          r=[xs.r, ident.r], w=[ps.r], signal=(k == 3))
                kb.op("act", lambda: A.copy(out=memT[:, half * 4:(half + 1) * 4, i * 128:(i + 1) * 128], in_=ps[:].rearrange("p (k t) -> p k t", k=4)),
                      r=[ps.r], w=[memT.r])
        kTm = Tile(kb, pm, "m_kT", [128, 4, 256], BF16)
        vm = Tile(kb, pm, "m_v", [128, 2, 512], BF16)
        for h in range(4):
            ps = next_ps()
            for kc in range(8):
                kb.op("pe", lambda: PE.matmul(ps[:, 0:256], lhsT=w_kv[:, kc, h * 128:(h + 1) * 128], rhs=memT[:, kc, :], start=(kc == 0), stop=(kc == 7)),
                      r=[w_kv.r, memT.r], w=[ps.r], signal=(kc == 7))
            kb.op("act", lambda: A.copy(out=kTm[:, h, :], in_=ps[:, 0:256]), r=[ps.r], w=[kTm.r])
        for mc in range(2):
            ps = next_ps()
            for kc in range(8):
                kb.op("pe", lambda: PE.matmul(ps[:, :], lhsT=memT[:, kc, mc * 128:(mc + 1) * 128], rhs=w_kv[:, kc, 512:1024], start=(kc == 0), stop=(kc == 7)),
                      r=[w_kv.r, memT.r], w=[ps.r], signal=(kc == 7))
            kb.op("act", lambda: A.copy(out=vm[:, mc, :], in_=ps[:, :]), r=[ps.r], w=[vm.r])
        qT = Tile(kb, pm, "m_qT", [128, 4, NT], BF16)
        for h in range(4):
            for tt in range(4):
                ps = next_ps()
                for kc in range(8):
                    kb.op("pe", lambda: PE.matmul(ps[:, :], lhsT=w_q[:, kc, h * 128:(h + 1) * 128], rhs=hT[:, kc, tt * 512:(tt + 1) * 512], start=(kc == 0), stop=(kc == 7)),
                          r=[w_q.r, hT.r], w=[ps.r], signal=(kc == 7))
                if tt % 2 == 0:
                    kb.op("act", lambda: A.copy(out=qT[:, h, tt * 512:(tt + 1) * 512], in_=ps[:, :]), r=[ps.r], w=[qT.r])
                else:
                    kb.op("dve", lambda: V.tensor_copy(out=qT[:, h, tt * 512:(tt + 1) * 512], in_=ps[:, :]), r=[ps.r], w=[qT.r])
        sc = 1.0 / math.sqrt(128.0)
        MB = 8
        Pn = [Tile(kb, pm, f"m_P{i}", [128, 256], F32) for i in range(MB)]
        Pb = [Tile(kb, pm, f"m_Pb{i}", [128, 256], BF16) for i in range(MB)]
        PT = [Tile(kb, pm, f"m_PT{i}", [128, 256], BF16) for i in range(MB)]
        st = [Tile(kb, pm, f"m_st{i}", [128, 4], F32) for i in range(MB)]
        oT = Tile(kb, pm, "m_oT", [128, 4, NT], BF16)
        mpools = {"qk": [0, [0, 1]], "t": [0, [2, 3, 4]], "pv": [0, [5, 6, 7]]}

        def mps(name):
            p = mpools[name]
            b = ctx["psum"][p[1][p[0] % len(p[1])]]
            p[0] += 1
            return b
        munits = [(h, sub) for h in range(4) for sub in range(16)]
        mqk, mtb = {}, {}

        def m1(u):
            h, sub = munits[u]
            s_ = st[u % MB]
            ps = mps("qk"); mqk[u] = ps
            kb.op("pe", lambda: PE.matmul(ps[:, 0:256], lhsT=qT[:, h, sub * 128:(sub + 1) * 128], rhs=kTm[:, h, :], start=True, stop=True),
                  r=[qT.r, kTm.r], w=[ps.r])
            kb.op("dve", lambda: V.reduce_max(out=s_[:, 0:1], in_=ps[:, 0:256], axis=AX.X), r=[ps.r], w=[s_.r])
            kb.op("dve", lambda: V.tensor_scalar(out=s_[:, 1:2], in0=s_[:, 0:1], scalar1=-sc, scalar2=None, op0=ALU.mult), r=[s_.r], w=[s_.r])

        def m2(u):
            p_ = Pn[u % MB]; pb_ = Pb[u % MB]; s_ = st[u % MB]
            ps = mqk.pop(u)
            kb.op("act", lambda: A.activation(out=p_[:], in_=ps[:, 0:256], func=AF.Exp, scale=sc, bias=s_[:, 1:2], accum_out=s_[:, 2:3]),
                  r=[ps.r, s_.r], w=[p_.r, s_.r])
            kb.op("dve", lambda: V.reciprocal(out=s_[:, 3:4], in_=s_[:, 2:3]), r=[s_.r], w=[s_.r])
            kb.op("dve", lambda: V.tensor_scalar(out=pb_[:], in0=p_[:], scalar1=s_[:, 3:4], scalar2=None, op0=ALU.mult), r=[p_.r, s_.r], w=[pb_.r])

        def m3(u):
            pb_ = Pb[u % MB]
            ps2 = mps("t"); mtb[u] = ps2
            for mc in range(2):
                kb.op("pe", lambda: PE.matmul(ps2[:, mc * 128:(mc + 1) * 128], lhsT=pb_[:, mc * 128:(mc + 1) * 128], rhs=ident[:], start=True, stop=True),
                      r=[pb_.r, ident.r], w=[ps2.r], signal=(mc == 1))

        def m4(u):
            pt_ = PT[u % MB]
            ps2 = mtb.pop(u)
            kb.op("act", lambda: A.copy(out=pt_[:], in_=ps2[:, 0:256]), r=[ps2.r], w=[pt_.r])

        def m5(u):
            h, sub = munits[u]
            pt_ = PT[u % MB]
            ps3 = mps("pv")
            for mc in range(2):
                kb.op("pe", lambda: PE.matmul(ps3[:, 0:128], lhsT=vm[:, mc, h * 128:(h + 1) * 128], rhs=pt_[:, mc * 128:(mc + 1) * 128], start=(mc == 0), stop=(mc == 1)),
                      r=[vm.r, pt_.r], w=[ps3.r], signal=(mc == 1))
            kb.op("dve", lambda: V.tensor_copy(out=oT[:, h, sub * 128:(sub + 1) * 128], in_=ps3[:, 0:128]), r=[ps3.r], w=[oT.r])

        swpipe(len(munits), [m1, m2, m3, m4, m5], lag=1)
        kb.dma(memT_d, oT[:], r=[oT.r], w=[memT_res])
    kb.barrier()

    with ExitStack() as pn:
        NKV_T = 2560
        bext8 = Tile(kb, pn, "n_bext8", [128, 4, 15, 64], BF16)
        rowv = Tile(kb, pn, "n_rowv", [1, 7 * 768], BF16)
        kb.dma(rowv[:], rowvec_d, w=[rowv.r], q="pool")
        ones1 = Tile(kb, pn, "n_ones1", [1, 128], BF16)
        kb.op("dve", lambda: V.memset(ones1[:], 1.0), w=[ones1.r])
        qT = Tile(kb, pn, "n_qT", [128, 4, NT], BF16)
        kT = Tile(kb, pn, "n_kT", [128, 4, NKV_T], BF16)
        Vt = Tile(kb, pn, "n_V", [128, 40, 512], BF16)
        yna = Tile(kb, pn, "n_y", [128, 8, 512], BF16)
        ob = [Tile(kb, pn, f"n_ob{i}", [128, 512], BF16) for i in range(2)]
        pn2 = ExitStack()
        bext = Tile(kb, pn2, "n_bext", [128, 4, 15, 64], F32)
        kb.dma(bext[:].rearrange("p h r k -> p (h r k)"), bext_d, w=[bext.r])
        maskw = Tile(kb, pn2, "n_maskw", [128, 64], F32)
        kb.dma(maskw[:], maskw_d, w=[maskw.r])
        kb.op("dve", lambda: V.tensor_tensor(out=bext[:].rearrange("p h r k -> p (h r) k"), in0=bext[:].rearrange("p h r k -> p (h r) k"),
                                              in1=maskw[:].unsqueeze(1).to_broadcast([128, 60, 64]), op=ALU.add), r=[bext.r, maskw.r], w=[bext.r])
        kb.op("dve", lambda: V.tensor_scalar(out=bext8[:].rearrange("p h r k -> p (h r k)"), in0=bext[:].rearrange("p h r k -> p (h r k)"),
                                              scalar1=8.0, scalar2=None, op0=ALU.mult), r=[bext.r], w=[bext8.r])
        hk = Tile(kb, pn2, "n_hT", [128, 8, NKV_T], BF16)
        kb.dma(hk[:, :, 0:256], hT_d[:, :, 2 * NT:2 * NT + 256], r=[hT_res], w=[hk.r])
        kb.dma(hk[:, :, 256:256 + NT], hT_d[:, :, 0:NT], r=[hT_res], w=[hk.r])
        kb.dma(hk[:, :, 256 + NT:NKV_T], hT_d[:, :, 2 * NT + 256:2 * NT + 512], r=[hT_res], w=[hk.r])
        wqkv = Tile(kb, pn2, "n_w", [128, 8, 1536], BF16)
        kb.dma(wqkv[:], w_in_v[:, :, 0:1536], w=[wqkv.r], q="pool")
        n = 0
        for c4 in range(4):
            for tt in range(4):
                ps = next_ps()
                for kc in range(8):
                    kb.op("pe", lambda: PE.matmul(ps[:, :], lhsT=wqkv[:, kc, c4 * 128:(c4 + 1) * 128], rhs=hk[:, kc, 256 + tt * 512:256 + (tt + 1) * 512], start=(kc == 0), stop=(kc == 7)),
                          r=[wqkv.r, hk.r], w=[ps.r], signal=(kc == 7))
                eng = "act" if n % 2 == 0 else "dve"; n += 1
                kb.op(eng, lambda: (A.copy if eng == "act" else V.tensor_copy)(out=qT[:, c4, tt * 512:(tt + 1) * 512], in_=ps[:, :]), r=[ps.r], w=[qT.r])
            for tt in range(5):
                ps = next_ps()
                for kc in range(8):
                    kb.op("pe", lambda: PE.matmul(ps[:, :], lhsT=wqkv[:, kc, 512 + c4 * 128:512 + (c4 + 1) * 128], rhs=hk[:, kc, tt * 512:(tt + 1) * 512], start=(kc == 0), stop=(kc == 7)),
                          r=[wqkv.r, hk.r], w=[ps.r], signal=(kc == 7))
                eng = "act" if n % 2 == 0 else "dve"; n += 1
                kb.op(eng, lambda: (A.copy if eng == "act" else V.tensor_copy)(out=kT[:, c4, tt * 512:(tt + 1) * 512], in_=ps[:, :]), r=[ps.r], w=[kT.r])
        for row in range(0, 40, 2):
            ps = next_ps()
            for kc in range(8):
                kb.op("pe", lambda: PE.matmul(ps[:, :], lhsT=hk[:, kc, row * 64:row * 64 + 128], rhs=wqkv[:, kc, 1024:1536], start=(kc == 0), stop=(kc == 7)),
                      r=[wqkv.r, hk.r], w=[ps.r], signal=(kc == 7))
            eng = "act" if (row // 2) % 2 == 0 else "dve"
            kb.op(eng, lambda: (A.copy if eng == "act" else V.tensor_copy)(out=Vt[:, row, :], in_=ps[:, :]), r=[ps.r], w=[Vt.r])
        kb.dma(Vt[0:64, 1:39:2, :], Vt[64:128, 0:38:2, :], r=[Vt.r], w=[Vt.r], semres=Vt.r)
        kb.dma(Vt[64:128, 1:39:2, :], Vt[0:64, 2:40:2, :], r=[Vt.r], w=[Vt.r], semres=Vt.r)
        pn2.close()
        kb.barrier()
        Ps = [Tile(kb, pn, f"n_P{i}", [128, 768], BF16) for i in range(8)]
        PTs = [Tile(kb, pn, f"n_PT{i}", [128, 6, 128], BF16) for i in range(8)]
        sts = [Tile(kb, pn, f"n_st{i}", [128, 8], F32) for i in range(8)]
        ctx["right_es"] = ExitStack()
        ctx["wg"] = Tile(kb, ctx["right_es"], "e_wg", [128, 8, 3072], BF16, side="right")
        kb.dma(ctx["wg"][:], ctx["w_gate_d"].rearrange("(kc p) n -> p kc n", p=128), w=[ctx["wg"].r], q="pool")
        NB = 8
        pvbank = {}
        pools = {"qk": [0, [0, 1, 2, 3]], "t": [0, [4, 5]], "pv": [0, [6, 7]]}

        def pool_ps(name):
            p = pools[name]
            b = psum[p[1][p[0] % len(p[1])]]
            p[0] += 1
            return b
        units = []
        for rl in range(32):
            if rl <= 3:
                start, nrows, bs, edge = rl, 12, 3, rl
            elif rl >= 29:
                start, nrows, bs, edge = rl - 3, 12, 0, 4 + (rl - 29)
            else:
                start, nrows, bs, edge = rl, 8, 3, None
            for c4 in range(4):
                units.append((rl, c4, start, nrows, bs, edge))
        if "naInterior" in debug:
            units = [x for x in units if 4 <= x[0] < 12]
        if "naEdge" in debug:
            units = [x for x in units if x[0] < 2]
        for fl in debug:
            if fl.startswith("naR"):
                a, b = fl[3:].split("-")
                units = [x for x in units if int(a) <= x[0] < int(b)]
        ito = [0]

        def _u(u):
            rl, c4, start, nrows, bs, edge = units[u]
            parts = [(0, 8)] + ([(8, nrows - 8)] if nrows > 8 else [])
            return rl, c4, start, nrows, bs, edge, parts

        qkb = {}

        def na_s1(u):
            rl, c4, start, nrows, bs, edge, parts = _u(u)
            s_ = sts[u % NB]
            qkb[u] = []
            for pi, (r0, nr) in enumerate(parts):
                ps = pool_ps("qk")
                qkb[u].append(ps)
                for hp in range(2):
                    kb.op("pe", lambda: PE.matmul(ps[hp * 64:(hp + 1) * 64, 0:nr * 64], lhsT=qT[hp * 64:(hp + 1) * 64, c4, rl * 64:(rl + 1) * 64],
                                                  rhs=kT[hp * 64:(hp + 1) * 64, c4, (start + r0) * 64:(start + r0 + nr) * 64], start=True, stop=False,
                                                  skip_group_check=True),
                          r=[qT.r, kT.r], w=[ps.r], signal=False)
                kb.op("pe", lambda: PE.matmul(ps[:, 0:nr * 64], lhsT=ident[:], rhs=bext8[:, c4, bs + r0:bs + r0 + nr, :].rearrange("p r k -> p (r k)"),
                                              start=False, stop=(edge is None), skip_group_check=True),
                      r=[ident.r, bext8.r], w=[ps.r], signal=(edge is None))
                if edge is not None:
                    kb.op("pe", lambda: PE.matmul(ps[:, 0:nr * 64], lhsT=ones1[0:1, :], rhs=rowv[0:1, edge * 768 + r0 * 64:edge * 768 + (r0 + nr) * 64],
                                                  start=False, stop=True, skip_group_check=True),
                          r=[ones1.r, rowv.r], w=[ps.r])
                kb.op("dve", lambda: V.reduce_max(out=s_[:, 4 * pi:4 * pi + 1], in_=ps[:, 0:nr * 64], axis=AX.X), r=[ps.r], w=[s_.r])
            if len(parts) == 2:
                kb.op("dve", lambda: V.tensor_tensor(out=s_[:, 0:1], in0=s_[:, 0:1], in1=s_[:, 4:5], op=ALU.max), r=[s_.r], w=[s_.r])
            kb.op("dve", lambda: V.tensor_scalar(out=s_[:, 1:2], in0=s_[:, 0:1], scalar1=-0.125, scalar2=None, op0=ALU.mult), r=[s_.r], w=[s_.r])

        def na_s2(u):
            rl, c4, start, nrows, bs, edge, parts = _u(u)
            P_ = Ps[u % NB]; s_ = sts[u % NB]
            for pi, ((r0, nr), ps) in enumerate(zip(parts, qkb.pop(u))):
                kb.op("act", lambda: A.activation(out=P_[:, r0 * 64:(r0 + nr) * 64], in_=ps[:, 0:nr * 64], func=AF.Exp, scale=0.125, bias=s_[:, 1:2],
                                                  accum_out=s_[:, 2 + 4 * pi:3 + 4 * pi]), r=[ps.r, s_.r], w=[P_.r, s_.r])
            if len(parts) == 2:
                kb.op("dve", lambda: V.tensor_tensor(out=s_[:, 2:3], in0=s_[:, 2:3], in1=s_[:, 6:7], op=ALU.add), r=[s_.r], w=[s_.r])
            kb.op("dve", lambda: V.reciprocal(out=s_[:, 3:4], in_=s_[:, 2:3]), r=[s_.r], w=[s_.r])

        tbank = {}

        def na_s3(u):
            rl, c4, start, nrows, bs, edge, parts = _u(u)
            P_ = Ps[u % NB]
            ps = pool_ps("t")
            tbank[u] = ps
            psb = ps[:].bitcast(BF16)
            npair = nrows // 2
            for m in range(npair):
                kb.op("pe", lambda: PE.transpose(out=psb[:, m * 128:(m + 1) * 128], in_=P_[:, m * 128:(m + 1) * 128], identity=ident[:]),
                      r=[P_.r, ident.r], w=[ps.r], signal=(m == npair - 1))

        def na_s4(u):
            rl, c4, start, nrows, bs, edge, parts = _u(u)
            PT_ = PTs[u % NB]
            ps = tbank.pop(u)
            psb = ps[:].bitcast(BF16)
            npair = nrows // 2
            kb.op("act", lambda: A.copy(out=PT_[:, 0:npair, :], in_=psb[:, 0:npair * 128].rearrange("p (r k) -> p r k", k=128)), r=[ps.r], w=[PT_.r])

        def na_s5(u):
            rl, c4, start, nrows, bs, edge, parts = _u(u)
            PT_ = PTs[u % NB]; s_ = sts[u % NB]
            pvb = pool_ps("pv"); pvc = 0; pvr = pvb.r
            npair = nrows // 2
            for m in range(npair):
                kb.op("pe", lambda: PE.matmul(pvb[:, pvc:pvc + 128], lhsT=PT_[:, m, :], rhs=Vt[:, start + 2 * m, c4 * 128:(c4 + 1) * 128], start=(m == 0), stop=(m == npair - 1)),
                      r=[PT_.r, Vt.r], w=[pvr], signal=(m == npair - 1))
            kb.op("dve", lambda: V.tensor_scalar(out=yna[0:64, rl % 8, c4 * 128:c4 * 128 + 64], in0=pvb[0:64, pvc:pvc + 64], scalar1=s_[0:64, 3:4], scalar2=None, op0=ALU.mult),
                  r=[pvr, s_.r], w=[yna.r])
            kb.op("act", lambda: A.activation(out=yna[64:128, rl % 8, c4 * 128 + 64:c4 * 128 + 128], in_=pvb[64:128, pvc + 64:pvc + 128], func=AF.Copy, scale=s_[64:128, 3:4]),
                  r=[pvr, s_.r], w=[yna.r])
            if rl % 8 == 7 and c4 == 3:
                r8 = rl // 8
                for c4b in range(4):
                    ps = pool_ps("pv")
                    o_ = ob[ito[0] % 2]; ito[0] += 1
                    for rr_ in range(8):
                        for hp in range(2):
                            lo, hi = hp * 64, (hp + 1) * 64
                            kb.op("pe", lambda: PE.matmul(ps[lo:hi, rr_ * 64:(rr_ + 1) * 64], lhsT=yna[lo:hi, rr_, c4b * 128 + lo:c4b * 128 + hi], rhs=ident[lo:hi, lo:hi], start=True, stop=True),
                                  r=[yna.r, ident.r], w=[ps.r], signal=(rr_ == 7 and hp == 1))
                    kb.op("act", lambda: A.copy(out=o_[:], in_=ps[:, :]), r=[ps.r], w=[o_.r])
                    kb.dma(naT_d[:, c4b, r8 * 512:(r8 + 1) * 512], o_[:], r=[o_.r], w=[naT_res], semres=naT_res)

        swpipe(len(units), [na_s1, na_s2, na_s3, na_s4, na_s5], lag=NA_LAG)
    kb.barrier()
    return [memT_res, naT_res]


def phase_merge_ffn(ctx):
    nc, kb, es = ctx["nc"], ctx["kb"], ctx["es"]
    scratch, debug = ctx["scratch"], ctx["debug"]
    ident, next_ps = ctx["ident"], ctx["next_ps"]
    hT_d, hT_res, x_all, out = ctx["hT_d"], ctx["hT_res"], ctx["x_all"], ctx["out"]
    gvec = ctx["gvec"]
    V, G, A, PE = nc.vector, nc.gpsimd, nc.scalar, nc.tensor
    w_gate_d = ctx["w_gate_d"]
    b_gate_d = dram_in(nc, "b_gate", [128, 24])
    w_br_d = dram_in(nc, "w_branch", [3, 512, D])
    w_o_d = dram_in(nc, "w_o", [D, D])
    w1_d = dram_in(nc, "w_ffn1", [D, DFF]); w3_d = dram_in(nc, "w_ffn3", [D, DFF]); w2_d = dram_in(nc, "w_ffn2", [DFF, D])
    x1_d = scratch("x1_scr", [NT, D], F32); x1_res = kb.res("x1_d")
    ys_d = [(ctx["naT_d"], ctx["naT_res"]), (ctx["s5T_d"], ctx["s5T_res"]), (ctx["memT_d"], ctx["memT_res"])]

    pw = ExitStack()
    w1 = Tile(kb, pw, "f_w1", [128, 8, DFF], BF16)
    with ExitStack() as pe_:
        wg = ctx["wg"]
        wb = Tile(kb, pe_, "e_wb", [128, 3, 4, D], BF16)
        for n_ in range(3):
            kb.dma(wb[:, n_, :, :], w_br_d[n_].rearrange("(kc p) n -> p kc n", p=128), w=[wb.r], q="pool")
        wo = Tile(kb, pe_, "e_wo", [128, 8, D], BF16)
        kb.dma(wo[:], w_o_d.rearrange("(kc p) n -> p kc n", p=128), w=[wo.r], q="pool")
        bg = Tile(kb, pe_, "e_bg", [128, 24], F32)
        kb.dma(bg[:], b_gate_d, w=[bg.r])
        for kc in range(8):
            kb.dma(w1[:, kc, :], w1_d[kc * 128:(kc + 1) * 128, :], w=[w1.r], q="pool")
        hTs = [Tile(kb, pe_, f"e_hT{i}", [128, 8, 512], BF16) for i in range(2)]
        yss = [[Tile(kb, pe_, f"e_ys{i}_{n_}", [128, 4, 512], BF16) for n_ in range(3)] for i in range(2)]
        mT = Tile(kb, pe_, "e_mT", [128, 8, 512], BF16)
        gs = [Tile(kb, pe_, f"e_gs{i}", [128, 512], F32) for i in range(2)]
        macc = Tile(kb, pe_, "e_macc", [128, 512], F32)
        mtmp = Tile(kb, pe_, "e_mtmp", [128, 512], F32)
        xts = [Tile(kb, pe_, f"e_xt{i}", [128, D], F32) for i in range(2)]
        it = 0; ix = 0
        def load_tile(tt):
            hT = hTs[tt % 2]; ys = yss[tt % 2]
            kb.dma(hT[:], hT_d[:, :, tt * 512:(tt + 1) * 512], r=[hT_res], w=[hT.r])
            for n_ in range(3):
                kb.dma(ys[n_][:], ys_d[n_][0][:, :, tt * 512:(tt + 1) * 512], r=[ys_d[n_][1]], w=[ys[n_].r])
        load_tile(0)
        for tt in range(4):
            hT = hTs[tt % 2]; ys = yss[tt % 2]
            if tt + 1 < 4:
                load_tile(tt + 1)
            for dc in range(8):
                for n_ in range(3):
                    g_ = gs[it % 2]; it += 1
                    ps = next_ps()
                    for kc in range(8):
                        kb.op("pe", lambda: PE.matmul(ps[:, :], lhsT=wg[:, kc, n_ * 1024 + dc * 128:n_ * 1024 + (dc + 1) * 128], rhs=hT[:, kc, :], start=(kc == 0), stop=(kc == 7)),
                              r=[wg.r, hT.r], w=[ps.r], signal=(kc == 7))
                    kb.op("act", lambda: A.activation(out=g_[:], in_=ps[:, :], func=AF.Sigmoid, bias=bg[:, n_ * 8 + dc:n_ * 8 + dc + 1]), r=[ps.r, bg.r], w=[g_.r])
                    ps2 = next_ps()
                    for kc in range(4):
                        kb.op("pe", lambda: PE.matmul(ps2[:, :], lhsT=wb[:, n_, kc, dc * 128:(dc + 1) * 128], rhs=ys[n_][:, kc, :], start=(kc == 0), stop=(kc == 3)),
                              r=[wb.r, ys[n_].r], w=[ps2.r], signal=(kc == 3))
                    if n_ == 0:
                        kb.op("dve", lambda: V.tensor_tensor(out=macc[:], in0=g_[:], in1=ps2[:, :], op=ALU.mult), r=[g_.r, ps2.r], w=[macc.r])
                    else:
                        kb.op("dve", lambda: V.tensor_tensor(out=mtmp[:], in0=g_[:], in1=ps2[:, :], op=ALU.mult), r=[g_.r, ps2.r], w=[mtmp.r])
                        if n_ == 1:
                            kb.op("dve", lambda: V.tensor_tensor(out=macc[:], in0=macc[:], in1=mtmp[:], op=ALU.add), r=[macc.r, mtmp.r], w=[macc.r])
                        else:
                            kb.op("dve", lambda: V.tensor_tensor(out=mT[:, dc, :], in0=macc[:], in1=mtmp[:], op=ALU.add), r=[macc.r, mtmp.r], w=[mT.r])
            for sub in range(4):
                x_ = xts[ix % 2]; ix += 1
                t0 = tt * 512 + sub * 128
                kb.dma(x_[:], x_all[t0:t0 + 128, :], w=[x_.r])
                for half in range(2):
                    ps = next_ps()
                    for dc in range(8):
                        kb.op("pe", lambda: PE.matmul(ps[:, :], lhsT=mT[:, dc, sub * 128:(sub + 1) * 128], rhs=wo[:, dc, half * 512:(half + 1) * 512], start=(dc == 0), stop=(dc == 7)),
                              r=[mT.r, wo.r], w=[ps.r], signal=(dc == 7))
                    kb.op("dve", lambda: V.tensor_tensor(out=x_[:, half * 512:(half + 1) * 512], in0=x_[:, half * 512:(half + 1) * 512], in1=ps[:, :], op=ALU.add),
                          r=[x_.r, ps.r], w=[x_.r])
                kb.dma(x1_d[t0:t0 + 128, :], x_[:], r=[x_.r], w=[x1_res], semres=x_.r)
    ctx["right_es"].close()
    kb.barrier()

    out_res = kb.res("out_d")
    with ExitStack() as pf:
        w3 = Tile(kb, pf, "f_w3", [128, 8, DFF], BF16)
        w2 = Tile(kb, pf, "f_w2", [128, NFC, D], BF16)
        for kc in range(8):
            kb.dma(w3[:, kc, :], w3_d[kc * 128:(kc + 1) * 128, :], w=[w3.r], q="pool")
        kb.dma(w2[:], w2_d.rearrange("(fc p) n -> p fc n", p=128), w=[w2.r], q="pool")
        gf = Tile(kb, pf, "f_gf", [128, D], F32)
        kb.dma(gf[:], gvec[2:3, :].to_broadcast([128, D]), w=[gf.r])
        gfin = Tile(kb, pf, "f_gfin", [128, D], F32)
        kb.dma(gfin[:], gvec[3:4, :].to_broadcast([128, D]), w=[gfin.r])
        x1 = Tile(kb, pf, "f_x1", [128, 4, D], F32)
        xl = Tile(kb, pf, "f_xl", [128, D], F32)
        xs = [Tile(kb, pf, f"f_xs{i}", [128, D], BF16) for i in range(4)]
        ss = [Tile(kb, pf, f"f_ss{i}", [128, 4], F32) for i in range(4)]
        h2Ts = [Tile(kb, pf, f"f_h2T{i}", [128, 8, 512], BF16) for i in range(2)]
        aT = Tile(kb, pf, "f_aT", [128, NFC, 512], BF16)
        sl = [Tile(kb, pf, f"f_sl{i}", [128, 512], BF16) for i in range(1)]

        class _Sub:
            def __init__(self, t, i):
                self.t, self.i, self.r = t, i, t.r

            def __getitem__(self, idx):
                return self.t[:, self.i, :][idx]

        def prepA(tt):
            for sub in range(4):
                t0 = tt * 512 + sub * 128
                kb.dma(xl[:], x1_d[t0:t0 + 128, :], r=[x1_res], w=[xl.r])
                norm_rows(ctx, pf, xl, gf, xs[sub], ss[sub], xs[sub])

        def prepB(tt):
            h2T = h2Ts[tt % 2]
            pb = {}

            def p2(sub):
                xs_ = xs[sub]
                pb[sub] = []
                for half in range(2):
                    ps = next_ps()
                    pb[sub].append(ps)
                    for k in range(4):
                        kc = half * 4 + k
                        kb.op("pe", lambda: PE.matmul(ps[:, k * 128:(k + 1) * 128], lhsT=xs_[:, kc * 128:(kc + 1) * 128], rhs=ident[:], start=True, stop=True),
                              r=[xs_.r, ident.r], w=[ps.r], signal=(k == 3))

            def p3(sub):
                for half, ps in enumerate(pb.pop(sub)):
                    kb.op("act", lambda: A.copy(out=h2T[:, half * 4:(half + 1) * 4, sub * 128:(sub + 1) * 128], in_=ps[:].rearrange("p (k t) -> p k t", k=4)),
                          r=[ps.r], w=[h2T.r])
            swpipe(4, [p2, p3], lag=1)

        itc = [0]

        def ffn1(tt):
            h2T = h2Ts[tt % 2]
            kb.dma(x1[:], x1_d[tt * 512:(tt + 1) * 512, :].rearrange("(s p) n -> p s n", p=128), r=[x1_res], w=[x1.r])
            for fc in range(NFC):
                s_ = sl[0]; itc[0] += 1
                ps1 = next_ps()
                for kc in range(8):
                    kb.op("pe", lambda: PE.matmul(ps1[:, :], lhsT=w1[:, kc, fc * 128:(fc + 1) * 128], rhs=h2T[:, kc, :], start=(kc == 0), stop=(kc == 7)),
                          r=[w1.r, h2T.r], w=[ps1.r], signal=(kc == 7))
                ps3 = next_ps()
                for kc in range(8):
                    kb.op("pe", lambda: PE.matmul(ps3[:, :], lhsT=w3[:, kc, fc * 128:(fc + 1) * 128], rhs=h2T[:, kc, :], start=(kc == 0), stop=(kc == 7)),
                          r=[w3.r, h2T.r], w=[ps3.r], signal=(kc == 7))
                kb.op("act", lambda: A.activation(out=s_[:], in_=ps1[:, :], func=AF.Silu), r=[ps1.r], w=[s_.r])
                kb.op("dve", lambda: V.tensor_tensor(out=aT[:, fc, :], in0=s_[:], in1=ps3[:, :], op=ALU.mult), r=[s_.r, ps3.r], w=[aT.r])

        def ffn2(tt, sub):
            s_ = ss[sub % 4]
            junk = xl
            for half in range(2):
                ps = next_ps()
                for fc in range(NFC):
                    kb.op("pe", lambda: PE.matmul(ps[:, :], lhsT=aT[:, fc, sub * 128:(sub + 1) * 128], rhs=w2[:, fc, half * 512:(half + 1) * 512], start=(fc == 0), stop=(fc == NFC - 1)),
                          r=[aT.r, w2.r], w=[ps.r], signal=(fc == NFC - 1))
                kb.op("dve", lambda: V.tensor_tensor(out=x1[:, sub, half * 512:(half + 1) * 512], in0=x1[:, sub, half * 512:(half + 1) * 512], in1=ps[:, :], op=ALU.add),
                      r=[x1.r, ps.r], w=[x1.r])
            xsub = _Sub(x1, sub)
            kb.op("act", lambda: A.activation(out=junk[:], in_=xsub[:], func=AF.Square, accum_out=s_[:, 0:1]), r=[x1.r], w=[junk.r, s_.r])
            kb.op("act", lambda: A.activation(out=s_[:, 1:2], in_=s_[:, 0:1], func=AF.Sqrt, scale=1.0 / D, bias=ctx["epsb"][:, 0:1]), r=[s_.r, ctx["epsb"].r], w=[s_.r])
            kb.op("dve", lambda: V.reciprocal(out=s_[:, 2:3], in_=s_[:, 1:2]), r=[s_.r], w=[s_.r])
            kb.op("dve", lambda: V.scalar_tensor_tensor(out=xsub[:], in0=xsub[:], scalar=s_[:, 2:3], in1=gfin[:], op0=ALU.mult, op1=ALU.mult),
                  r=[x1.r, s_.r, gfin.r], w=[x1.r])
            t0 = tt * 512 + sub * 128
            kb.dma(out[t0:t0 + 128, :], xsub[:], r=[x1.r], w=[out_res], semres=out_res)

        prepA(0)
        prepB(0)
        for tt in range(4):
            ffn1(tt)
            if tt + 1 < 4:
                prepA(tt + 1)
            ffn2(tt, 0)
            ffn2(tt, 1)
            ffn2(tt, 2)
            if tt + 1 < 4:
                prepB(tt + 1)
            ffn2(tt, 3)
    pw.close()
    return [out_res]


def s5_host(inp):
    f32 = np.float32

    def P2(a):
        return np.asarray(a, f32).reshape(16, 2, 64).transpose(1, 2, 0).reshape(128, 16)

    def P3(a):
        return np.asarray(a, f32).reshape(16, 2, 64, 16).transpose(1, 2, 0, 3).reshape(128, 256)
    small = np.zeros((128, 2, 3, 16), f32)
    bc = np.zeros((128, 2, 4, 256), f32)
    for d in range(2):
        small[:, d, 0] = P2(inp["a_re"][0, d])
        small[:, d, 1] = P2(inp["a_im"][0, d])
        small[:, d, 2] = P2(np.broadcast_to(np.asarray(inp["log_dt"][0, d])[:, None], (32, 64)))
        bc[:, d, 0] = P3(inp["b_re"][0, d])
        bc[:, d, 1] = P3(inp["b_im"][0, d])
        bc[:, d, 2] = P3(np.asarray(inp["c_re"][0, d]).transpose(0, 2, 1))
        bc[:, d, 3] = P3(np.asarray(inp["c_im"][0, d]).transpose(0, 2, 1))
    kvv, _ = kv_layout()
    ip = np.arange(128) // 16
    ii = np.arange(256) // 16
    mask = np.zeros((128, 2, 2, 256), f32)
    for kc2 in range(2):
        ia = kc2 * 8 + ip
        mask[:, 0, kc2, :] = (ii[None, :] >= ia[:, None])
        mask[:, 1, kc2, :] = (ii[None, :] <= ia[:, None])
    return {
        "s5_small": small, "s5_bc": bc,
        "s5_dbc": np.broadcast_to(np.asarray(inp["s5_d"][0], f32), (128, 512)).copy(),
        "kv": np.broadcast_to(kvv, (128, len(kvv))).copy(),
        "s5_mask": mask,
        "w_glu": np.ascontiguousarray(inp["w_glu"][0], f32),
    }


def make_in_maps(inp):
    x = np.asarray(inp["x"], np.float32)
    maps = []
    gvec = np.stack([inp["g_mix"][0], inp["g_mem"][0], inp["g_ffn"][0], inp["g_final"]]).astype(np.float32)
    s5h = s5_host(inp)
    rpb = np.asarray(inp["rpb"][0], np.float32)
    cq = np.arange(64)[:, None]; ck = np.arange(64)[None, :]
    relc = np.clip(ck - cq + 15, 0, 30)
    bg_ = rpb[:, :, relc].reshape(4, 2, 15, 64, 64)
    bext = np.ascontiguousarray(bg_.transpose(1, 3, 0, 2, 4).reshape(128, 4 * 15 * 64))
    cs = np.clip(np.arange(64) - 8, 0, 48)
    inwin = (ck >= cs[:, None]) & (ck < cs[:, None] + 16)
    maskw = np.tile(np.where(inwin, 0.0, NEG).astype(np.float32), (2, 1))
    rowmasks = []
    for half in range(2):
        rm = np.full((7, 12), NEG, np.float32)
        for e, rl in enumerate([0, 1, 2, 3, 29, 30, 31]):
            r = half * 32 + rl
            rs = min(max(r - 4, 0), 56)
            base = (rl - 4) if rl <= 3 else (rl - 7)
            nrows = 12 if rl <= 3 else 11
            for k in range(nrows):
                ra = half * 32 + base + k
                if rs <= ra < rs + 8:
                    rm[e, k] = 0.0
        rowmasks.append(np.ascontiguousarray((np.repeat(rm[:, :, None], 64, axis=2) * 1.0).reshape(1, 7 * 768)))
    for core in range(8):
        b, half = core // 2, core % 2
        own = x[b, half * NT:(half + 1) * NT]
        oth = x[b, (1 - half) * NT:(2 - half) * NT]
        halo = np.zeros((NHALO, D), np.float32)
        if half == 0:
            halo[256:512] = x[b, NT:NT + 256]
        else:
            halo[0:256] = x[b, NT - 256:NT]
        m = dict(s5h)
        m["flags"] = np.broadcast_to(np.array([1.0 if half == 1 else 0.0, 1.0 if half == 0 else 0.0], np.float32), (128, 2)).copy()
        m.update({
            "x_all": np.ascontiguousarray(np.concatenate([own, oth, halo], 0)),
            "mem": np.ascontiguousarray(inp["mem"][b]),
            "gvec": gvec,
            "w_in": np.ascontiguousarray(inp["w_in"][0]),
            "ident": np.eye(128, dtype=np.float32),
            "w_mem_kv": np.ascontiguousarray(inp["w_mem_kv"][0], np.float32),
            "bext": bext, "maskw": maskw, "rowvec": rowmasks[half],
            "w_gate": np.ascontiguousarray(inp["w_gate"][0], np.float32),
            "b_gate": np.ascontiguousarray(np.asarray(inp["b_gate"][0], np.float32).reshape(24, 128).T),
            "w_branch": np.ascontiguousarray(inp["w_branch"][0], np.float32),
            "w_o": np.ascontiguousarray(inp["w_o"][0], np.float32),
            "w_ffn1": np.ascontiguousarray(inp["w_ffn1"][0], np.float32),
            "w_ffn3": np.ascontiguousarray(inp["w_ffn3"][0], np.float32),
            "w_ffn2": np.ascontiguousarray(inp["w_ffn2"][0], np.float32),
        })
        maps.append(m)
    return maps


def kernel(**inp):
    nc = build_nc()
    maps = make_in_maps(inp)
    res = run_bass_kernel_spmd(nc, maps, core_ids=list(range(8)))
    outp = np.zeros((4, 4096, D), np.float32)
    for core in range(8):
        b, half = core // 2, core % 2
        outp[b, half * NT:(half + 1) * NT] = res.results[core]["out"]
    return outp
```
